# Optimizing a Trainium2 kernel written in Bass

```python
import jax, jax.numpy as jnp
from jax import lax
import numpy as np

D_MODEL = 1024
BATCH = 8
SEQ = 2048
DEPTH = 2
DEC_BATCH = 128
DEC_SEQ = 4
PAST_LEN = 16384
PAGE_SIZE = 128

EXPAND = 2
EXPAND_WIDTH = EXPAND * D_MODEL
CONV_WIDTH = 3
POOL_WINDOWS = (2, 4, 8, 16)
N_POOL_GROUPS = len(POOL_WINDOWS)
POOL_GROUP_WIDTH = EXPAND_WIDTH // N_POOL_GROUPS
POOL_HIST = max(POOL_WINDOWS) - 1
N_MIXERS = 2
N_CONV_LAYERS = (DEPTH + 1) // 2
N_POOL_LAYERS = DEPTH // 2
RMS_EPS = 1e-6

kernel_name = "hybrid_shortconv_pool_decoder_step"


def rmsnorm(x, g):
    xf = x.astype(jnp.float32)
    r = lax.rsqrt(jnp.mean(xf * xf, axis=-1, keepdims=True) + RMS_EPS)
    return (xf * r * g.astype(jnp.float32)).astype(x.dtype)


def conv_mixer(h, hist, w_in, conv_w, conv_b, w_out):
    T = h.shape[1]
    proj = h @ w_in
    gb, gc, v, z = jnp.split(proj, 4, axis=-1)
    cv = gc * v
    full = jnp.concatenate([hist.astype(cv.dtype), cv], axis=1)
    conv = conv_b
    for k in range(CONV_WIDTH):
        conv = conv + full[:, k:k + T] * conv_w[k]
    y = gb * conv * jax.nn.silu(z)
    out = y @ w_out
    new_hist = full[:, -(CONV_WIDTH - 1):]
    return out, new_hist


def pool_mixer(h, hist, start_pos, w_in, w_grp, scale, w_out):
    b, T, _ = h.shape
    proj = h @ w_in
    u, z = jnp.split(proj, 2, axis=-1)
    full = jnp.concatenate([hist.astype(u.dtype), u], axis=1)
    fullf = full.astype(jnp.float32)
    cs = jnp.concatenate([jnp.zeros((b, 1, EXPAND_WIDTH), jnp.float32),
                          jnp.cumsum(fullf, axis=1)], axis=1)
    P = POOL_HIST
    pos = (start_pos + jnp.arange(T)).astype(jnp.float32)
    diffs = []
    for g, w in enumerate(POOL_WINDOWS):
        sl = slice(g * POOL_GROUP_WIDTH, (g + 1) * POOL_GROUP_WIDTH)
        win = cs[:, P + 1:P + 1 + T, sl] - cs[:, P + 1 - w:P + 1 - w + T, sl]
        cnt = jnp.minimum(jnp.float32(w), pos + 1.0)
        diffs.append(win / cnt[None, :, None] - fullf[:, P:, sl])
    pooled = jnp.stack(diffs, axis=2).astype(u.dtype)
    mixed = jnp.einsum('btgc,gcd->btgd', pooled, w_grp).reshape(b, T, EXPAND_WIDTH)
    y = mixed * scale * jax.nn.silu(z)
    out = y @ w_out
    new_hist = full[:, -P:]
    return out, new_hist


def _stack(lst, b, rows):
    if lst:
        return jnp.stack(lst)
    return jnp.zeros((0, b, rows, EXPAND_WIDTH), jnp.float32)


def trunk(x, conv_hist, pool_hist, start_pos, norm_g, final_norm_g,
          conv_w_in, conv_w, conv_b, conv_w_out,
          pool_w_in, pool_w_grp, pool_scale, pool_w_out):
    b = x.shape[0]
    new_conv, new_pool = [], []
    ia, ib = 0, 0
    for i in range(DEPTH):
        hn = rmsnorm(x, norm_g[i])
        if i % N_MIXERS == 0:
            out, nh = conv_mixer(hn, conv_hist[ia], conv_w_in[ia], conv_w[ia],
                                 conv_b[ia], conv_w_out[ia])
            new_conv.append(nh)
            ia += 1
        else:
            out, nh = pool_mixer(hn, pool_hist[ib], start_pos, pool_w_in[ib],
                                 pool_w_grp[ib], pool_scale[ib], pool_w_out[ib])
            new_pool.append(nh)
            ib += 1
        x = x + out
    return (rmsnorm(x, final_norm_g), _stack(new_conv, b, CONV_WIDTH - 1),
            _stack(new_pool, b, POOL_HIST))


def setup_inputs(seed: int = 0) -> dict:
    key = jax.random.key(seed)
    ks = jax.random.split(key, 14)
    D, E, Gc = D_MODEL, EXPAND_WIDTH, POOL_GROUP_WIDTH
    nrm = jax.random.normal
    return {
        "x_prompt": nrm(ks[0], (BATCH, SEQ, D), jnp.float32),
        "x_sample": nrm(ks[1], (DEC_BATCH, DEC_SEQ, D), jnp.float32),
        "state_conv": nrm(ks[2], (N_CONV_LAYERS, DEC_BATCH, CONV_WIDTH - 1, E), jnp.float32),
        "state_pool": nrm(ks[3], (N_POOL_LAYERS, DEC_BATCH, POOL_HIST, E), jnp.float32),
        "norm_g": 1.0 + 0.02 * nrm(ks[4], (DEPTH, D), jnp.float32),
        "final_norm_g": 1.0 + 0.02 * nrm(ks[5], (D,), jnp.float32),
        "conv_w_in": nrm(ks[6], (N_CONV_LAYERS, D, 4 * E), jnp.float32) * D ** -0.5,
        "conv_w": nrm(ks[7], (N_CONV_LAYERS, CONV_WIDTH, E), jnp.float32) * CONV_WIDTH ** -0.5,
        "conv_b": 0.02 * nrm(ks[8], (N_CONV_LAYERS, E), jnp.float32),
        "conv_w_out": nrm(ks[9], (N_CONV_LAYERS, E, D), jnp.float32) * E ** -0.5,
        "pool_w_in": nrm(ks[10], (N_POOL_LAYERS, D, 2 * E), jnp.float32) * D ** -0.5,
        "pool_w_grp": nrm(ks[11], (N_POOL_LAYERS, N_POOL_GROUPS, Gc, Gc), jnp.float32) * Gc ** -0.5,
        "pool_scale": 1.0 + 0.02 * nrm(ks[12], (N_POOL_LAYERS, E), jnp.float32),
        "pool_w_out": nrm(ks[13], (N_POOL_LAYERS, E, D), jnp.float32) * E ** -0.5,
    }


def reference(x_prompt, x_sample, state_conv, state_pool, norm_g, final_norm_g,
              conv_w_in, conv_w, conv_b, conv_w_out,
              pool_w_in, pool_w_grp, pool_scale, pool_w_out):
    b = x_prompt.shape[0]
    conv_hist0 = jnp.zeros((N_CONV_LAYERS, b, CONV_WIDTH - 1, EXPAND_WIDTH), x_prompt.dtype)
    pool_hist0 = jnp.zeros((N_POOL_LAYERS, b, POOL_HIST, EXPAND_WIDTH), x_prompt.dtype)
    y_prompt, new_conv_prompt, new_pool_prompt = trunk(
        x_prompt, conv_hist0, pool_hist0, 0, norm_g, final_norm_g,
        conv_w_in, conv_w, conv_b, conv_w_out,
        pool_w_in, pool_w_grp, pool_scale, pool_w_out)
    y_sample, new_conv_sample, new_pool_sample = trunk(
        x_sample, state_conv, state_pool, PAST_LEN, norm_g, final_norm_g,
        conv_w_in, conv_w, conv_b, conv_w_out,
        pool_w_in, pool_w_grp, pool_scale, pool_w_out)
    return (y_prompt, y_sample, new_conv_prompt, new_conv_sample,
            new_pool_prompt, new_pool_sample)
```

```python
from contextlib import ExitStack

import numpy as np
import concourse.bass as bass
import concourse.mybir as mybir
from concourse.bass_utils import run_bass_kernel_spmd

F32 = mybir.dt.float32
BF16 = mybir.dt.bfloat16
AF = mybir.ActivationFunctionType
ALU = mybir.AluOpType

NCORE = 8
D = 1024
E = 2048
SEQ = 2048
DEC_B = 128
DEC_T = 4
CONV_H = 2
POOL_H = 15
WINDOWS = (2, 4, 8, 16)
EPS = 1e-6

KD = D // 128
KE = E // 128
NH = 2
PH = SEQ // NH
SH = (DEC_B // NCORE) // NH
TH = PH + SH * DEC_T
NB = 3
BS = TH // NB
PB2 = PH - 2 * BS
CVW = CONV_H + PH + SH * (CONV_H + DEC_T)
UFW = POOL_H + PH + SH * (POOL_H + DEC_T)
CV_S0 = CONV_H + PH
UF_S0 = POOL_H + PH
TW = PB2 + SH * (CONV_H + DEC_T)

V_CW = 0
V_CB = 48
V_PS = 64
V_G0 = 80
V_G1 = 88
V_GF = 96
NV = 104

HN = DEC_T * SH
HK = (POOL_H - DEC_T) * SH
SEM_ROLL = 3000


class Res:
    __slots__ = ("name", "w", "r", "al")

    def __init__(self, name):
        self.name = name
        self.w = None
        self.r = {}
        self.al = []


def alias(a, bs):
    for b in bs:
        a.al.append(b)
        b.al.append(a)


class Sched:
    ENGS = ("pe", "act", "dve", "pool", "sp")

    def __init__(self, nc, stack):
        self.nc = nc
        self.stack = stack
        self.streams = {e: [] for e in self.ENGS}
        self.sems = {}
        self.cnt = {}
        self.cur = {e: (e, 0) for e in ("pe", "act", "dve", "pool")}
        self.seen = {e: {} for e in self.ENGS}
        self.nwaits = 0

    def _sem(self, key):
        if key not in self.sems:
            nm = "s_" + "_".join(str(k) for k in key)
            self.sems[key] = self.stack.enter_context(self.nc.semaphore(nm))
            self.cnt[key] = 0
        return self.sems[key]

    def _deps(self, engine, reads, writes):
        toks = []
        for r0 in reads:
            for r in [r0] + r0.al:
                if r.w is not None:
                    toks.append((r.w, False))
        for r0 in writes:
            for r in [r0] + r0.al:
                if r.w is not None:
                    toks.append((r.w, False))
                for k, v in r.r.items():
                    toks.append(((k, v), True))
        need = {}
        for (k, v), is_war in toks:
            if k[0] == engine and engine == "pe":
                continue
            if self.seen[engine].get(k, 0) >= v:
                continue
            if need.get(k, 0) < v:
                need[k] = v
        waits = []
        for k, v in need.items():
            self.seen[engine][k] = v
            waits.append((k, v))
        return waits

    def op(self, engine, fn, reads=(), writes=(), inc=True, dma=None):
        waits = self._deps(engine, reads, writes)
        self.nwaits += len(waits)
        tok = None
        incspec = None
        if dma is not None:
            key = ("dma", dma)
            self._sem(key)
            self.cnt[key] += 16
            tok = (key, self.cnt[key])
            incspec = (key, 16)
        elif inc:
            key = self.cur[engine]
            self._sem(key)
            if self.cnt[key] >= SEM_ROLL:
                key = (engine, key[1] + 1)
                self.cur[engine] = key
                self._sem(key)
            self.cnt[key] += 1
            tok = (key, self.cnt[key])
            incspec = (key, 1)
        self.streams[engine].append((waits, fn, incspec))
        if tok is not None:
            for r in writes:
                r.w = tok
                r.r = {}
            for r in reads:
                if r.r.get(tok[0], 0) < tok[1]:
                    r.r[tok[0]] = tok[1]
        return tok

    def final_wait(self, engine, toks):
        need = {}
        for (k, v) in toks:
            if self.seen[engine].get(k, 0) >= v:
                continue
            if need.get(k, 0) < v:
                need[k] = v
        waits = []
        for k, v in need.items():
            self.seen[engine][k] = v
            waits.append((k, v))
        self.streams[engine].append((waits, None, None))

    def emit(self, block):
        for e, attr in (("pe", "tensor"), ("act", "scalar"), ("dve", "vector"),
                        ("pool", "gpsimd"), ("sp", "sync")):
            stream = self.streams[e]

            def body(eng, stream=stream):
                for waits, fn, incspec in stream:
                    for (k, v) in waits:
                        eng.wait_ge(self.sems[k], v)
                    if fn is None:
                        continue
                    ins = fn(eng)
                    if incspec is not None:
                        ins.then_inc(self.sems[incspec[0]], incspec[1])

            getattr(block, attr)(body)


def build_program():
    nc = bass.Bass("TRN2", target_bir_lowering=False)

    def din(name, shape):
        return nc.dram_tensor(name, list(shape), F32, kind="ExternalInput").ap()

    def dout(name, shape):
        return nc.dram_tensor(name, list(shape), F32, kind="ExternalOutput").ap()

    xT = din("xT", (NH, NB, 128, KD, BS))
    w0in = din("w0in", (KE, 128, KD, 512))
    w0out = din("w0out", (4, 128, 4, D))
    w1in = din("w1in", (16, 128, KD, 256))
    w1g = din("w1g", (4, 128, 4, 512))
    w1out = din("w1out", (4, 128, 4, D))
    vecs_d = din("vecs", (128, NV))
    stc_d = din("stc", (NH, 128, KE, SH, CONV_H))
    stp_d = din("stp", (NH, KE, 128, POOL_H * SH))
    invc_d = din("invc", (128, 4, 16))

    yT = dout("yT", (NH, NB, 128, KD, BS))
    ncp_d = dout("ncp", (128, KE, CONV_H))
    ncs_d = dout("ncs", (NH, 128, KE, SH, CONV_H))
    npp_d = dout("npp", (128, KE, POOL_H))
    nps_d = dout("nps", (NH, KE, 128, POOL_H * SH))

    with ExitStack() as stack:
        def sb(name, shape, dt=F32):
            return stack.enter_context(nc.sbuf_tensor(name, list(shape), dt))

        def ps(name):
            return stack.enter_context(nc.psum_tensor(name, [128, 512], F32))

        S = Sched(nc, stack)

        xs = sb("xs", (128, KD, TH))
        xn = sb("xn", (128, KD, TH), BF16)
        yb = sb("yb", (128, KE, TH), BF16)
        win = [sb("win%d" % i, (128, KD, 512), BF16) for i in range(2)]
        wout = sb("wout", (128, KE, D), BF16)
        wg = [sb("wg%d" % i, (128, 4, 512), BF16) for i in range(2)]
        vecs = sb("vecs_sb", (128, NV))
        ones = sb("ones", (128, 128))
        zeros = sb("zeros", (128, 16))
        invc = sb("invc_sb", (128, 4, 16))
        stc = sb("stc_sb", (128, KE, SH, CONV_H))
        ocs = sb("ocs", (128, KE, SH, CONV_H))
        ocp = sb("ocp", (128, KE, CONV_H))
        opp = sb("opp", (128, KE, POOL_H))
        hconv = sb("hconv", (128, KE, CONV_H))
        hpool = sb("hpool", (128, KE, POOL_H))
        NU = 3
        scr = sb("scr", (128, 5 * UFW + 16 + 2 * BS))
        uf = [scr[:, i * UFW:(i + 1) * UFW] for i in range(NU)]
        tmpA = scr[:, 3 * UFW:4 * UFW]
        tmpB = scr[:, 4 * UFW:5 * UFW]
        o5 = 5 * UFW
        pfx = scr[:, o5:o5 + 16]
        szp = [scr[:, o5 + 16 + i * BS:o5 + 16 + (i + 1) * BS] for i in range(2)]
        cvf = [uf[0][:, 0:CVW], uf[1][:, 0:CVW]]
        vsb = [uf[2][:, 0:BS], uf[2][:, BS:2 * BS]]
        szc = [uf[2][:, 2 * BS:3 * BS], tmpB[:, TW:TW + BS]]
        gsb = [tmpB[:, TW + BS:TW + 2 * BS], szp[0]]
        t1 = [tmpA[:, 0:TW], tmpA[:, TW:2 * TW]]
        t2 = [tmpA[:, 2 * TW:3 * TW], tmpB[:, 0:TW]]
        pbraw = sb("pbraw", (128, 8 * (TH // 2)))
        pb = [pbraw[:, i * (TH // 2):(i + 1) * (TH // 2)].bitcast(BF16) for i in range(8)]
        XB = KD * BS
        x2s = [scr[:, 0:XB].rearrange("p (k t) -> p k t", k=KD),
               scr[:, XB:2 * XB].rearrange("p (k t) -> p k t", k=KD),
               pbraw[:, 0:XB].rearrange("p (k t) -> p k t", k=KD)]
        NST = 2
        acc = [sb("acc%d" % i, (128, BS)) for i in range(NST)]
        sq = [sb("sq%d" % i, (128, BS)) for i in range(2)]
        rt = [sb("rt%d" % i, (128, BS)) for i in range(NST)]
        rstd = rt
        rrow = sb("rrow", (128, BS))
        ps0 = sb("ps0", (128, 4))
        sq3 = sb("sq3", (128, BS))
        stp_sb = sb("stp_sb", (128, KE, POOL_H * SH))
        onew = sb("onew", (128, KE, DEC_T * SH))

        banks = [ps("bank%d" % i) for i in range(8)]

        R_x = [[Res("x%d_%d" % (k, b)) for b in range(NB)] for k in range(KD)]
        R_xn = [[Res("xn%d_%d" % (k, b)) for b in range(NB)] for k in range(KD)]
        R_y = [[Res("y%d_%d" % (c, b)) for b in range(NB)] for c in range(KE)]
        R_win = [Res("win%d" % i) for i in range(2)]
        R_wout = [Res("wout%d" % i) for i in range(4)]
        R_wg = [Res("wg%d" % i) for i in range(2)]
        R_const = Res("const")
        R_stc = Res("stc")
        R_ocs = Res("ocs")
        R_ocp = Res("ocp")
        R_opp = Res("opp")
        R_hconv = [Res("hconv%d" % c) for c in range(KE)]
        R_hpool = [Res("hpool%d" % c) for c in range(KE)]
        R_cvf = [Res("cvf%d" % i) for i in range(2)]
        R_vsb = [Res("vsb%d" % i) for i in range(2)]
        R_szc = [Res("szc%d" % i) for i in range(2)]
        R_gsb = [Res("gsb%d" % i) for i in range(2)]
        R_t1 = [Res("t1_%d" % i) for i in range(2)]
        R_t2 = [Res("t2_%d" % i) for i in range(2)]
        R_ufh = [Res("ufh%d" % i) for i in range(NU)]
        R_ufs = [Res("ufs%d" % i) for i in range(NU)]
        R_ufb = [Res("ufb%d" % i) for i in range(NU)]
        R_winh = [Res("winh%d" % i) for i in range(4)]
        R_tmpA = Res("tmpA")
        R_tmpB = Res("tmpB")
        R_pfx = Res("pfx")
        R_pb = [[Res("pb%d_%d" % (i, b)) for b in range(NB)] for i in range(8)]
        R_szp = [Res("szp%d" % i) for i in range(2)]
        R_acc = [Res("acc%d" % i) for i in range(3)]
        R_sq = [Res("sq%d" % i) for i in range(2)]
        R_rt = [Res("rt%d" % i) for i in range(3)]
        R_x2 = [Res("x2s%d" % i) for i in range(NB)]
        R_stp = Res("stp")
        R_onew = Res("onew")
        R_rrow = [Res("rrow%d" % i) for i in range(NB)]
        R_sq3 = Res("sq3")
        R_bank = [Res("bank%d" % i) for i in range(8)]
        for rr in (R_ufh[0], R_ufs[0], R_ufb[0]):
            alias(rr, [R_cvf[0]])
        for rr in (R_ufh[1], R_ufs[1], R_ufb[1]):
            alias(rr, [R_cvf[1]])
        for rr in (R_ufh[2], R_ufs[2], R_ufb[2]):
            alias(rr, [R_vsb[0], R_vsb[1], R_szc[0]])
        alias(R_tmpA, [R_t1[0], R_t1[1], R_t2[0]])
        alias(R_tmpB, [R_t2[1], R_szc[1], R_gsb[0]])
        alias(R_szp[0], [R_gsb[1]])
        scr_all = (R_ufh + R_ufs + R_ufb + [R_tmpA, R_tmpB, R_pfx] + R_szp + R_cvf + R_vsb + R_szc
                   + R_gsb + R_t1 + R_t2)
        alias(R_x2[0], scr_all)
        alias(R_x2[1], scr_all)
        alias(R_x2[2], [r for rs in R_pb for r in rs])
        alias(R_win[0], [R_winh[0], R_winh[1]])
        alias(R_win[1], [R_winh[2], R_winh[3]])

        out_toks = []
        ctr = {"sb": 0, "w": 0, "wh": 0, "wg": 0, "cb": 0, "wo": 0, "ss": 0, "sq": 0, "ot": 0,
               "ub": 0, "zb": 0, "qb": 0}

        def blk(b):
            return slice(b * BS, (b + 1) * BS)

        def v1(col):
            return vecs[:, col:col + 1]

        def sv(ap2d, r):
            return ap2d.rearrange("p (s r) -> p s r", r=r)

        def mm_group(bank, fns, reads_list):
            allr = []
            for rl in reads_list:
                for r in rl:
                    if r not in allr:
                        allr.append(r)
            n = len(fns)
            for i in range(n):
                last = i == n - 1
                S.op("pe", fns[i], reads=(allr if last else reads_list[i]), writes=[R_bank[bank]], inc=last)

        S.op("sp", lambda e: e.dma_start(out=vecs[:], in_=vecs_d), writes=[R_const], dma="const")
        S.op("sp", lambda e: e.dma_start(out=invc[:], in_=invc_d), writes=[R_const], dma="const")
        S.op("pool", lambda e: e.memset(ones[:], 1.0), writes=[R_const])
        S.op("pool", lambda e: e.memset(zeros[:], 0.0), writes=[R_const])
        R_ps0 = Res("ps0")
        S.op("dve", lambda e: e.tensor_scalar_mul(out=ps0[:], in0=vecs[:, V_PS:V_PS + 4], scalar1=0.5),
             reads=[R_const], writes=[R_ps0])

        def load_win(src_ap):
            slot = ctr["w"] % 2
            ctr["w"] += 1
            S.op("pool", lambda e, slot=slot, src_ap=src_ap: e.dma_start(out=win[slot][:], in_=src_ap),
                 writes=[R_win[slot]], dma="win%d" % slot)
            return slot

        def load_wg(src_ap):
            slot = ctr["wg"] % 2
            ctr["wg"] += 1
            S.op("pool", lambda e, slot=slot, src_ap=src_ap: e.dma_start(out=wg[slot][:], in_=src_ap),
                 writes=[R_wg[slot]], dma="wg%d" % slot)
            return slot

        def load_wout_q(wsrc, q):
            S.op("pool", lambda e, q=q, wsrc=wsrc: e.dma_start(out=wout[:, 4 * q:4 * q + 4, :], in_=wsrc[q]),
                 writes=[R_wout[q]], dma="wout%d" % q)

        def stat_begin():
            for k_ in list(padd.keys()):
                flush_add(k_)
            a = ctr["ss"] % NST
            ctr["ss"] += 1
            return a

        padd = {}

        def flush_add(a):
            f = padd.pop(a, None)
            if f is not None:
                f()

        def stat_tile(a, k, b, staged=False, add_eng="dve"):
            for k_ in list(padd.keys()):
                flush_add(k_)
            xin = x2s[b][:, k, :] if staged else xs[:, k, blk(b)]
            rin = R_x2[b] if staged else R_x[k][b]
            if k == 0:
                S.op("act", lambda e, a=a, xin=xin: e.activation(out=acc[a][:], in_=xin, func=AF.Square),
                     reads=[rin], writes=[R_acc[a]])
            else:
                s = ctr["sq"] % 2
                ctr["sq"] += 1
                S.op("act", lambda e, s=s, xin=xin: e.activation(out=sq[s][:], in_=xin, func=AF.Square),
                     reads=[rin], writes=[R_sq[s]])
                padd[a] = lambda s=s, a=a: S.op(
                    add_eng, lambda e, s=s, a=a: e.tensor_tensor(out=acc[a][:], in0=acc[a][:], in1=sq[s][:], op=ALU.add),
                    reads=[R_acc[a], R_sq[s]], writes=[R_acc[a]])

        def stat_finish(a):
            flush_add(a)
            bank = 6 + ctr["sb"] % 2
            ctr["sb"] += 1
            S.op("pe", lambda e, a=a, bank=bank: e.matmul(banks[bank][:, 0:BS], ones[:], acc[a][:], start=True, stop=True),
                 reads=[R_acc[a], R_const], writes=[R_bank[bank]])
            S.op("act", lambda e, a=a, bank=bank: e.activation(out=rt[a][:], in_=banks[bank][:, 0:BS], func=AF.Sqrt,
                                                                bias=EPS, scale=1.0 / D),
                 reads=[R_bank[bank]], writes=[R_rt[a]])
            S.op("dve", lambda e, a=a: e.reciprocal(out=rt[a][:], in_=rt[a][:]),
                 reads=[R_rt[a]], writes=[R_rt[a]])

        def apply_norm(a, b, gcol, staged=False):
            for k in range(KD):
                xin = x2s[b][:, k, :] if staged else xs[:, k, blk(b)]
                rin = R_x2[b] if staged else R_x[k][b]
                S.op("dve", lambda e, a=a, k=k, b=b, xin=xin: e.scalar_tensor_tensor(
                    out=xn[:, k, blk(b)], in0=xin, scalar=v1(gcol + k), in1=rt[a][:],
                    op0=ALU.mult, op1=ALU.mult),
                    reads=[rin, R_rt[a], R_const], writes=[R_xn[k][b]])

        def apply_final(a, h, b):
            for k in range(KD):
                S.op("dve", lambda e, a=a, k=k, b=b: e.scalar_tensor_tensor(
                    out=xs[:, k, blk(b)], in0=xs[:, k, blk(b)], scalar=v1(V_GF + k), in1=rt[a][:],
                    op0=ALU.mult, op1=ALU.mult),
                    reads=[R_x[k][b], R_rt[a], R_const], writes=[R_x[k][b]])
                tok = S.op("sp", lambda e, h=h, b=b, k=k: e.dma_start(out=yT[h, b, :, k, :], in_=xs[:, k, blk(b)]),
                           reads=[R_x[k][b]], dma="yo%d" % (k % 4))
                out_toks.append(tok)

        pro = {}

        def prologue_load(h, b):
            S.op("sp", lambda e, h=h, b=b: e.dma_start(out=xs[:, :, blk(b)], in_=xT[h, b]),
                 writes=[R_x[k][b] for k in range(KD)], dma="x%d" % b)

        def prologue_state(h):
            S.op("sp", lambda e, h=h: e.dma_start(out=stc[:], in_=stc_d[h]), writes=[R_stc], dma="stc")
            S.op("sp", lambda e, h=h: e.dma_start(out=stp_sb[:], in_=stp_d[h].rearrange("c p f -> p c f")),
                 writes=[R_stp], dma="stp")
            tok = S.op("sp", lambda e, h=h: e.dma_start(out=nps_d[h][:, :, 0:HK], in_=stp_d[h][:, :, HN:HN + HK]),
                       dma="npsh")
            out_toks.append(tok)

        pst = {}

        def pre_stream_step(i):
            lbuf = [sq[0], sq[1], sq3]
            lres = [R_sq[0], R_sq[1], R_sq3]
            if i < NB * KD:
                b, k = divmod(i, KD)
                s = i % 3
                S.op("sp", lambda e, s=s, b=b, k=k: e.dma_start(out=lbuf[s][:], in_=xT[1, b][:, k, :]),
                     writes=[lres[s]], dma="xs%d" % s)
            j = i - 2
            if 0 <= j < NB * KD:
                b, k = divmod(j, KD)
                s = j % 3
                if k == 0:
                    pst["a"] = stat_begin()
                a = pst["a"]
                if k == 0:
                    S.op("act", lambda e, a=a, s=s: e.activation(out=acc[a][:], in_=lbuf[s][:], func=AF.Square),
                         reads=[lres[s]], writes=[R_acc[a]])
                else:
                    S.op("act", lambda e, s=s: e.activation(out=lbuf[s][:], in_=lbuf[s][:], func=AF.Square),
                         reads=[lres[s]], writes=[lres[s]])
                    S.op("dve", lambda e, s=s, a=a: e.tensor_tensor(out=acc[a][:], in0=acc[a][:], in1=lbuf[s][:], op=ALU.add),
                         reads=[R_acc[a], lres[s]], writes=[R_acc[a]])
                if k == KD - 1:
                    pst[("fin", i + 3)] = (a, b)
            if ("fin", i) in pst:
                a, b = pst.pop(("fin", i))
                stat_finish(a)
                pst[("row", i + 3)] = (a, b)
            if ("row", i) in pst:
                a, b = pst.pop(("row", i))
                S.op("sp", lambda e, a=a, b=b: e.dma_start(out=rrow[32 * b:32 * b + 1, :], in_=rt[a][0:1, :]),
                     reads=[R_rt[a]], writes=[R_rrow[b]], dma="rrow%d" % b)

        def pre_apply(b):
            bank = 6 + ctr["sb"] % 2
            ctr["sb"] += 1
            p0 = 32 * b
            S.op("pe", lambda e, bank=bank, p0=p0: e.matmul(banks[bank][:, 0:BS], ones[p0:p0 + 1, :], rrow[p0:p0 + 1, :],
                                                           start=True, stop=True),
                 reads=[R_rrow[b], R_const], writes=[R_bank[bank]])
            for k in range(KD):
                S.op("dve", lambda e, k=k, b=b, bank=bank: e.scalar_tensor_tensor(
                    out=xn[:, k, blk(b)], in0=x2s[b][:, k, :], scalar=v1(V_G0 + k), in1=banks[bank][:, 0:BS],
                    op0=ALU.mult, op1=ALU.mult),
                    reads=[R_x2[b], R_bank[bank], R_const], writes=[R_xn[k][b]])

        def prologue_stage(h, b):
            S.op("sp", lambda e, h=h, b=b: e.dma_start(out=x2s[b], in_=xT[h, b]),
                 writes=[R_x2[b]], dma="x2s%d" % b)

        def prologue_unstage(b):
            S.op("sp", lambda e, b=b: e.dma_start(out=xs[:, :, blk(b)], in_=x2s[b]),
                 reads=[R_x2[b]], writes=[R_x[k][b] for k in range(KD)], dma="x%d" % b)

        def prologue_tiles(b, staged=False, add_eng="dve"):
            pro[b] = stat_begin()
            for k in range(KD):
                stat_tile(pro[b], k, b, staged, add_eng)

        def prologue_finish(b, staged=False):
            stat_finish(pro[b])
            apply_norm(pro[b], b, V_G0, staged)

        def preload_conv():
            return {0: load_win(w0in[0]), 1: load_win(w0in[1])}

        def preload_pool():
            return {(0, 0): load_winh(w1in[0]), (0, 1): load_winh(w1in[1]), "wg": load_wg(w1g[0])}

        def wout_phase(h, layer):
            nxt = (layer == 1 and h + 1 < NH)
            pre = None
            if layer == 0:
                pre = preload_pool()
            elif nxt:
                pre = preload_conv()
                prologue_state(h + 1)
                for b in range(NB):
                    prologue_stage(h + 1, b)
            pend = []
            for b in range(NB):
                a = stat_begin()
                for k in range(KD):
                    if k == 2:
                        for f in pend:
                            f()
                        pend = []
                    if k == 5 and nxt:
                        pre_apply(b)
                    bank = ctr["wo"] % 6
                    ctr["wo"] += 1
                    mm_group(bank, [lambda e, bank=bank, ec=ec, k=k, b=b: e.matmul(
                        banks[bank][:, 0:BS], wout[:, ec, k * 128:(k + 1) * 128], yb[:, ec, blk(b)],
                        start=(ec == 0), stop=(ec == KE - 1)) for ec in range(KE)],
                        [[R_wout[ec // 4], R_y[ec][b]] for ec in range(KE)])
                    S.op("dve", lambda e, bank=bank, k=k, b=b: e.tensor_tensor(
                        out=xs[:, k, blk(b)], in0=xs[:, k, blk(b)], in1=banks[bank][:, 0:BS], op=ALU.add),
                        reads=[R_x[k][b], R_bank[bank]], writes=[R_x[k][b]])
                    stat_tile(a, k, b)

                def fin(a=a, b=b):
                    stat_finish(a)
                    if layer == 0:
                        apply_norm(a, b, V_G1)
                    else:
                        apply_final(a, h, b)
                    if nxt:
                        prologue_unstage(b)

                if b < NB - 1:
                    pend.append(fin)
                else:
                    fin()
            return pre

        def conv_begin(h, c):
                cf = c % 2
                CF = cvf[cf]
                src = zeros[:, 0:CONV_H] if h == 0 else hconv[:, c, :]
                S.op("act", lambda e, CF=CF, src=src: e.activation(out=CF[:, 0:CONV_H], in_=src, func=AF.Copy),
                     reads=[R_hconv[c], R_const], writes=[R_cvf[cf]])
                S.op("act", lambda e, CF=CF, c=c: e.activation(
                    out=sv(CF[:, CV_S0:CVW], CONV_H + DEC_T)[:, :, 0:CONV_H], in_=stc[:, c, :, :], func=AF.Copy),
                    reads=[R_stc], writes=[R_cvf[cf]])
        cstate = {}

        def conv_block(h, c, b, slot, part="both"):
                    cf = c % 2
                    CF = cvf[cf]
                    if part in ("both", "mm"):
                        st = ctr["cb"] % 2
                        ctr["cb"] += 1
                        bk = [4 * st + q for q in range(4)]
                        if slot == "wq0":
                            wt, rw = wq0, R_wout[0]
                        else:
                            wt, rw = win[slot], R_win[slot]
                        for q in range(4):
                            mm_group(bk[q], [lambda e, q=q, k=k, b=b, wt=wt, bank=bk[q]: e.matmul(
                                banks[bank][:, 0:BS], wt[:, k, q * 128:(q + 1) * 128], xn[:, k, blk(b)],
                                start=(k == 0), stop=(k == KD - 1)) for k in range(KD)],
                                [[rw, R_xn[k][b]] for k in range(KD)])
                        cstate[(c, b)] = (st, bk)
                        if part == "mm":
                            return
                    st, bk = cstate.pop((c, b))
                    Pgb, Pgc, Pv, Pz = (banks[x] for x in bk)
                    S.op("act", lambda e, st=st, Pv=Pv: e.activation(out=vsb[st][:], in_=Pv[:, 0:BS], func=AF.Copy),
                         reads=[R_bank[bk[2]]], writes=[R_vsb[st]])
                    S.op("act", lambda e, st=st, Pz=Pz: e.activation(out=szc[st][:], in_=Pz[:, 0:BS], func=AF.Silu),
                         reads=[R_bank[bk[3]]], writes=[R_szc[st]])
                    lo = b * BS
                    if b < 2:
                        n = BS
                        S.op("dve", lambda e, CF=CF, Pgc=Pgc, st=st, lo=lo: e.tensor_tensor(
                            out=CF[:, CONV_H + lo:CONV_H + lo + BS], in0=Pgc[:, 0:BS], in1=vsb[st][:], op=ALU.mult),
                            reads=[R_bank[bk[1]], R_vsb[st]], writes=[R_cvf[cf]])
                    else:
                        n = TW
                        S.op("dve", lambda e, CF=CF, Pgc=Pgc, st=st, lo=lo: e.tensor_tensor(
                            out=CF[:, CONV_H + lo:CONV_H + lo + PB2], in0=Pgc[:, 0:PB2], in1=vsb[st][:, 0:PB2],
                            op=ALU.mult),
                            reads=[R_bank[bk[1]], R_vsb[st]], writes=[R_cvf[cf]])
                        S.op("dve", lambda e, CF=CF, Pgc=Pgc, st=st: e.tensor_tensor(
                            out=sv(CF[:, CV_S0:CVW], CONV_H + DEC_T)[:, :, CONV_H:],
                            in0=sv(Pgc[:, PB2:BS], DEC_T), in1=sv(vsb[st][:, PB2:BS], DEC_T), op=ALU.mult),
                            reads=[R_bank[bk[1]], R_vsb[st]], writes=[R_cvf[cf]])
                    S.op("dve", lambda e, Pgb=Pgb, st=st: e.tensor_tensor(
                        out=gsb[st][:], in0=Pgb[:, 0:BS], in1=szc[st][:], op=ALU.mult),
                        reads=[R_bank[bk[0]], R_szc[st]], writes=[R_gsb[st]])
                    S.op("act", lambda e, CF=CF, st=st, lo=lo, n=n, c=c: e.activation(
                        out=t1[st][:, 0:n], in_=CF[:, lo + 2:lo + 2 + n], func=AF.Identity,
                        bias=v1(V_CB + c), scale=v1(V_CW + 3 * c + 2)),
                        reads=[R_cvf[cf], R_const], writes=[R_t1[st]])
                    S.op("dve", lambda e, CF=CF, st=st, lo=lo, n=n, c=c: e.scalar_tensor_tensor(
                        out=t2[st][:, 0:n], in0=CF[:, lo + 1:lo + 1 + n], scalar=v1(V_CW + 3 * c + 1),
                        in1=t1[st][:, 0:n], op0=ALU.mult, op1=ALU.add),
                        reads=[R_cvf[cf], R_t1[st], R_const], writes=[R_t2[st]])
                    S.op("dve", lambda e, CF=CF, st=st, lo=lo, n=n, c=c: e.scalar_tensor_tensor(
                        out=t1[st][:, 0:n], in0=CF[:, lo:lo + n], scalar=v1(V_CW + 3 * c + 0),
                        in1=t2[st][:, 0:n], op0=ALU.mult, op1=ALU.add),
                        reads=[R_cvf[cf], R_t2[st], R_const], writes=[R_t1[st]])
                    if b < 2:
                        S.op("dve", lambda e, st=st, c=c, b=b: e.tensor_tensor(
                            out=yb[:, c, blk(b)], in0=gsb[st][:], in1=t1[st][:, 0:BS], op=ALU.mult),
                            reads=[R_gsb[st], R_t1[st]], writes=[R_y[c][b]])
                    else:
                        S.op("dve", lambda e, st=st, c=c: e.tensor_tensor(
                            out=yb[:, c, 2 * BS:2 * BS + PB2], in0=gsb[st][:, 0:PB2], in1=t1[st][:, 0:PB2],
                            op=ALU.mult),
                            reads=[R_gsb[st], R_t1[st]], writes=[R_y[c][b]])
                        S.op("dve", lambda e, st=st, c=c: e.tensor_tensor(
                            out=sv(yb[:, c, PH:TH], DEC_T), in0=sv(gsb[st][:, PB2:BS], DEC_T),
                            in1=sv(t1[st][:, PB2:TW], CONV_H + DEC_T)[:, :, CONV_H:], op=ALU.mult),
                            reads=[R_gsb[st], R_t1[st]], writes=[R_y[c][b]])
        def conv_end(h, c):
                cf = c % 2
                CF = cvf[cf]
                if h == 0:
                    S.op("act", lambda e, CF=CF, c=c: e.activation(out=hconv[:, c, :], in_=CF[:, PH:PH + CONV_H], func=AF.Copy),
                         reads=[R_cvf[cf]], writes=[R_hconv[c]])
                else:
                    S.op("act", lambda e, CF=CF, c=c: e.activation(out=ocp[:, c, :], in_=CF[:, PH:PH + CONV_H], func=AF.Copy),
                         reads=[R_cvf[cf]], writes=[R_ocp])
                S.op("act", lambda e, CF=CF, c=c: e.activation(
                    out=ocs[:, c, :, :], in_=sv(CF[:, CV_S0:CVW], CONV_H + DEC_T)[:, :, DEC_T:], func=AF.Copy),
                    reads=[R_cvf[cf]], writes=[R_ocs])
        def conv_phase(h, pre, hooks=None):
            slots = dict(pre)
            if hooks is not None:
                for c in (0, 1):
                    conv_begin(h, c)
                for b in range(NB):
                    conv_block(h, 0, b, slots[0], part="mm")
                    if b in hooks:
                        hooks[b]()
                    conv_block(h, 0, b, slots[0], part="ew")
                    conv_block(h, 1, b, slots[1])
                for c in (0, 1):
                    conv_end(h, c)
            else:
                for c in (0, 1):
                    conv_begin(h, c)
                    for b in range(NB):
                        conv_block(h, c, b, slots[c])
                    conv_end(h, c)
            for c in range(2, KE):
                if h == 0 and c == 2:
                    slot = "wq0"
                else:
                    slot = load_win(w0in[c])
                if h == 0:
                    if c in (3, 4, 6, 8):
                        load_wout_q(w0out, {3: 0, 4: 1, 6: 2, 8: 3}[c])
                elif c in (2, 4, 6, 8):
                    load_wout_q(w0out, (c - 2) // 2)
                conv_begin(h, c)
                for b in range(NB):
                    if h == 0:
                        pre_stream_step((c - 2) * NB + b)
                    conv_block(h, c, b, slot)
                conv_end(h, c)
            tok = S.op("sp", lambda e, h=h: e.dma_start(out=ncs_d[h], in_=ocs[:]), reads=[R_ocs], dma="ocs")
            out_toks.append(tok)
            if h == 1:
                tok = S.op("sp", lambda e: e.dma_start(out=ncp_d, in_=ocp[:]), reads=[R_ocp], dma="ocp")
                out_toks.append(tok)

        def load_winh(src_ap):
            i = ctr["wh"] % 4
            ctr["wh"] += 1
            dst = win[i // 2][:, :, (i % 2) * 256:(i % 2) * 256 + 256]
            S.op("pool", lambda e, dst=dst, src_ap=src_ap: e.dma_start(out=dst, in_=src_ap),
                 writes=[R_winh[i]], dma="winh%d" % i)
            return i

        def whs(i, j):
            base = (i % 2) * 256 + j * 128
            return win[i // 2], base

        def pool_A_begin(h, c):
            ui = c % NU
            U = uf[ui]
            src = zeros[:, 0:POOL_H] if h == 0 else hpool[:, c, :]
            S.op("act", lambda e, U=U, src=src: e.activation(out=U[:, 0:POOL_H], in_=src, func=AF.Copy),
                 reads=[R_hpool[c], R_const], writes=[R_ufh[ui]])
            S.op("act", lambda e, U=U, c=c: e.activation(
                out=sv(U[:, UF_S0:UFW], POOL_H + DEC_T)[:, :, 0:POOL_H],
                in_=stp_sb[:, c, :].rearrange("p (r s) -> p s r", s=SH), func=AF.Copy),
                reads=[R_stp], writes=[R_ufs[ui]])
        def pool_A_block(h, c, b, hslot):
                ui = c % NU
                U = uf[ui]
                wt, wbase = whs(hslot, c % 2)
                bank = ctr["ub"] % 3
                ctr["ub"] += 1
                mm_group(bank, [lambda e, bank=bank, k=k, b=b, wt=wt, wbase=wbase: e.matmul(
                    banks[bank][:, 0:BS], wt[:, k, wbase:wbase + 128], xn[:, k, blk(b)],
                    start=(k == 0), stop=(k == KD - 1)) for k in range(KD)],
                    [[R_winh[hslot], R_xn[k][b]] for k in range(KD)])
                P = banks[bank]
                lo = POOL_H + b * BS
                if b < 2:
                    S.op("act", lambda e, U=U, P=P, lo=lo: e.activation(out=U[:, lo:lo + BS], in_=P[:, 0:BS], func=AF.Copy),
                         reads=[R_bank[bank]], writes=[R_ufb[ui]])
                else:
                    S.op("act", lambda e, U=U, P=P, lo=lo: e.activation(out=U[:, lo:lo + PB2], in_=P[:, 0:PB2], func=AF.Copy),
                         reads=[R_bank[bank]], writes=[R_ufb[ui]])
                    S.op("act", lambda e, U=U, P=P: e.activation(
                        out=sv(U[:, UF_S0:UFW], POOL_H + DEC_T)[:, :, POOL_H:], in_=sv(P[:, PB2:BS], DEC_T),
                        func=AF.Copy),
                        reads=[R_bank[bank]], writes=[R_ufb[ui]])
        def pool_A_end(h, c):
            ui = c % NU
            U = uf[ui]
            if h == 0:
                S.op("act", lambda e, U=U, c=c: e.activation(out=hpool[:, c, :], in_=U[:, PH:PH + POOL_H], func=AF.Copy),
                     reads=[R_ufb[ui]], writes=[R_hpool[c]])
            else:
                S.op("act", lambda e, U=U, c=c: e.activation(out=opp[:, c, :], in_=U[:, PH:PH + POOL_H], func=AF.Copy),
                     reads=[R_ufb[ui]], writes=[R_opp])
            S.op("act", lambda e, U=U, c=c: e.activation(
                out=onew[:, c, :].rearrange("p (r s) -> p s r", s=SH),
                in_=sv(U[:, UF_S0:UFW], POOL_H + DEC_T)[:, :, POOL_H:], func=AF.Copy),
                reads=[R_ufb[ui]], writes=[R_onew])

        def pool_A_chunk(h, c, hslot):
            pool_A_begin(h, c)
            for b in range(NB):
                pool_A_block(h, c, b, hslot)
            pool_A_end(h, c)

        def pool_chunk(h, c):
            g = c // 4
            ui = c % NU
            U = uf[ui]
            RU = [R_ufh[ui], R_ufs[ui], R_ufb[ui]]
            w = WINDOWS[g]
            cur, Rcur = U, RU
            tmps = [(tmpA, [R_tmpA]), (tmpB, [R_tmpB])]
            if w == 2:
                pi = c % 8
                P = pb[pi]
                Rp = R_pb[pi]
                S.op("dve", lambda e, P=P, U=U: e.tensor_tensor(
                    out=P[:, 0:PH], in0=U[:, POOL_H - 1:POOL_H - 1 + PH], in1=U[:, POOL_H:POOL_H + PH],
                    op=ALU.subtract),
                    reads=RU, writes=[Rp[0], Rp[1], Rp[2]])
                S.op("dve", lambda e, P=P, U=U: e.tensor_tensor(
                    out=sv(P[:, PH:TH], DEC_T),
                    in0=sv(U[:, UF_S0:UFW], POOL_H + DEC_T)[:, :, POOL_H - 1:POOL_H - 1 + DEC_T],
                    in1=sv(U[:, UF_S0:UFW], POOL_H + DEC_T)[:, :, POOL_H:], op=ALU.subtract),
                    reads=RU, writes=[Rp[2]])
                if h == 0:
                    S.op("dve", lambda e, P=P: e.memset(P[:, 0:1], 0.0), writes=[Rp[0]])
                return
            sh = 1
            lvl = 0
            if w == 16:
                S.op("dve", lambda e, U=U: e.tensor_tensor_scan(
                    out=tmpA[:, 0:UFW], data0=U[:, 0:UFW], data1=U[:, 0:UFW], initial=0.0,
                    op0=ALU.add, op1=ALU.bypass),
                    reads=RU, writes=[R_tmpA])
                S.op("dve", lambda e: e.tensor_tensor(
                    out=tmpB[:, 16:UFW], in0=tmpA[:, 16:UFW], in1=tmpA[:, 0:UFW - 16], op=ALU.subtract),
                    reads=[R_tmpA], writes=[R_tmpB])
                S.op("dve", lambda e: e.tensor_copy(out=tmpB[:, 15:16], in_=tmpA[:, 15:16]),
                     reads=[R_tmpA], writes=[R_tmpB])
                cur, Rcur = tmpB, [R_tmpB]
                sh = w
            while sh < w:
                dst, Rdst = tmps[lvl % 2]
                lo = 2 * sh - 1
                S.op("dve", lambda e, dst=dst, cur=cur, lo=lo, sh=sh: e.tensor_tensor(
                    out=dst[:, lo:UFW], in0=cur[:, lo:UFW], in1=cur[:, lo - sh:UFW - sh], op=ALU.add),
                    reads=Rcur, writes=Rdst)
                cur, Rcur = dst, Rdst
                sh *= 2
                lvl += 1
            pi = c % 8
            P = pb[pi]
            Rp = R_pb[pi]
            if h == 0:
                S.op("dve", lambda e, cur=cur, g=g: e.tensor_tensor(
                    out=cur[:, POOL_H:POOL_H + 16], in0=cur[:, POOL_H:POOL_H + 16], in1=invc[:, g, :], op=ALU.mult),
                    reads=Rcur + [R_const], writes=Rcur)
            S.op("dve", lambda e, P=P, cur=cur, U=U, w=w: e.scalar_tensor_tensor(
                out=P[:, 0:PH], in0=cur[:, POOL_H:POOL_H + PH], scalar=1.0 / w, in1=U[:, POOL_H:POOL_H + PH],
                op0=ALU.mult, op1=ALU.subtract),
                reads=Rcur + RU, writes=[Rp[0], Rp[1], Rp[2]])
            S.op("dve", lambda e, P=P, cur=cur, U=U, w=w: e.scalar_tensor_tensor(
                out=sv(P[:, PH:TH], DEC_T), in0=sv(cur[:, UF_S0:UFW], POOL_H + DEC_T)[:, :, POOL_H:],
                scalar=1.0 / w, in1=sv(U[:, UF_S0:UFW], POOL_H + DEC_T)[:, :, POOL_H:],
                op0=ALU.mult, op1=ALU.subtract),
                reads=Rcur + RU, writes=[Rp[2]])

        def pool_B_chunk(h, c, hslot, gslot, blocks=range(NB)):
            g = c // 4
            ci = c % 4
            wt, wbase = whs(hslot, c % 2)
            for b in blocks:
                zb = 3 + ctr["zb"] % 3
                ctr["zb"] += 1
                qb = 6 + ctr["qb"] % 2
                ctr["qb"] += 1
                mm_group(zb, [lambda e, zb=zb, k=k, b=b, wt=wt, wbase=wbase: e.matmul(
                    banks[zb][:, 0:BS], wt[:, k, wbase:wbase + 128], xn[:, k, blk(b)],
                    start=(k == 0), stop=(k == KD - 1)) for k in range(KD)],
                    [[R_winh[hslot], R_xn[k][b]] for k in range(KD)])
                mm_group(qb, [lambda e, qb=qb, kc=kc, b=b, gslot=gslot, ci=ci, pi=(4 * g + kc) % 8: e.matmul(
                    banks[qb][:, 0:BS], wg[gslot][:, kc, ci * 128:(ci + 1) * 128], pb[pi][:, blk(b)],
                    start=(kc == 0), stop=(kc == 3)) for kc in range(4)],
                    [[R_wg[gslot], R_pb[(4 * g + kc) % 8][b]] for kc in range(4)])
                s = ctr["zb"] % 2
                S.op("act", lambda e, s=s, zb=zb: e.activation(out=szp[s][:], in_=banks[zb][:, 0:BS], func=AF.Silu),
                     reads=[R_bank[zb]], writes=[R_szp[s]])
                sc = ps0[:, c:c + 1] if c < 4 else v1(V_PS + c)
                S.op("dve", lambda e, s=s, qb=qb, c=c, b=b, sc=sc: e.scalar_tensor_tensor(
                    out=yb[:, c, blk(b)], in0=banks[qb][:, 0:BS], scalar=sc, in1=szp[s][:],
                    op0=ALU.mult, op1=ALU.mult),
                    reads=[R_bank[qb], R_szp[s], R_const, R_ps0], writes=[R_y[c][b]])

        def pool_phase(h, pre):
            us = dict(pre)
            for c in range(3):
                pool_A_begin(h, c)
                for b in (0, 1):
                    pool_A_block(h, c, b, us[(0, c // 2)])
            for c in range(3):
                pool_A_block(h, c, 2, us[(0, c // 2)])
                pool_A_end(h, c)
                pool_chunk(h, c)
            pool_A_chunk(h, 3, us[(0, 1)])
            pool_chunk(h, 3)
            gnext = us.pop("wg")
            for g in range(4):
                gslot = gnext
                load_wout_q(w1out, g)
                for hg in range(2):
                    if g + 1 < 4:
                        us[(g + 1, hg)] = load_winh(w1in[((g + 1) * 2 + 0) * 2 + hg])
                    zs = load_winh(w1in[(g * 2 + 1) * 2 + hg])
                    if hg == 1 and g + 1 < 4:
                        gnext = load_wg(w1g[g + 1])
                    for cj in range(2):
                        c = 4 * g + 2 * hg + cj
                        if g + 1 < 4:
                            pool_A_chunk(h, c + 4, us[(g + 1, hg)])
                            if c % 4 == 3:
                                pool_chunk(h, c + 4)
                                pool_B_chunk(h, c, zs, gslot)
                            else:
                                pool_B_chunk(h, c, zs, gslot, blocks=(0, 1))
                                pool_chunk(h, c + 4)
                                pool_B_chunk(h, c, zs, gslot, blocks=(2,))
                        else:
                            pool_B_chunk(h, c, zs, gslot)
            tok = S.op("sp", lambda e, h=h: e.dma_start(
                out=nps_d[h].rearrange("c p f -> p c f")[:, :, HK:HK + HN], in_=onew[:]),
                reads=[R_onew], dma="onew")
            out_toks.append(tok)
            if h == 1:
                tok = S.op("sp", lambda e: e.dma_start(out=npp_d, in_=opp[:]), reads=[R_opp], dma="opp")
                out_toks.append(tok)

        for b in range(NB):
            prologue_load(0, b)
        prologue_state(0)
        pre = preload_conv()
        wq0 = wout[:, 0:4, :].rearrange("p a (b f) -> p (a b) f", f=512)
        S.op("pool", lambda e: e.dma_start(out=wq0, in_=w0in[2]), writes=[R_wout[0]], dma="wout0")
        prologue_tiles(0)
        prologue_finish(0)
        prologue_tiles(1)
        for h in range(NH):
            conv_phase(h, pre, hooks=({0: lambda: (prologue_finish(1), prologue_tiles(2)), 1: lambda: prologue_finish(2)} if h == 0 else None))
            pre = wout_phase(h, 0)
            pool_phase(h, pre)
            pre = wout_phase(h, 1)

        S.final_wait("sp", out_toks)

        with nc.Block() as block:
            S.emit(block)
    return nc


_NC_CACHE = {}


def _get_program():
    if "nc" not in _NC_CACHE:
        _NC_CACHE["nc"] = build_program()
    return _NC_CACHE["nc"]


def _chunk_vec(v):
    return np.ascontiguousarray(v.reshape(-1, 128).T)


def kernel(x_prompt, x_sample, state_conv, state_pool, norm_g, final_norm_g,
           conv_w_in, conv_w, conv_b, conv_w_out,
           pool_w_in, pool_w_grp, pool_scale, pool_w_out):
    f = np.float32
    x_prompt = np.asarray(x_prompt, f)
    x_sample = np.asarray(x_sample, f)
    state_conv = np.asarray(state_conv, f)
    state_pool = np.asarray(state_pool, f)
    SPC = DEC_B // NCORE

    w0 = np.asarray(conv_w_in, f)[0]
    w0in = np.ascontiguousarray(
        w0.reshape(KD, 128, 4, KE, 128).transpose(3, 1, 0, 2, 4).reshape(KE, 128, KD, 512))
    wo0 = np.asarray(conv_w_out, f)[0]
    w0out = np.ascontiguousarray(wo0.reshape(4, 4, 128, D).transpose(0, 2, 1, 3))
    w1 = np.asarray(pool_w_in, f)[0]
    w1in = np.ascontiguousarray(
        w1.reshape(KD, 128, 2, 4, 2, 256).transpose(3, 2, 4, 1, 0, 5).reshape(16, 128, KD, 256))
    wgm = np.asarray(pool_w_grp, f)[0]
    w1g = np.ascontiguousarray(wgm.reshape(4, 4, 128, 512).transpose(0, 2, 1, 3))
    wo1 = np.asarray(pool_w_out, f)[0]
    w1out = np.ascontiguousarray(wo1.reshape(4, 4, 128, D).transpose(0, 2, 1, 3))

    vecs = np.zeros((128, NV), f)
    cw = np.asarray(conv_w, f)[0]
    vecs[:, V_CW:V_CW + 48] = cw.reshape(3, KE, 128).transpose(2, 1, 0).reshape(128, 48)
    vecs[:, V_CB:V_CB + KE] = _chunk_vec(np.asarray(conv_b, f)[0])
    vecs[:, V_PS:V_PS + KE] = _chunk_vec(np.asarray(pool_scale, f)[0])
    ng = np.asarray(norm_g, f)
    vecs[:, V_G0:V_G0 + KD] = _chunk_vec(ng[0])
    vecs[:, V_G1:V_G1 + KD] = _chunk_vec(ng[1])
    vecs[:, V_GF:V_GF + KD] = _chunk_vec(np.asarray(final_norm_g, f))

    invc = np.zeros((128, 4, 16), f)
    for g, w in enumerate(WINDOWS):
        invc[:, g, :] = (np.float32(w) / np.minimum(np.float32(w), np.arange(16, dtype=f) + 1.0)).astype(f)[None, :]

    in_maps = []
    for core in range(NCORE):
        xp = x_prompt[core]
        xsm = x_sample[core * SPC:(core + 1) * SPC]
        halves = []
        for h in range(NH):
            halves.append(np.concatenate(
                [xp[h * PH:(h + 1) * PH], xsm[h * SH:(h + 1) * SH].reshape(SH * DEC_T, D)], axis=0))
        X = np.stack(halves)
        xT = np.ascontiguousarray(X.reshape(NH, NB, BS, KD, 128).transpose(0, 1, 4, 3, 2))
        sc = state_conv[0, core * SPC:(core + 1) * SPC]
        stc = np.ascontiguousarray(sc.reshape(NH, SH, CONV_H, KE, 128).transpose(0, 4, 3, 1, 2))
        sp_ = state_pool[0, core * SPC:(core + 1) * SPC]
        stp = np.ascontiguousarray(
            sp_.reshape(NH, SH, POOL_H, KE, 128).transpose(0, 3, 4, 2, 1).reshape(NH, KE, 128, POOL_H * SH))
        in_maps.append({"xT": xT, "w0in": w0in, "w0out": w0out, "w1in": w1in, "w1g": w1g,
                        "w1out": w1out, "vecs": vecs, "stc": stc, "stp": stp, "invc": invc})

    nc = _get_program()
    res = run_bass_kernel_spmd(nc, in_maps, core_ids=list(range(NCORE)))

    y_prompt = np.empty((NCORE, SEQ, D), f)
    y_sample = np.empty((DEC_B, DEC_T, D), f)
    ncp = np.empty((1, NCORE, CONV_H, E), f)
    ncs = np.empty((1, DEC_B, CONV_H, E), f)
    npp = np.empty((1, NCORE, POOL_H, E), f)
    nps = np.empty((1, DEC_B, POOL_H, E), f)
    for core in range(NCORE):
        r = res.results[core]
        Y = np.asarray(r["yT"]).transpose(0, 1, 4, 3, 2).reshape(NH, TH, D)
        for h in range(NH):
            y_prompt[core, h * PH:(h + 1) * PH] = Y[h, :PH]
            y_sample[core * SPC + h * SH:core * SPC + (h + 1) * SH] = Y[h, PH:].reshape(SH, DEC_T, D)
        ncp[0, core] = np.asarray(r["ncp"]).transpose(2, 1, 0).reshape(CONV_H, E)
        a = np.asarray(r["ncs"])
        ncs[0, core * SPC:(core + 1) * SPC] = a.transpose(0, 3, 4, 2, 1).reshape(SPC, CONV_H, E)
        npp[0, core] = np.asarray(r["npp"]).transpose(2, 1, 0).reshape(POOL_H, E)
        a = np.asarray(r["nps"]).reshape(NH, KE, 128, POOL_H, SH)
        nps[0, core * SPC:(core + 1) * SPC] = a.transpose(0, 4, 3, 1, 2).reshape(SPC, POOL_H, E)
    return (y_prompt, y_sample, ncp, ncs, npp, nps)
```

```python
from contextlib import ExitStack

import numpy as np
import concourse.bass as bass
import concourse.mybir as mybir
from concourse.bass_utils import run_bass_kernel_spmd

F32 = mybir.dt.float32
BF16 = mybir.dt.bfloat16
AF = mybir.ActivationFunctionType
ALU = mybir.AluOpType

NCORE = 8
D = 1024
E = 2048
SEQ = 2048
DEC_B = 128
DEC_T = 4
CONV_H = 2
POOL_H = 15
WINDOWS = (2, 4, 8, 16)
EPS = 1e-6

KD = D // 128
KE = E // 128
NH = 2
PH = SEQ // NH
SH = (DEC_B // NCORE) // NH
TH = PH + SH * DEC_T
NB = 3
BS = TH // NB
PB2 = PH - 2 * BS
CVW = CONV_H + PH + SH * (CONV_H + DEC_T)
UFW = POOL_H + PH + SH * (POOL_H + DEC_T)
CV_S0 = CONV_H + PH
UF_S0 = POOL_H + PH
TW = PB2 + SH * (CONV_H + DEC_T)

V_CW = 0
V_CB = 48
V_PS = 64
V_G0 = 80
V_G1 = 88
V_GF = 96
NV = 104

HN = DEC_T * SH
HK = (POOL_H - DEC_T) * SH
SEM_ROLL = 3000


class Res:
    __slots__ = ("name", "w", "r", "al")

    def __init__(self, name):
        self.name = name
        self.w = None
        self.r = {}
        self.al = []


def alias(a, bs):
    for b in bs:
        a.al.append(b)
        b.al.append(a)


class Sched:
    ENGS = ("pe", "act", "dve", "pool", "sp")

    def __init__(self, nc, stack):
        self.nc = nc
        self.stack = stack
        self.streams = {e: [] for e in self.ENGS}
        self.sems = {}
        self.cnt = {}
        self.cur = {e: (e, 0) for e in ("pe", "act", "dve", "pool")}
        self.seen = {e: {} for e in self.ENGS}
        self.nwaits = 0

    def _sem(self, key):
        if key not in self.sems:
            nm = "s_" + "_".join(str(k) for k in key)
            self.sems[key] = self.stack.enter_context(self.nc.semaphore(nm))
            self.cnt[key] = 0
        return self.sems[key]

    def _deps(self, engine, reads, writes):
        toks = []
        for r0 in reads:
            for r in [r0] + r0.al:
                if r.w is not None:
                    toks.append((r.w, False))
        for r0 in writes:
            for r in [r0] + r0.al:
                if r.w is not None:
                    toks.append((r.w, False))
                for k, v in r.r.items():
                    toks.append(((k, v), True))
        need = {}
        for (k, v), is_war in toks:
            if k[0] == engine and engine == "pe":
                continue
            if self.seen[engine].get(k, 0) >= v:
                continue
            if need.get(k, 0) < v:
                need[k] = v
        waits = []
        for k, v in need.items():
            self.seen[engine][k] = v
            waits.append((k, v))
        return waits

    def op(self, engine, fn, reads=(), writes=(), inc=True, dma=None):
        waits = self._deps(engine, reads, writes)
        self.nwaits += len(waits)
        tok = None
        incspec = None
        if dma is not None:
            key = ("dma", dma)
            self._sem(key)
            self.cnt[key] += 16
            tok = (key, self.cnt[key])
            incspec = (key, 16)
        elif inc:
            key = self.cur[engine]
            self._sem(key)
            if self.cnt[key] >= SEM_ROLL:
                key = (engine, key[1] + 1)
                self.cur[engine] = key
                self._sem(key)
            self.cnt[key] += 1
            tok = (key, self.cnt[key])
            incspec = (key, 1)
        self.streams[engine].append((waits, fn, incspec))
        if tok is not None:
            for r in writes:
                r.w = tok
                r.r = {}
            for r in reads:
                if r.r.get(tok[0], 0) < tok[1]:
                    r.r[tok[0]] = tok[1]
        return tok

    def final_wait(self, engine, toks):
        need = {}
        for (k, v) in toks:
            if self.seen[engine].get(k, 0) >= v:
                continue
            if need.get(k, 0) < v:
                need[k] = v
        waits = []
        for k, v in need.items():
            self.seen[engine][k] = v
            waits.append((k, v))
        self.streams[engine].append((waits, None, None))

    def emit(self, block):
        for e, attr in (("pe", "tensor"), ("act", "scalar"), ("dve", "vector"),
                        ("pool", "gpsimd"), ("sp", "sync")):
            stream = self.streams[e]

            def body(eng, stream=stream):
                for waits, fn, incspec in stream:
                    for (k, v) in waits:
                        eng.wait_ge(self.sems[k], v)
                    if fn is None:
                        continue
                    ins = fn(eng)
                    if incspec is not None:
                        ins.then_inc(self.sems[incspec[0]], incspec[1])

            getattr(block, attr)(body)


def build_program():
    nc = bass.Bass("TRN2", target_bir_lowering=False)

    def din(name, shape):
        return nc.dram_tensor(name, list(shape), F32, kind="ExternalInput").ap()

    def dout(name, shape):
        return nc.dram_tensor(name, list(shape), F32, kind="ExternalOutput").ap()

    xT = din("xT", (NH, NB, 128, KD, BS))
    w0in = din("w0in", (KE, 128, KD, 512))
    w0out = din("w0out", (4, 128, 4, D))
    w1in = din("w1in", (16, 128, KD, 256))
    w1g = din("w1g", (4, 128, 4, 512))
    w1out = din("w1out", (4, 128, 4, D))
    vecs_d = din("vecs", (128, NV))
    stc_d = din("stc", (NH, 128, KE, SH, CONV_H))
    stp_d = din("stp", (NH, KE, 128, POOL_H * SH))
    invc_d = din("invc", (128, 4, 16))

    yT = dout("yT", (NH, NB, 128, KD, BS))
    ncp_d = dout("ncp", (128, KE, CONV_H))
    ncs_d = dout("ncs", (NH, 128, KE, SH, CONV_H))
    npp_d = dout("npp", (128, KE, POOL_H))
    nps_d = dout("nps", (NH, KE, 128, POOL_H * SH))

    with ExitStack() as stack:
        def sb(name, shape, dt=F32):
            return stack.enter_context(nc.sbuf_tensor(name, list(shape), dt))

        def ps(name):
            return stack.enter_context(nc.psum_tensor(name, [128, 512], F32))

        S = Sched(nc, stack)

        xs = sb("xs", (128, KD, TH))
        xn = sb("xn", (128, KD, TH), BF16)
        yb = sb("yb", (128, KE, TH), BF16)
        win = [sb("win%d" % i, (128, KD, 512), BF16) for i in range(2)]
        wout = sb("wout", (128, KE, D), BF16)
        wg = [sb("wg%d" % i, (128, 4, 512), BF16) for i in range(2)]
        vecs = sb("vecs_sb", (128, NV))
        ones = sb("ones", (128, 128))
        zeros = sb("zeros", (128, 16))
        invc = sb("invc_sb", (128, 4, 16))
        stc = sb("stc_sb", (128, KE, SH, CONV_H))
        ocs = sb("ocs", (128, KE, SH, CONV_H))
        ocp = sb("ocp", (128, KE, CONV_H))
        opp = sb("opp", (128, KE, POOL_H))
        hconv = sb("hconv", (128, KE, CONV_H))
        hpool = sb("hpool", (128, KE, POOL_H))
        NU = 3
        scr = sb("scr", (128, 5 * UFW + 16 + 2 * BS))
        uf = [scr[:, i * UFW:(i + 1) * UFW] for i in range(NU)]
        tmpA = scr[:, 3 * UFW:4 * UFW]
        tmpB = scr[:, 4 * UFW:5 * UFW]
        o5 = 5 * UFW
        pfx = scr[:, o5:o5 + 16]
        szp = [scr[:, o5 + 16 + i * BS:o5 + 16 + (i + 1) * BS] for i in range(2)]
        cvf = [uf[0][:, 0:CVW], uf[1][:, 0:CVW]]
        vsb = [uf[2][:, 0:BS], uf[2][:, BS:2 * BS]]
        szc = [uf[2][:, 2 * BS:3 * BS], tmpB[:, TW:TW + BS]]
        gsb = [tmpB[:, TW + BS:TW + 2 * BS], szp[0]]
        t1 = [tmpA[:, 0:TW], tmpA[:, TW:2 * TW]]
        t2 = [tmpA[:, 2 * TW:3 * TW], tmpB[:, 0:TW]]
        pbraw = sb("pbraw", (128, 8 * (TH // 2)))
        pb = [pbraw[:, i * (TH // 2):(i + 1) * (TH // 2)].bitcast(BF16) for i in range(8)]
        XB = KD * BS
        x2s = [scr[:, 0:XB].rearrange("p (k t) -> p k t", k=KD),
               scr[:, XB:2 * XB].rearrange("p (k t) -> p k t", k=KD),
               pbraw[:, 0:XB].rearrange("p (k t) -> p k t", k=KD)]
        NST = 2
        acc = [sb("acc%d" % i, (128, BS)) for i in range(NST)]
        sq = [sb("sq%d" % i, (128, BS)) for i in range(2)]
        rt = [sb("rt%d" % i, (128, BS)) for i in range(NST)]
        rstd = rt
        rrow = sb("rrow", (128, BS))
        ps0 = sb("ps0", (128, 4))
        sq3 = sb("sq3", (128, BS))
        stp_sb = sb("stp_sb", (128, KE, POOL_H * SH))
        onew = sb("onew", (128, KE, DEC_T * SH))

        banks = [ps("bank%d" % i) for i in range(8)]

        R_x = [[Res("x%d_%d" % (k, b)) for b in range(NB)] for k in range(KD)]
        R_xn = [[Res("xn%d_%d" % (k, b)) for b in range(NB)] for k in range(KD)]
        R_y = [[Res("y%d_%d" % (c, b)) for b in range(NB)] for c in range(KE)]
        R_win = [Res("win%d" % i) for i in range(2)]
        R_wout = [Res("wout%d" % i) for i in range(4)]
        R_wg = [Res("wg%d" % i) for i in range(2)]
        R_const = Res("const")
        R_stc = Res("stc")
        R_ocs = Res("ocs")
        R_ocp = Res("ocp")
        R_opp = Res("opp")
        R_hconv = [Res("hconv%d" % c) for c in range(KE)]
        R_hpool = [Res("hpool%d" % c) for c in range(KE)]
        R_cvf = [Res("cvf%d" % i) for i in range(2)]
        R_vsb = [Res("vsb%d" % i) for i in range(2)]
        R_szc = [Res("szc%d" % i) for i in range(2)]
        R_gsb = [Res("gsb%d" % i) for i in range(2)]
        R_t1 = [Res("t1_%d" % i) for i in range(2)]
        R_t2 = [Res("t2_%d" % i) for i in range(2)]
        R_ufh = [Res("ufh%d" % i) for i in range(NU)]
        R_ufs = [Res("ufs%d" % i) for i in range(NU)]
        R_ufb = [Res("ufb%d" % i) for i in range(NU)]
        R_winh = [Res("winh%d" % i) for i in range(4)]
        R_tmpA = Res("tmpA")
        R_tmpB = Res("tmpB")
        R_pfx = Res("pfx")
        R_pb = [[Res("pb%d_%d" % (i, b)) for b in range(NB)] for i in range(8)]
        R_szp = [Res("szp%d" % i) for i in range(2)]
        R_acc = [Res("acc%d" % i) for i in range(3)]
        R_sq = [Res("sq%d" % i) for i in range(2)]
        R_rt = [Res("rt%d" % i) for i in range(3)]
        R_x2 = [Res("x2s%d" % i) for i in range(NB)]
        R_stp = Res("stp")
        R_onew = Res("onew")
        R_rrow = [Res("rrow%d" % i) for i in range(NB)]
        R_sq3 = Res("sq3")
        R_bank = [Res("bank%d" % i) for i in range(8)]
        for rr in (R_ufh[0], R_ufs[0], R_ufb[0]):
            alias(rr, [R_cvf[0]])
        for rr in (R_ufh[1], R_ufs[1], R_ufb[1]):
            alias(rr, [R_cvf[1]])
        for rr in (R_ufh[2], R_ufs[2], R_ufb[2]):
            alias(rr, [R_vsb[0], R_vsb[1], R_szc[0]])
        alias(R_tmpA, [R_t1[0], R_t1[1], R_t2[0]])
        alias(R_tmpB, [R_t2[1], R_szc[1], R_gsb[0]])
        alias(R_szp[0], [R_gsb[1]])
        scr_all = (R_ufh + R_ufs + R_ufb + [R_tmpA, R_tmpB, R_pfx] + R_szp + R_cvf + R_vsb + R_szc
                   + R_gsb + R_t1 + R_t2)
        alias(R_x2[0], scr_all)
        alias(R_x2[1], scr_all)
        alias(R_x2[2], [r for rs in R_pb for r in rs])
        alias(R_win[0], [R_winh[0], R_winh[1]])
        alias(R_win[1], [R_winh[2], R_winh[3]])

        out_toks = []
        ctr = {"sb": 0, "w": 0, "wh": 0, "wg": 0, "cb": 0, "wo": 0, "ss": 0, "sq": 0, "ot": 0,
               "ub": 0, "zb": 0, "qb": 0}

        def blk(b):
            return slice(b * BS, (b + 1) * BS)

        def v1(col):
            return vecs[:, col:col + 1]

        def sv(ap2d, r):
            return ap2d.rearrange("p (s r) -> p s r", r=r)

        def mm_group(bank, fns, reads_list):
            allr = []
            for rl in reads_list:
                for r in rl:
                    if r not in allr:
                        allr.append(r)
            n = len(fns)
            for i in range(n):
                last = i == n - 1
                S.op("pe", fns[i], reads=(allr if last else reads_list[i]), writes=[R_bank[bank]], inc=last)

        S.op("sp", lambda e: e.dma_start(out=vecs[:], in_=vecs_d), writes=[R_const], dma="const")
        S.op("sp", lambda e: e.dma_start(out=invc[:], in_=invc_d), writes=[R_const], dma="const")
        S.op("pool", lambda e: e.memset(ones[:], 1.0), writes=[R_const])
        S.op("pool", lambda e: e.memset(zeros[:], 0.0), writes=[R_const])
        R_ps0 = Res("ps0")
        S.op("dve", lambda e: e.tensor_scalar_mul(out=ps0[:], in0=vecs[:, V_PS:V_PS + 4], scalar1=0.5),
             reads=[R_const], writes=[R_ps0])

        def load_win(src_ap, after=()):
            slot = ctr["w"] % 2
            ctr["w"] += 1
            S.op("pool", lambda e, slot=slot, src_ap=src_ap: e.dma_start(out=win[slot][:], in_=src_ap),
                 reads=list(after), writes=[R_win[slot]], dma="win%d" % slot)
            return slot

        def load_wg(src_ap):
            slot = ctr["wg"] % 2
            ctr["wg"] += 1
            S.op("pool", lambda e, slot=slot, src_ap=src_ap: e.dma_start(out=wg[slot][:], in_=src_ap),
                 writes=[R_wg[slot]], dma="wg%d" % slot)
            return slot

        def load_wout_q(wsrc, q):
            S.op("pool", lambda e, q=q, wsrc=wsrc: e.dma_start(out=wout[:, 4 * q:4 * q + 4, :], in_=wsrc[q]),
                 writes=[R_wout[q]], dma="wout%d" % q)

        def stat_begin():
            for k_ in list(padd.keys()):
                flush_add(k_)
            a = ctr["ss"] % NST
            ctr["ss"] += 1
            return a

        padd = {}

        def flush_add(a):
            f = padd.pop(a, None)
            if f is not None:
                f()

        def stat_tile(a, k, b, staged=False, add_eng="dve"):
            for k_ in list(padd.keys()):
                flush_add(k_)
            xin = x2s[b][:, k, :] if staged else xs[:, k, blk(b)]
            rin = R_x2[b] if staged else R_x[k][b]
            if k == 0:
                S.op("act", lambda e, a=a, xin=xin: e.activation(out=acc[a][:], in_=xin, func=AF.Square),
                     reads=[rin], writes=[R_acc[a]])
            else:
                s = ctr["sq"] % 2
                ctr["sq"] += 1
                S.op("act", lambda e, s=s, xin=xin: e.activation(out=sq[s][:], in_=xin, func=AF.Square),
                     reads=[rin], writes=[R_sq[s]])
                padd[a] = lambda s=s, a=a: S.op(
                    add_eng, lambda e, s=s, a=a: e.tensor_tensor(out=acc[a][:], in0=acc[a][:], in1=sq[s][:], op=ALU.add),
                    reads=[R_acc[a], R_sq[s]], writes=[R_acc[a]])

        def stat_finish(a):
            flush_add(a)
            bank = 6 + ctr["sb"] % 2
            ctr["sb"] += 1
            S.op("pe", lambda e, a=a, bank=bank: e.matmul(banks[bank][:, 0:BS], ones[:], acc[a][:], start=True, stop=True),
                 reads=[R_acc[a], R_const], writes=[R_bank[bank]])
            S.op("act", lambda e, a=a, bank=bank: e.activation(out=rt[a][:], in_=banks[bank][:, 0:BS], func=AF.Sqrt,
                                                                bias=EPS, scale=1.0 / D),
                 reads=[R_bank[bank]], writes=[R_rt[a]])
            S.op("dve", lambda e, a=a: e.reciprocal(out=rt[a][:], in_=rt[a][:]),
                 reads=[R_rt[a]], writes=[R_rt[a]])

        def apply_norm(a, b, gcol, staged=False):
            for k in range(KD):
                xin = x2s[b][:, k, :] if staged else xs[:, k, blk(b)]
                rin = R_x2[b] if staged else R_x[k][b]
                S.op("dve", lambda e, a=a, k=k, b=b, xin=xin: e.scalar_tensor_tensor(
                    out=xn[:, k, blk(b)], in0=xin, scalar=v1(gcol + k), in1=rt[a][:],
                    op0=ALU.mult, op1=ALU.mult),
                    reads=[rin, R_rt[a], R_const], writes=[R_xn[k][b]])

        def apply_final(a, h, b):
            for k in range(KD):
                S.op("dve", lambda e, a=a, k=k, b=b: e.scalar_tensor_tensor(
                    out=xs[:, k, blk(b)], in0=xs[:, k, blk(b)], scalar=v1(V_GF + k), in1=rt[a][:],
                    op0=ALU.mult, op1=ALU.mult),
                    reads=[R_x[k][b], R_rt[a], R_const], writes=[R_x[k][b]])
                tok = S.op("sp", lambda e, h=h, b=b, k=k: e.dma_start(out=yT[h, b, :, k, :], in_=xs[:, k, blk(b)]),
                           reads=[R_x[k][b]], dma="yo%d" % (k % 4))
                out_toks.append(tok)

        pro = {}

        def prologue_load(h, b):
            S.op("sp", lambda e, h=h, b=b: e.dma_start(out=xs[:, :, blk(b)], in_=xT[h, b]),
                 writes=[R_x[k][b] for k in range(KD)], dma="x%d" % b)

        def prologue_state(h):
            S.op("sp", lambda e, h=h: e.dma_start(out=stc[:], in_=stc_d[h]), writes=[R_stc], dma="stc")
            S.op("sp", lambda e, h=h: e.dma_start(out=stp_sb[:], in_=stp_d[h].rearrange("c p f -> p c f")),
                 writes=[R_stp], dma="stp")
            tok = S.op("sp", lambda e, h=h: e.dma_start(out=nps_d[h][:, :, 0:HK], in_=stp_d[h][:, :, HN:HN + HK]),
                       dma="npsh")
            out_toks.append(tok)

        pst = {}

        def pre_stream_step(i):
            lbuf = [sq[0], sq[1], sq3]
            lres = [R_sq[0], R_sq[1], R_sq3]
            if i < NB * KD:
                b, k = divmod(i, KD)
                s = i % 3
                S.op("sp", lambda e, s=s, b=b, k=k: e.dma_start(out=lbuf[s][:], in_=xT[1, b][:, k, :]),
                     writes=[lres[s]], dma="xs%d" % s)
            j = i - 2
            if 0 <= j < NB * KD:
                b, k = divmod(j, KD)
                s = j % 3
                if k == 0:
                    pst["a"] = stat_begin()
                a = pst["a"]
                if k == 0:
                    S.op("act", lambda e, a=a, s=s: e.activation(out=acc[a][:], in_=lbuf[s][:], func=AF.Square),
                         reads=[lres[s]], writes=[R_acc[a]])
                else:
                    S.op("act", lambda e, s=s: e.activation(out=lbuf[s][:], in_=lbuf[s][:], func=AF.Square),
                         reads=[lres[s]], writes=[lres[s]])
                    S.op("dve", lambda e, s=s, a=a: e.tensor_tensor(out=acc[a][:], in0=acc[a][:], in1=lbuf[s][:], op=ALU.add),
                         reads=[R_acc[a], lres[s]], writes=[R_acc[a]])
                if k == KD - 1:
                    pst[("fin", i + 3)] = (a, b)
            if ("fin", i) in pst:
                a, b = pst.pop(("fin", i))
                stat_finish(a)
                pst[("row", i + 3)] = (a, b)
            if ("row", i) in pst:
                a, b = pst.pop(("row", i))
                S.op("sp", lambda e, a=a, b=b: e.dma_start(out=rrow[32 * b:32 * b + 1, :], in_=rt[a][0:1, :]),
                     reads=[R_rt[a]], writes=[R_rrow[b]], dma="rrow%d" % b)

        def pre_apply(b):
            bank = 6 + ctr["sb"] % 2
            ctr["sb"] += 1
            p0 = 32 * b
            S.op("pe", lambda e, bank=bank, p0=p0: e.matmul(banks[bank][:, 0:BS], ones[p0:p0 + 1, :], rrow[p0:p0 + 1, :],
                                                           start=True, stop=True),
                 reads=[R_rrow[b], R_const], writes=[R_bank[bank]])
            for k in range(KD):
                S.op("dve", lambda e, k=k, b=b, bank=bank: e.scalar_tensor_tensor(
                    out=xn[:, k, blk(b)], in0=x2s[b][:, k, :], scalar=v1(V_G0 + k), in1=banks[bank][:, 0:BS],
                    op0=ALU.mult, op1=ALU.mult),
                    reads=[R_x2[b], R_bank[bank], R_const], writes=[R_xn[k][b]])

        def prologue_stage(h, b):
            S.op("sp", lambda e, h=h, b=b: e.dma_start(out=x2s[b], in_=xT[h, b]),
                 writes=[R_x2[b]], dma="x2s%d" % b)

        def prologue_unstage(b):
            S.op("sp", lambda e, b=b: e.dma_start(out=xs[:, :, blk(b)], in_=x2s[b]),
                 reads=[R_x2[b]], writes=[R_x[k][b] for k in range(KD)], dma="x%d" % b)

        def prologue_tiles(b, staged=False, add_eng="dve"):
            pro[b] = stat_begin()
            for k in range(KD):
                stat_tile(pro[b], k, b, staged, add_eng)

        def prologue_finish(b, staged=False):
            stat_finish(pro[b])
            apply_norm(pro[b], b, V_G0, staged)

        def preload_conv(after=()):
            return {0: load_win(w0in[0], after), 1: load_win(w0in[1])}

        def preload_pool():
            return {(0, 0): load_winh(w1in[0]), (0, 1): load_winh(w1in[1]), "wg": load_wg(w1g[0])}

        def wout_phase(h, layer):
            nxt = (layer == 1 and h + 1 < NH)
            pre = None
            if layer == 0:
                pre = preload_pool()
            elif nxt:
                pre = preload_conv()
                prologue_state(h + 1)
                for b in range(NB):
                    prologue_stage(h + 1, b)
            pend = []
            for b in range(NB):
                a = stat_begin()
                for k in range(KD):
                    if k == 2:
                        for f in pend:
                            f()
                        pend = []
                    if k == 5 and nxt:
                        pre_apply(b)
                    bank = ctr["wo"] % 6
                    ctr["wo"] += 1
                    mm_group(bank, [lambda e, bank=bank, ec=ec, k=k, b=b: e.matmul(
                        banks[bank][:, 0:BS], wout[:, ec, k * 128:(k + 1) * 128], yb[:, ec, blk(b)],
                        start=(ec == 0), stop=(ec == KE - 1)) for ec in range(KE)],
                        [[R_wout[ec // 4], R_y[ec][b]] for ec in range(KE)])
                    S.op("dve", lambda e, bank=bank, k=k, b=b: e.tensor_tensor(
                        out=xs[:, k, blk(b)], in0=xs[:, k, blk(b)], in1=banks[bank][:, 0:BS], op=ALU.add),
                        reads=[R_x[k][b], R_bank[bank]], writes=[R_x[k][b]])
                    stat_tile(a, k, b)

                def fin(a=a, b=b):
                    stat_finish(a)
                    if layer == 0:
                        apply_norm(a, b, V_G1)
                    else:
                        apply_final(a, h, b)
                    if nxt:
                        prologue_unstage(b)

                if b < NB - 1:
                    pend.append(fin)
                else:
                    fin()
            return pre

        def conv_begin(h, c):
                cf = c % 2
                CF = cvf[cf]
                src = zeros[:, 0:CONV_H] if h == 0 else hconv[:, c, :]
                S.op("act", lambda e, CF=CF, src=src: e.activation(out=CF[:, 0:CONV_H], in_=src, func=AF.Copy),
                     reads=[R_hconv[c], R_const], writes=[R_cvf[cf]])
                S.op("act", lambda e, CF=CF, c=c: e.activation(
                    out=sv(CF[:, CV_S0:CVW], CONV_H + DEC_T)[:, :, 0:CONV_H], in_=stc[:, c, :, :], func=AF.Copy),
                    reads=[R_stc], writes=[R_cvf[cf]])
        cstate = {}

        def conv_block(h, c, b, slot, part="both"):
                    cf = c % 2
                    CF = cvf[cf]
                    if part in ("both", "mm"):
                        st = ctr["cb"] % 2
                        ctr["cb"] += 1
                        bk = [4 * st + q for q in range(4)]
                        if slot == "wq0":
                            wt, rw = wq0, R_wout[0]
                        else:
                            wt, rw = win[slot], R_win[slot]
                        for q in range(4):
                            mm_group(bk[q], [lambda e, q=q, k=k, b=b, wt=wt, bank=bk[q]: e.matmul(
                                banks[bank][:, 0:BS], wt[:, k, q * 128:(q + 1) * 128], xn[:, k, blk(b)],
                                start=(k == 0), stop=(k == KD - 1)) for k in range(KD)],
                                [[rw, R_xn[k][b]] for k in range(KD)])
                        cstate[(c, b)] = (st, bk)
                        if part == "mm":
                            return
                    st, bk = cstate.pop((c, b))
                    Pgb, Pgc, Pv, Pz = (banks[x] for x in bk)
                    S.op("act", lambda e, st=st, Pv=Pv: e.activation(out=vsb[st][:], in_=Pv[:, 0:BS], func=AF.Copy),
                         reads=[R_bank[bk[2]]], writes=[R_vsb[st]])
                    S.op("act", lambda e, st=st, Pz=Pz: e.activation(out=szc[st][:], in_=Pz[:, 0:BS], func=AF.Silu),
                         reads=[R_bank[bk[3]]], writes=[R_szc[st]])
                    lo = b * BS
                    if b < 2:
                        n = BS
                        S.op("dve", lambda e, CF=CF, Pgc=Pgc, st=st, lo=lo: e.tensor_tensor(
                            out=CF[:, CONV_H + lo:CONV_H + lo + BS], in0=Pgc[:, 0:BS], in1=vsb[st][:], op=ALU.mult),
                            reads=[R_bank[bk[1]], R_vsb[st]], writes=[R_cvf[cf]])
                    else:
                        n = TW
                        S.op("dve", lambda e, CF=CF, Pgc=Pgc, st=st, lo=lo: e.tensor_tensor(
                            out=CF[:, CONV_H + lo:CONV_H + lo + PB2], in0=Pgc[:, 0:PB2], in1=vsb[st][:, 0:PB2],
                            op=ALU.mult),
                            reads=[R_bank[bk[1]], R_vsb[st]], writes=[R_cvf[cf]])
                        S.op("dve", lambda e, CF=CF, Pgc=Pgc, st=st: e.tensor_tensor(
                            out=sv(CF[:, CV_S0:CVW], CONV_H + DEC_T)[:, :, CONV_H:],
                            in0=sv(Pgc[:, PB2:BS], DEC_T), in1=sv(vsb[st][:, PB2:BS], DEC_T), op=ALU.mult),
                            reads=[R_bank[bk[1]], R_vsb[st]], writes=[R_cvf[cf]])
                    S.op("dve", lambda e, Pgb=Pgb, st=st: e.tensor_tensor(
                        out=gsb[st][:], in0=Pgb[:, 0:BS], in1=szc[st][:], op=ALU.mult),
                        reads=[R_bank[bk[0]], R_szc[st]], writes=[R_gsb[st]])
                    S.op("act", lambda e, CF=CF, st=st, lo=lo, n=n, c=c: e.activation(
                        out=t1[st][:, 0:n], in_=CF[:, lo + 2:lo + 2 + n], func=AF.Identity,
                        bias=v1(V_CB + c), scale=v1(V_CW + 3 * c + 2)),
                        reads=[R_cvf[cf], R_const], writes=[R_t1[st]])
                    S.op("dve", lambda e, CF=CF, st=st, lo=lo, n=n, c=c: e.scalar_tensor_tensor(
                        out=t2[st][:, 0:n], in0=CF[:, lo + 1:lo + 1 + n], scalar=v1(V_CW + 3 * c + 1),
                        in1=t1[st][:, 0:n], op0=ALU.mult, op1=ALU.add),
                        reads=[R_cvf[cf], R_t1[st], R_const], writes=[R_t2[st]])
                    S.op("dve", lambda e, CF=CF, st=st, lo=lo, n=n, c=c: e.scalar_tensor_tensor(
                        out=t1[st][:, 0:n], in0=CF[:, lo:lo + n], scalar=v1(V_CW + 3 * c + 0),
                        in1=t2[st][:, 0:n], op0=ALU.mult, op1=ALU.add),
                        reads=[R_cvf[cf], R_t2[st], R_const], writes=[R_t1[st]])
                    if b < 2:
                        S.op("dve", lambda e, st=st, c=c, b=b: e.tensor_tensor(
                            out=yb[:, c, blk(b)], in0=gsb[st][:], in1=t1[st][:, 0:BS], op=ALU.mult),
                            reads=[R_gsb[st], R_t1[st]], writes=[R_y[c][b]])
                    else:
                        S.op("dve", lambda e, st=st, c=c: e.tensor_tensor(
                            out=yb[:, c, 2 * BS:2 * BS + PB2], in0=gsb[st][:, 0:PB2], in1=t1[st][:, 0:PB2],
                            op=ALU.mult),
                            reads=[R_gsb[st], R_t1[st]], writes=[R_y[c][b]])
                        S.op("dve", lambda e, st=st, c=c: e.tensor_tensor(
                            out=sv(yb[:, c, PH:TH], DEC_T), in0=sv(gsb[st][:, PB2:BS], DEC_T),
                            in1=sv(t1[st][:, PB2:TW], CONV_H + DEC_T)[:, :, CONV_H:], op=ALU.mult),
                            reads=[R_gsb[st], R_t1[st]], writes=[R_y[c][b]])
        def conv_end(h, c):
                cf = c % 2
                CF = cvf[cf]
                if h == 0:
                    S.op("act", lambda e, CF=CF, c=c: e.activation(out=hconv[:, c, :], in_=CF[:, PH:PH + CONV_H], func=AF.Copy),
                         reads=[R_cvf[cf]], writes=[R_hconv[c]])
                else:
                    S.op("act", lambda e, CF=CF, c=c: e.activation(out=ocp[:, c, :], in_=CF[:, PH:PH + CONV_H], func=AF.Copy),
                         reads=[R_cvf[cf]], writes=[R_ocp])
                S.op("act", lambda e, CF=CF, c=c: e.activation(
                    out=ocs[:, c, :, :], in_=sv(CF[:, CV_S0:CVW], CONV_H + DEC_T)[:, :, DEC_T:], func=AF.Copy),
                    reads=[R_cvf[cf]], writes=[R_ocs])
        def conv_phase(h, pre, hooks=None):
            slots = dict(pre)
            if hooks is not None:
                for c in (0, 1):
                    conv_begin(h, c)
                for b in range(NB):
                    conv_block(h, 0, b, slots[0], part="mm")
                    if b in hooks:
                        hooks[b]()
                    conv_block(h, 0, b, slots[0], part="ew")
                    conv_block(h, 1, b, slots[1])
                for c in (0, 1):
                    conv_end(h, c)
            else:
                for c in (0, 1):
                    conv_begin(h, c)
                    for b in range(NB):
                        conv_block(h, c, b, slots[c])
                    conv_end(h, c)
            for c in range(2, KE):
                if h == 0 and c == 2:
                    slot = "wq0"
                else:
                    slot = load_win(w0in[c])
                if h == 0:
                    if c in (3, 4, 6, 8):
                        load_wout_q(w0out, {3: 0, 4: 1, 6: 2, 8: 3}[c])
                elif c in (2, 4, 6, 8):
                    load_wout_q(w0out, (c - 2) // 2)
                conv_begin(h, c)
                for b in range(NB):
                    if h == 0:
                        pre_stream_step((c - 2) * NB + b)
                    conv_block(h, c, b, slot)
                conv_end(h, c)
            tok = S.op("sp", lambda e, h=h: e.dma_start(out=ncs_d[h], in_=ocs[:]), reads=[R_ocs], dma="ocs")
            out_toks.append(tok)
            if h == 1:
                tok = S.op("sp", lambda e: e.dma_start(out=ncp_d, in_=ocp[:]), reads=[R_ocp], dma="ocp")
                out_toks.append(tok)

        def load_winh(src_ap):
            i = ctr["wh"] % 4
            ctr["wh"] += 1
            dst = win[i // 2][:, :, (i % 2) * 256:(i % 2) * 256 + 256]
            S.op("pool", lambda e, dst=dst, src_ap=src_ap: e.dma_start(out=dst, in_=src_ap),
                 writes=[R_winh[i]], dma="winh%d" % i)
            return i

        def whs(i, j):
            base = (i % 2) * 256 + j * 128
            return win[i // 2], base

        def pool_A_begin(h, c):
            ui = c % NU
            U = uf[ui]
            src = zeros[:, 0:POOL_H] if h == 0 else hpool[:, c, :]
            S.op("act", lambda e, U=U, src=src: e.activation(out=U[:, 0:POOL_H], in_=src, func=AF.Copy),
                 reads=[R_hpool[c], R_const], writes=[R_ufh[ui]])
            S.op("act", lambda e, U=U, c=c: e.activation(
                out=sv(U[:, UF_S0:UFW], POOL_H + DEC_T)[:, :, 0:POOL_H],
                in_=stp_sb[:, c, :].rearrange("p (r s) -> p s r", s=SH), func=AF.Copy),
                reads=[R_stp], writes=[R_ufs[ui]])
        def pool_A_block(h, c, b, hslot):
                ui = c % NU
                U = uf[ui]
                wt, wbase = whs(hslot, c % 2)
                bank = ctr["ub"] % 3
                ctr["ub"] += 1
                mm_group(bank, [lambda e, bank=bank, k=k, b=b, wt=wt, wbase=wbase: e.matmul(
                    banks[bank][:, 0:BS], wt[:, k, wbase:wbase + 128], xn[:, k, blk(b)],
                    start=(k == 0), stop=(k == KD - 1)) for k in range(KD)],
                    [[R_winh[hslot], R_xn[k][b]] for k in range(KD)])
                P = banks[bank]
                lo = POOL_H + b * BS
                if b < 2:
                    S.op("act", lambda e, U=U, P=P, lo=lo: e.activation(out=U[:, lo:lo + BS], in_=P[:, 0:BS], func=AF.Copy),
                         reads=[R_bank[bank]], writes=[R_ufb[ui]])
                else:
                    S.op("act", lambda e, U=U, P=P, lo=lo: e.activation(out=U[:, lo:lo + PB2], in_=P[:, 0:PB2], func=AF.Copy),
                         reads=[R_bank[bank]], writes=[R_ufb[ui]])
                    S.op("act", lambda e, U=U, P=P: e.activation(
                        out=sv(U[:, UF_S0:UFW], POOL_H + DEC_T)[:, :, POOL_H:], in_=sv(P[:, PB2:BS], DEC_T),
                        func=AF.Copy),
                        reads=[R_bank[bank]], writes=[R_ufb[ui]])
        def pool_A_end(h, c):
            ui = c % NU
            U = uf[ui]
            if h == 0:
                S.op("act", lambda e, U=U, c=c: e.activation(out=hpool[:, c, :], in_=U[:, PH:PH + POOL_H], func=AF.Copy),
                     reads=[R_ufb[ui]], writes=[R_hpool[c]])
            else:
                S.op("act", lambda e, U=U, c=c: e.activation(out=opp[:, c, :], in_=U[:, PH:PH + POOL_H], func=AF.Copy),
                     reads=[R_ufb[ui]], writes=[R_opp])
            S.op("act", lambda e, U=U, c=c: e.activation(
                out=onew[:, c, :].rearrange("p (r s) -> p s r", s=SH),
                in_=sv(U[:, UF_S0:UFW], POOL_H + DEC_T)[:, :, POOL_H:], func=AF.Copy),
                reads=[R_ufb[ui]], writes=[R_onew])

        def pool_A_chunk(h, c, hslot):
            pool_A_begin(h, c)
            for b in range(NB):
                pool_A_block(h, c, b, hslot)
            pool_A_end(h, c)

        def pool_chunk(h, c):
            g = c // 4
            ui = c % NU
            U = uf[ui]
            RU = [R_ufh[ui], R_ufs[ui], R_ufb[ui]]
            w = WINDOWS[g]
            cur, Rcur = U, RU
            tmps = [(tmpA, [R_tmpA]), (tmpB, [R_tmpB])]
            if w == 2:
                pi = c % 8
                P = pb[pi]
                Rp = R_pb[pi]
                S.op("dve", lambda e, P=P, U=U: e.tensor_tensor(
                    out=P[:, 0:PH], in0=U[:, POOL_H - 1:POOL_H - 1 + PH], in1=U[:, POOL_H:POOL_H + PH],
                    op=ALU.subtract),
                    reads=RU, writes=[Rp[0], Rp[1], Rp[2]])
                S.op("dve", lambda e, P=P, U=U: e.tensor_tensor(
                    out=sv(P[:, PH:TH], DEC_T),
                    in0=sv(U[:, UF_S0:UFW], POOL_H + DEC_T)[:, :, POOL_H - 1:POOL_H - 1 + DEC_T],
                    in1=sv(U[:, UF_S0:UFW], POOL_H + DEC_T)[:, :, POOL_H:], op=ALU.subtract),
                    reads=RU, writes=[Rp[2]])
                if h == 0:
                    S.op("dve", lambda e, P=P: e.memset(P[:, 0:1], 0.0), writes=[Rp[0]])
                return
            sh = 1
            lvl = 0
            if w == 16:
                S.op("dve", lambda e, U=U: e.tensor_tensor_scan(
                    out=tmpA[:, 0:UFW], data0=U[:, 0:UFW], data1=U[:, 0:UFW], initial=0.0,
                    op0=ALU.add, op1=ALU.bypass),
                    reads=RU, writes=[R_tmpA])
                S.op("dve", lambda e: e.tensor_tensor(
                    out=tmpB[:, 16:UFW], in0=tmpA[:, 16:UFW], in1=tmpA[:, 0:UFW - 16], op=ALU.subtract),
                    reads=[R_tmpA], writes=[R_tmpB])
                S.op("dve", lambda e: e.tensor_copy(out=tmpB[:, 15:16], in_=tmpA[:, 15:16]),
                     reads=[R_tmpA], writes=[R_tmpB])
                cur, Rcur = tmpB, [R_tmpB]
                sh = w
            while sh < w:
                dst, Rdst = tmps[lvl % 2]
                lo = 2 * sh - 1
                S.op("dve", lambda e, dst=dst, cur=cur, lo=lo, sh=sh: e.tensor_tensor(
                    out=dst[:, lo:UFW], in0=cur[:, lo:UFW], in1=cur[:, lo - sh:UFW - sh], op=ALU.add),
                    reads=Rcur, writes=Rdst)
                cur, Rcur = dst, Rdst
                sh *= 2
                lvl += 1
            pi = c % 8
            P = pb[pi]
            Rp = R_pb[pi]
            if h == 0:
                S.op("dve", lambda e, cur=cur, g=g: e.tensor_tensor(
                    out=cur[:, POOL_H:POOL_H + 16], in0=cur[:, POOL_H:POOL_H + 16], in1=invc[:, g, :], op=ALU.mult),
                    reads=Rcur + [R_const], writes=Rcur)
            S.op("dve", lambda e, P=P, cur=cur, U=U, w=w: e.scalar_tensor_tensor(
                out=P[:, 0:PH], in0=cur[:, POOL_H:POOL_H + PH], scalar=1.0 / w, in1=U[:, POOL_H:POOL_H + PH],
                op0=ALU.mult, op1=ALU.subtract),
                reads=Rcur + RU, writes=[Rp[0], Rp[1], Rp[2]])
            S.op("dve", lambda e, P=P, cur=cur, U=U, w=w: e.scalar_tensor_tensor(
                out=sv(P[:, PH:TH], DEC_T), in0=sv(cur[:, UF_S0:UFW], POOL_H + DEC_T)[:, :, POOL_H:],
                scalar=1.0 / w, in1=sv(U[:, UF_S0:UFW], POOL_H + DEC_T)[:, :, POOL_H:],
                op0=ALU.mult, op1=ALU.subtract),
                reads=Rcur + RU, writes=[Rp[2]])

        def pool_B_chunk(h, c, hslot, gslot, blocks=range(NB)):
            g = c // 4
            ci = c % 4
            wt, wbase = whs(hslot, c % 2)
            for b in blocks:
                zb = 3 + ctr["zb"] % 2
                ctr["zb"] += 1
                qb = 5 + ctr["qb"] % 3
                ctr["qb"] += 1
                mm_group(zb, [lambda e, zb=zb, k=k, b=b, wt=wt, wbase=wbase: e.matmul(
                    banks[zb][:, 0:BS], wt[:, k, wbase:wbase + 128], xn[:, k, blk(b)],
                    start=(k == 0), stop=(k == KD - 1)) for k in range(KD)],
                    [[R_winh[hslot], R_xn[k][b]] for k in range(KD)])
                mm_group(qb, [lambda e, qb=qb, kc=kc, b=b, gslot=gslot, ci=ci, pi=(4 * g + kc) % 8: e.matmul(
                    banks[qb][:, 0:BS], wg[gslot][:, kc, ci * 128:(ci + 1) * 128], pb[pi][:, blk(b)],
                    start=(kc == 0), stop=(kc == 3)) for kc in range(4)],
                    [[R_wg[gslot], R_pb[(4 * g + kc) % 8][b]] for kc in range(4)])
                s = ctr["zb"] % 2
                S.op("act", lambda e, s=s, zb=zb: e.activation(out=szp[s][:], in_=banks[zb][:, 0:BS], func=AF.Silu),
                     reads=[R_bank[zb]], writes=[R_szp[s]])
                sc = ps0[:, c:c + 1] if c < 4 else v1(V_PS + c)
                S.op("dve", lambda e, s=s, qb=qb, c=c, b=b, sc=sc: e.scalar_tensor_tensor(
                    out=yb[:, c, blk(b)], in0=banks[qb][:, 0:BS], scalar=sc, in1=szp[s][:],
                    op0=ALU.mult, op1=ALU.mult),
                    reads=[R_bank[qb], R_szp[s], R_const, R_ps0], writes=[R_y[c][b]])

        def pool_phase(h, pre):
            us = dict(pre)
            for c in range(3):
                pool_A_begin(h, c)
                for b in (0, 1):
                    pool_A_block(h, c, b, us[(0, c // 2)])
            for c in range(3):
                pool_A_block(h, c, 2, us[(0, c // 2)])
                pool_A_end(h, c)
                pool_chunk(h, c)
            pool_A_chunk(h, 3, us[(0, 1)])
            pool_chunk(h, 3)
            gnext = us.pop("wg")
            for g in range(4):
                gslot = gnext
                load_wout_q(w1out, g)
                for hg in range(2):
                    if g + 1 < 4:
                        us[(g + 1, hg)] = load_winh(w1in[((g + 1) * 2 + 0) * 2 + hg])
                    zs = load_winh(w1in[(g * 2 + 1) * 2 + hg])
                    if hg == 1 and g + 1 < 4:
                        gnext = load_wg(w1g[g + 1])
                    for cj in range(2):
                        c = 4 * g + 2 * hg + cj
                        if g + 1 < 4:
                            pool_A_chunk(h, c + 4, us[(g + 1, hg)])
                            if c % 4 == 3:
                                pool_chunk(h, c + 4)
                                pool_B_chunk(h, c, zs, gslot)
                            else:
                                pool_B_chunk(h, c, zs, gslot, blocks=(0, 1))
                                pool_chunk(h, c + 4)
                                pool_B_chunk(h, c, zs, gslot, blocks=(2,))
                        else:
                            pool_B_chunk(h, c, zs, gslot)
            tok = S.op("sp", lambda e, h=h: e.dma_start(
                out=nps_d[h].rearrange("c p f -> p c f")[:, :, HK:HK + HN], in_=onew[:]),
                reads=[R_onew], dma="onew")
            out_toks.append(tok)
            if h == 1:
                tok = S.op("sp", lambda e: e.dma_start(out=npp_d, in_=opp[:]), reads=[R_opp], dma="opp")
                out_toks.append(tok)

        for b in range(NB):
            prologue_load(0, b)
        prologue_state(0)
        pre = preload_conv(after=[R_x[k][0] for k in range(KD)])
        wq0 = wout[:, 0:4, :].rearrange("p a (b f) -> p (a b) f", f=512)
        S.op("pool", lambda e: e.dma_start(out=wq0, in_=w0in[2]), writes=[R_wout[0]], dma="wout0")
        prologue_tiles(0)
        prologue_finish(0)
        prologue_tiles(1)
        for h in range(NH):
            conv_phase(h, pre, hooks=({0: lambda: (prologue_finish(1), prologue_tiles(2)), 1: lambda: prologue_finish(2)} if h == 0 else None))
            pre = wout_phase(h, 0)
            pool_phase(h, pre)
            pre = wout_phase(h, 1)

        S.final_wait("sp", out_toks)

        with nc.Block() as block:
            S.emit(block)
    return nc


_NC_CACHE = {}


def _get_program():
    if "nc" not in _NC_CACHE:
        _NC_CACHE["nc"] = build_program()
    return _NC_CACHE["nc"]


def _chunk_vec(v):
    return np.ascontiguousarray(v.reshape(-1, 128).T)


def kernel(x_prompt, x_sample, state_conv, state_pool, norm_g, final_norm_g,
           conv_w_in, conv_w, conv_b, conv_w_out,
           pool_w_in, pool_w_grp, pool_scale, pool_w_out):
    f = np.float32
    x_prompt = np.asarray(x_prompt, f)
    x_sample = np.asarray(x_sample, f)
    state_conv = np.asarray(state_conv, f)
    state_pool = np.asarray(state_pool, f)
    SPC = DEC_B // NCORE

    w0 = np.asarray(conv_w_in, f)[0]
    w0in = np.ascontiguousarray(
        w0.reshape(KD, 128, 4, KE, 128).transpose(3, 1, 0, 2, 4).reshape(KE, 128, KD, 512))
    wo0 = np.asarray(conv_w_out, f)[0]
    w0out = np.ascontiguousarray(wo0.reshape(4, 4, 128, D).transpose(0, 2, 1, 3))
    w1 = np.asarray(pool_w_in, f)[0]
    w1in = np.ascontiguousarray(
        w1.reshape(KD, 128, 2, 4, 2, 256).transpose(3, 2, 4, 1, 0, 5).reshape(16, 128, KD, 256))
    wgm = np.asarray(pool_w_grp, f)[0]
    w1g = np.ascontiguousarray(wgm.reshape(4, 4, 128, 512).transpose(0, 2, 1, 3))
    wo1 = np.asarray(pool_w_out, f)[0]
    w1out = np.ascontiguousarray(wo1.reshape(4, 4, 128, D).transpose(0, 2, 1, 3))

    vecs = np.zeros((128, NV), f)
    cw = np.asarray(conv_w, f)[0]
    vecs[:, V_CW:V_CW + 48] = cw.reshape(3, KE, 128).transpose(2, 1, 0).reshape(128, 48)
    vecs[:, V_CB:V_CB + KE] = _chunk_vec(np.asarray(conv_b, f)[0])
    vecs[:, V_PS:V_PS + KE] = _chunk_vec(np.asarray(pool_scale, f)[0])
    ng = np.asarray(norm_g, f)
    vecs[:, V_G0:V_G0 + KD] = _chunk_vec(ng[0])
    vecs[:, V_G1:V_G1 + KD] = _chunk_vec(ng[1])
    vecs[:, V_GF:V_GF + KD] = _chunk_vec(np.asarray(final_norm_g, f))

    invc = np.zeros((128, 4, 16), f)
    for g, w in enumerate(WINDOWS):
        invc[:, g, :] = (np.float32(w) / np.minimum(np.float32(w), np.arange(16, dtype=f) + 1.0)).astype(f)[None, :]

    in_maps = []
    for core in range(NCORE):
        xp = x_prompt[core]
        xsm = x_sample[core * SPC:(core + 1) * SPC]
        halves = []
        for h in range(NH):
            halves.append(np.concatenate(
                [xp[h * PH:(h + 1) * PH], xsm[h * SH:(h + 1) * SH].reshape(SH * DEC_T, D)], axis=0))
        X = np.stack(halves)
        xT = np.ascontiguousarray(X.reshape(NH, NB, BS, KD, 128).transpose(0, 1, 4, 3, 2))
        sc = state_conv[0, core * SPC:(core + 1) * SPC]
        stc = np.ascontiguousarray(sc.reshape(NH, SH, CONV_H, KE, 128).transpose(0, 4, 3, 1, 2))
        sp_ = state_pool[0, core * SPC:(core + 1) * SPC]
        stp = np.ascontiguousarray(
            sp_.reshape(NH, SH, POOL_H, KE, 128).transpose(0, 3, 4, 2, 1).reshape(NH, KE, 128, POOL_H * SH))
        in_maps.append({"xT": xT, "w0in": w0in, "w0out": w0out, "w1in": w1in, "w1g": w1g,
                        "w1out": w1out, "vecs": vecs, "stc": stc, "stp": stp, "invc": invc})

    nc = _get_program()
    res = run_bass_kernel_spmd(nc, in_maps, core_ids=list(range(NCORE)))

    y_prompt = np.empty((NCORE, SEQ, D), f)
    y_sample = np.empty((DEC_B, DEC_T, D), f)
    ncp = np.empty((1, NCORE, CONV_H, E), f)
    ncs = np.empty((1, DEC_B, CONV_H, E), f)
    npp = np.empty((1, NCORE, POOL_H, E), f)
    nps = np.empty((1, DEC_B, POOL_H, E), f)
    for core in range(NCORE):
        r = res.results[core]
        Y = np.asarray(r["yT"]).transpose(0, 1, 4, 3, 2).reshape(NH, TH, D)
        for h in range(NH):
            y_prompt[core, h * PH:(h + 1) * PH] = Y[h, :PH]
            y_sample[core * SPC + h * SH:core * SPC + (h + 1) * SH] = Y[h, PH:].reshape(SH, DEC_T, D)
        ncp[0, core] = np.asarray(r["ncp"]).transpose(2, 1, 0).reshape(CONV_H, E)
        a = np.asarray(r["ncs"])
        ncs[0, core * SPC:(core + 1) * SPC] = a.transpose(0, 3, 4, 2, 1).reshape(SPC, CONV_H, E)
        npp[0, core] = np.asarray(r["npp"]).transpose(2, 1, 0).reshape(POOL_H, E)
        a = np.asarray(r["nps"]).reshape(NH, KE, 128, POOL_H, SH)
        nps[0, core * SPC:(core + 1) * SPC] = a.transpose(0, 4, 3, 1, 2).reshape(SPC, POOL_H, E)
    return (y_prompt, y_sample, ncp, ncs, npp, nps)
```

```python
from contextlib import ExitStack

import numpy as np
import concourse.bass as bass
import concourse.mybir as mybir
from concourse.bass_utils import run_bass_kernel_spmd

F32 = mybir.dt.float32
BF16 = mybir.dt.bfloat16
AF = mybir.ActivationFunctionType
ALU = mybir.AluOpType

NCORE = 8
D = 1024
E = 2048
SEQ = 2048
DEC_B = 128
DEC_T = 4
CONV_H = 2
POOL_H = 15
WINDOWS = (2, 4, 8, 16)
EPS = 1e-6

KD = D // 128
KE = E // 128
NH = 2
PH = SEQ // NH
SH = (DEC_B // NCORE) // NH
TH = PH + SH * DEC_T
NB = 3
BS = TH // NB
PB2 = PH - 2 * BS
CVW = CONV_H + PH + SH * (CONV_H + DEC_T)
UFW = POOL_H + PH + SH * (POOL_H + DEC_T)
CV_S0 = CONV_H + PH
UF_S0 = POOL_H + PH
TW = PB2 + SH * (CONV_H + DEC_T)

V_CW = 0
V_CB = 48
V_PS = 64
V_G0 = 80
V_G1 = 88
V_GF = 96
NV = 104

HN = DEC_T * SH
HK = (POOL_H - DEC_T) * SH
SEM_ROLL = 3000


class Res:
    __slots__ = ("name", "w", "r", "al")

    def __init__(self, name):
        self.name = name
        self.w = None
        self.r = {}
        self.al = []


def alias(a, bs):
    for b in bs:
        a.al.append(b)
        b.al.append(a)


class Sched:
    ENGS = ("pe", "act", "dve", "pool", "sp")

    def __init__(self, nc, stack):
        self.nc = nc
        self.stack = stack
        self.streams = {e: [] for e in self.ENGS}
        self.sems = {}
        self.cnt = {}
        self.cur = {e: (e, 0) for e in ("pe", "act", "dve", "pool")}
        self.seen = {e: {} for e in self.ENGS}
        self.nwaits = 0

    def _sem(self, key):
        if key not in self.sems:
            nm = "s_" + "_".join(str(k) for k in key)
            self.sems[key] = self.stack.enter_context(self.nc.semaphore(nm))
            self.cnt[key] = 0
        return self.sems[key]

    def _deps(self, engine, reads, writes):
        toks = []
        for r0 in reads:
            for r in [r0] + r0.al:
                if r.w is not None:
                    toks.append((r.w, False))
        for r0 in writes:
            for r in [r0] + r0.al:
                if r.w is not None:
                    toks.append((r.w, False))
                for k, v in r.r.items():
                    toks.append(((k, v), True))
        need = {}
        for (k, v), is_war in toks:
            if k[0] == engine and engine == "pe":
                continue
            if self.seen[engine].get(k, 0) >= v:
                continue
            if need.get(k, 0) < v:
                need[k] = v
        waits = []
        for k, v in need.items():
            self.seen[engine][k] = v
            waits.append((k, v))
        return waits

    def op(self, engine, fn, reads=(), writes=(), inc=True, dma=None):
        waits = self._deps(engine, reads, writes)
        self.nwaits += len(waits)
        tok = None
        incspec = None
        if dma is not None:
            key = ("dma", dma)
            self._sem(key)
            self.cnt[key] += 16
            tok = (key, self.cnt[key])
            incspec = (key, 16)
        elif inc:
            key = self.cur[engine]
            self._sem(key)
            if self.cnt[key] >= SEM_ROLL:
                key = (engine, key[1] + 1)
                self.cur[engine] = key
                self._sem(key)
            self.cnt[key] += 1
            tok = (key, self.cnt[key])
            incspec = (key, 1)
        self.streams[engine].append((waits, fn, incspec))
        if tok is not None:
            for r in writes:
                r.w = tok
                r.r = {}
            for r in reads:
                if r.r.get(tok[0], 0) < tok[1]:
                    r.r[tok[0]] = tok[1]
        return tok

    def final_wait(self, engine, toks):
        need = {}
        for (k, v) in toks:
            if self.seen[engine].get(k, 0) >= v:
                continue
            if need.get(k, 0) < v:
                need[k] = v
        waits = []
        for k, v in need.items():
            self.seen[engine][k] = v
            waits.append((k, v))
        self.streams[engine].append((waits, None, None))

    def emit(self, block):
        for e, attr in (("pe", "tensor"), ("act", "scalar"), ("dve", "vector"),
                        ("pool", "gpsimd"), ("sp", "sync")):
            stream = self.streams[e]

            def body(eng, stream=stream):
                for waits, fn, incspec in stream:
                    for (k, v) in waits:
                        eng.wait_ge(self.sems[k], v)
                    if fn is None:
                        continue
                    ins = fn(eng)
                    if incspec is not None:
                        ins.then_inc(self.sems[incspec[0]], incspec[1])

            getattr(block, attr)(body)


def build_program():
    nc = bass.Bass("TRN2", target_bir_lowering=False)

    def din(name, shape):
        return nc.dram_tensor(name, list(shape), F32, kind="ExternalInput").ap()

    def dout(name, shape):
        return nc.dram_tensor(name, list(shape), F32, kind="ExternalOutput").ap()

    xT = din("xT", (NH, NB, 128, KD, BS))
    w0in = din("w0in", (KE, 128, KD, 512))
    w0out = din("w0out", (4, 128, 4, D))
    w1in = din("w1in", (16, 128, KD, 256))
    w1g = din("w1g", (4, 128, 4, 512))
    w1out = din("w1out", (4, 128, 4, D))
    vecs_d = din("vecs", (128, NV))
    stc_d = din("stc", (NH, 128, KE, SH, CONV_H))
    stp_d = din("stp", (NH, KE, 128, POOL_H * SH))
    invc_d = din("invc", (128, 4, 16))

    yT = dout("yT", (NH, NB, 128, KD, BS))
    ncp_d = dout("ncp", (128, KE, CONV_H))
    ncs_d = dout("ncs", (NH, 128, KE, SH, CONV_H))
    npp_d = dout("npp", (128, KE, POOL_H))
    nps_d = dout("nps", (NH, KE, 128, POOL_H * SH))

    with ExitStack() as stack:
        def sb(name, shape, dt=F32):
            return stack.enter_context(nc.sbuf_tensor(name, list(shape), dt))

        def ps(name):
            return stack.enter_context(nc.psum_tensor(name, [128, 512], F32))

        S = Sched(nc, stack)

        xs = sb("xs", (128, KD, TH))
        xn = sb("xn", (128, KD, TH), BF16)
        yb = sb("yb", (128, KE, TH), BF16)
        win = [sb("win%d" % i, (128, KD, 512), BF16) for i in range(2)]
        wout = sb("wout", (128, KE, D), BF16)
        wg = [sb("wg%d" % i, (128, 4, 512), BF16) for i in range(2)]
        vecs = sb("vecs_sb", (128, NV))
        ones = sb("ones", (128, 128))
        zeros = sb("zeros", (128, 16))
        invc = sb("invc_sb", (128, 4, 16))
        stc = sb("stc_sb", (128, KE, SH, CONV_H))
        ocs = sb("ocs", (128, KE, SH, CONV_H))
        ocp = sb("ocp", (128, KE, CONV_H))
        opp = sb("opp", (128, KE, POOL_H))
        hconv = sb("hconv", (128, KE, CONV_H))
        hpool = sb("hpool", (128, KE, POOL_H))
        NU = 3
        scr = sb("scr", (128, 5 * UFW + 16 + 2 * BS))
        uf = [scr[:, i * UFW:(i + 1) * UFW] for i in range(NU)]
        tmpA = scr[:, 3 * UFW:4 * UFW]
        tmpB = scr[:, 4 * UFW:5 * UFW]
        o5 = 5 * UFW
        pfx = scr[:, o5:o5 + 16]
        szp = [scr[:, o5 + 16 + i * BS:o5 + 16 + (i + 1) * BS] for i in range(2)]
        cvf = [uf[0][:, 0:CVW], uf[1][:, 0:CVW]]
        vsb = [uf[2][:, 0:BS], uf[2][:, BS:2 * BS]]
        szc = [uf[2][:, 2 * BS:3 * BS], tmpB[:, TW:TW + BS]]
        gsb = [tmpB[:, TW + BS:TW + 2 * BS], szp[0]]
        t1 = [tmpA[:, 0:TW], tmpA[:, TW:2 * TW]]
        t2 = [tmpA[:, 2 * TW:3 * TW], tmpB[:, 0:TW]]
        pbraw = sb("pbraw", (128, 8 * (TH // 2)))
        pb = [pbraw[:, i * (TH // 2):(i + 1) * (TH // 2)].bitcast(BF16) for i in range(8)]
        XB = KD * BS
        x2s = [scr[:, 0:XB].rearrange("p (k t) -> p k t", k=KD),
               scr[:, XB:2 * XB].rearrange("p (k t) -> p k t", k=KD),
               pbraw[:, 0:XB].rearrange("p (k t) -> p k t", k=KD)]
        NST = 2
        acc = [sb("acc%d" % i, (128, BS)) for i in range(NST)]
        sq = [sb("sq%d" % i, (128, BS)) for i in range(2)]
        rt = [sb("rt%d" % i, (128, BS)) for i in range(NST)]
        rstd = rt
        rrow = sb("rrow", (128, BS))
        ps0 = sb("ps0", (128, 4))
        sq3 = sb("sq3", (128, BS))
        stp_sb = sb("stp_sb", (128, KE, POOL_H * SH))
        onew = sb("onew", (128, KE, DEC_T * SH))

        banks = [ps("bank%d" % i) for i in range(8)]

        R_x = [[Res("x%d_%d" % (k, b)) for b in range(NB)] for k in range(KD)]
        R_xn = [[Res("xn%d_%d" % (k, b)) for b in range(NB)] for k in range(KD)]
        R_y = [[Res("y%d_%d" % (c, b)) for b in range(NB)] for c in range(KE)]
        R_win = [Res("win%d" % i) for i in range(2)]
        R_wout = [Res("wout%d" % i) for i in range(4)]
        R_wg = [Res("wg%d" % i) for i in range(2)]
        R_const = Res("const")
        R_stc = Res("stc")
        R_ocs = Res("ocs")
        R_ocp = Res("ocp")
        R_opp = Res("opp")
        R_hconv = [Res("hconv%d" % c) for c in range(KE)]
        R_hpool = [Res("hpool%d" % c) for c in range(KE)]
        R_cvf = [Res("cvf%d" % i) for i in range(2)]
        R_vsb = [Res("vsb%d" % i) for i in range(2)]
        R_szc = [Res("szc%d" % i) for i in range(2)]
        R_gsb = [Res("gsb%d" % i) for i in range(2)]
        R_t1 = [Res("t1_%d" % i) for i in range(2)]
        R_t2 = [Res("t2_%d" % i) for i in range(2)]
        R_ufh = [Res("ufh%d" % i) for i in range(NU)]
        R_ufs = [Res("ufs%d" % i) for i in range(NU)]
        R_ufb = [Res("ufb%d" % i) for i in range(NU)]
        R_winh = [Res("winh%d" % i) for i in range(4)]
        R_tmpA = Res("tmpA")
        R_tmpB = Res("tmpB")
        R_pfx = Res("pfx")
        R_pb = [[Res("pb%d_%d" % (i, b)) for b in range(NB)] for i in range(8)]
        R_szp = [Res("szp%d" % i) for i in range(2)]
        R_acc = [Res("acc%d" % i) for i in range(3)]
        R_sq = [Res("sq%d" % i) for i in range(2)]
        R_rt = [Res("rt%d" % i) for i in range(3)]
        R_x2 = [Res("x2s%d" % i) for i in range(NB)]
        R_stp = Res("stp")
        R_onew = Res("onew")
        R_rrow = [Res("rrow%d" % i) for i in range(NB)]
        R_sq3 = Res("sq3")
        R_bank = [Res("bank%d" % i) for i in range(8)]
        for rr in (R_ufh[0], R_ufs[0], R_ufb[0]):
            alias(rr, [R_cvf[0]])
        for rr in (R_ufh[1], R_ufs[1], R_ufb[1]):
            alias(rr, [R_cvf[1]])
        for rr in (R_ufh[2], R_ufs[2], R_ufb[2]):
            alias(rr, [R_vsb[0], R_vsb[1], R_szc[0]])
        alias(R_tmpA, [R_t1[0], R_t1[1], R_t2[0]])
        alias(R_tmpB, [R_t2[1], R_szc[1], R_gsb[0]])
        alias(R_szp[0], [R_gsb[1]])
        scr_all = (R_ufh + R_ufs + R_ufb + [R_tmpA, R_tmpB, R_pfx] + R_szp + R_cvf + R_vsb + R_szc
                   + R_gsb + R_t1 + R_t2)
        alias(R_x2[0], scr_all)
        alias(R_x2[1], scr_all)
        alias(R_x2[2], [r for rs in R_pb for r in rs])
        alias(R_win[0], [R_winh[0], R_winh[1]])
        alias(R_win[1], [R_winh[2], R_winh[3]])

        out_toks = []
        ctr = {"sb": 0, "w": 0, "wh": 0, "wg": 0, "cb": 0, "wo": 0, "ss": 0, "sq": 0, "ot": 0,
               "ub": 0, "zb": 0, "qb": 0}

        def blk(b):
            return slice(b * BS, (b + 1) * BS)

        def v1(col):
            return vecs[:, col:col + 1]

        def sv(ap2d, r):
            return ap2d.rearrange("p (s r) -> p s r", r=r)

        def mm_group(bank, fns, reads_list):
            allr = []
            for rl in reads_list:
                for r in rl:
                    if r not in allr:
                        allr.append(r)
            n = len(fns)
            for i in range(n):
                last = i == n - 1
                S.op("pe", fns[i], reads=(allr if last else reads_list[i]), writes=[R_bank[bank]], inc=last)

        S.op("sp", lambda e: e.dma_start(out=vecs[:], in_=vecs_d), writes=[R_const], dma="const")
        S.op("sp", lambda e: e.dma_start(out=invc[:], in_=invc_d), writes=[R_const], dma="const")
        S.op("pool", lambda e: e.memset(ones[:], 1.0), writes=[R_const])
        S.op("pool", lambda e: e.memset(zeros[:], 0.0), writes=[R_const])
        R_ps0 = Res("ps0")
        S.op("dve", lambda e: e.tensor_scalar_mul(out=ps0[:], in0=vecs[:, V_PS:V_PS + 4], scalar1=0.5),
             reads=[R_const], writes=[R_ps0])

        def load_win(src_ap, after=()):
            slot = ctr["w"] % 2
            ctr["w"] += 1
            S.op("pool", lambda e, slot=slot, src_ap=src_ap: e.dma_start(out=win[slot][:], in_=src_ap),
                 reads=list(after), writes=[R_win[slot]], dma="win%d" % slot)
            return slot

        def load_wg(src_ap):
            slot = ctr["wg"] % 2
            ctr["wg"] += 1
            S.op("pool", lambda e, slot=slot, src_ap=src_ap: e.dma_start(out=wg[slot][:], in_=src_ap),
                 writes=[R_wg[slot]], dma="wg%d" % slot)
            return slot

        def load_wout_q(wsrc, q):
            S.op("pool", lambda e, q=q, wsrc=wsrc: e.dma_start(out=wout[:, 4 * q:4 * q + 4, :], in_=wsrc[q]),
                 writes=[R_wout[q]], dma="wout%d" % q)

        def stat_begin():
            for k_ in list(padd.keys()):
                flush_add(k_)
            a = ctr["ss"] % NST
            ctr["ss"] += 1
            return a

        padd = {}

        def flush_add(a):
            f = padd.pop(a, None)
            if f is not None:
                f()

        def stat_tile(a, k, b, staged=False, add_eng="dve"):
            for k_ in list(padd.keys()):
                flush_add(k_)
            xin = x2s[b][:, k, :] if staged else xs[:, k, blk(b)]
            rin = R_x2[b] if staged else R_x[k][b]
            if k == 0:
                S.op("act", lambda e, a=a, xin=xin: e.activation(out=acc[a][:], in_=xin, func=AF.Square),
                     reads=[rin], writes=[R_acc[a]])
            else:
                s = ctr["sq"] % 2
                ctr["sq"] += 1
                S.op("act", lambda e, s=s, xin=xin: e.activation(out=sq[s][:], in_=xin, func=AF.Square),
                     reads=[rin], writes=[R_sq[s]])
                padd[a] = lambda s=s, a=a: S.op(
                    add_eng, lambda e, s=s, a=a: e.tensor_tensor(out=acc[a][:], in0=acc[a][:], in1=sq[s][:], op=ALU.add),
                    reads=[R_acc[a], R_sq[s]], writes=[R_acc[a]])

        def stat_finish(a):
            flush_add(a)
            bank = 6 + ctr["sb"] % 2
            ctr["sb"] += 1
            S.op("pe", lambda e, a=a, bank=bank: e.matmul(banks[bank][:, 0:BS], ones[:], acc[a][:], start=True, stop=True),
                 reads=[R_acc[a], R_const], writes=[R_bank[bank]])
            S.op("act", lambda e, a=a, bank=bank: e.activation(out=rt[a][:], in_=banks[bank][:, 0:BS], func=AF.Sqrt,
                                                                bias=EPS, scale=1.0 / D),
                 reads=[R_bank[bank]], writes=[R_rt[a]])
            S.op("dve", lambda e, a=a: e.reciprocal(out=rt[a][:], in_=rt[a][:]),
                 reads=[R_rt[a]], writes=[R_rt[a]])

        def apply_norm(a, b, gcol, staged=False):
            for k in range(KD):
                xin = x2s[b][:, k, :] if staged else xs[:, k, blk(b)]
                rin = R_x2[b] if staged else R_x[k][b]
                S.op("dve", lambda e, a=a, k=k, b=b, xin=xin: e.scalar_tensor_tensor(
                    out=xn[:, k, blk(b)], in0=xin, scalar=v1(gcol + k), in1=rt[a][:],
                    op0=ALU.mult, op1=ALU.mult),
                    reads=[rin, R_rt[a], R_const], writes=[R_xn[k][b]])

        def apply_final(a, h, b):
            for k in range(KD):
                S.op("dve", lambda e, a=a, k=k, b=b: e.scalar_tensor_tensor(
                    out=xs[:, k, blk(b)], in0=xs[:, k, blk(b)], scalar=v1(V_GF + k), in1=rt[a][:],
                    op0=ALU.mult, op1=ALU.mult),
                    reads=[R_x[k][b], R_rt[a], R_const], writes=[R_x[k][b]])
                tok = S.op("sp", lambda e, h=h, b=b, k=k: e.dma_start(out=yT[h, b, :, k, :], in_=xs[:, k, blk(b)]),
                           reads=[R_x[k][b]], dma="yo%d" % (k % 4))
                out_toks.append(tok)

        pro = {}

        def prologue_load(h, b):
            S.op("sp", lambda e, h=h, b=b: e.dma_start(out=xs[:, :, blk(b)], in_=xT[h, b]),
                 writes=[R_x[k][b] for k in range(KD)], dma="x%d" % b)

        def prologue_state(h):
            S.op("sp", lambda e, h=h: e.dma_start(out=stc[:], in_=stc_d[h]), writes=[R_stc], dma="stc")
            S.op("sp", lambda e, h=h: e.dma_start(out=stp_sb[:], in_=stp_d[h].rearrange("c p f -> p c f")),
                 writes=[R_stp], dma="stp")
            tok = S.op("sp", lambda e, h=h: e.dma_start(out=nps_d[h][:, :, 0:HK], in_=stp_d[h][:, :, HN:HN + HK]),
                       dma="npsh")
            out_toks.append(tok)

        pst = {}

        def pre_stream_step(i):
            lbuf = [sq[0], sq[1], sq3]
            lres = [R_sq[0], R_sq[1], R_sq3]
            if i < NB * KD:
                b, k = divmod(i, KD)
                s = i % 3
                S.op("sp", lambda e, s=s, b=b, k=k: e.dma_start(out=lbuf[s][:], in_=xT[1, b][:, k, :]),
                     writes=[lres[s]], dma="xs%d" % s)
            j = i - 2
            if 0 <= j < NB * KD:
                b, k = divmod(j, KD)
                s = j % 3
                if k == 0:
                    pst["a"] = stat_begin()
                a = pst["a"]
                if k == 0:
                    S.op("act", lambda e, a=a, s=s: e.activation(out=acc[a][:], in_=lbuf[s][:], func=AF.Square),
                         reads=[lres[s]], writes=[R_acc[a]])
                else:
                    S.op("act", lambda e, s=s: e.activation(out=lbuf[s][:], in_=lbuf[s][:], func=AF.Square),
                         reads=[lres[s]], writes=[lres[s]])
                    S.op("dve", lambda e, s=s, a=a: e.tensor_tensor(out=acc[a][:], in0=acc[a][:], in1=lbuf[s][:], op=ALU.add),
                         reads=[R_acc[a], lres[s]], writes=[R_acc[a]])
                if k == KD - 1:
                    pst[("fin", i + 3)] = (a, b)
            if ("fin", i) in pst:
                a, b = pst.pop(("fin", i))
                stat_finish(a)
                pst[("row", i + 3)] = (a, b)
            if ("row", i) in pst:
                a, b = pst.pop(("row", i))
                S.op("sp", lambda e, a=a, b=b: e.dma_start(out=rrow[32 * b:32 * b + 1, :], in_=rt[a][0:1, :]),
                     reads=[R_rt[a]], writes=[R_rrow[b]], dma="rrow%d" % b)

        def pre_apply(b):
            bank = 6 + ctr["sb"] % 2
            ctr["sb"] += 1
            p0 = 32 * b
            S.op("pe", lambda e, bank=bank, p0=p0: e.matmul(banks[bank][:, 0:BS], ones[p0:p0 + 1, :], rrow[p0:p0 + 1, :],
                                                           start=True, stop=True),
                 reads=[R_rrow[b], R_const], writes=[R_bank[bank]])
            for k in range(KD):
                S.op("dve", lambda e, k=k, b=b, bank=bank: e.scalar_tensor_tensor(
                    out=xn[:, k, blk(b)], in0=x2s[b][:, k, :], scalar=v1(V_G0 + k), in1=banks[bank][:, 0:BS],
                    op0=ALU.mult, op1=ALU.mult),
                    reads=[R_x2[b], R_bank[bank], R_const], writes=[R_xn[k][b]])

        def prologue_stage(h, b):
            S.op("sp", lambda e, h=h, b=b: e.dma_start(out=x2s[b], in_=xT[h, b]),
                 writes=[R_x2[b]], dma="x2s%d" % b)

        def prologue_unstage(b):
            S.op("sp", lambda e, b=b: e.dma_start(out=xs[:, :, blk(b)], in_=x2s[b]),
                 reads=[R_x2[b]], writes=[R_x[k][b] for k in range(KD)], dma="x%d" % b)

        def prologue_tiles(b, staged=False, add_eng="dve"):
            pro[b] = stat_begin()
            for k in range(KD):
                stat_tile(pro[b], k, b, staged, add_eng)

        def prologue_finish(b, staged=False):
            stat_finish(pro[b])
            apply_norm(pro[b], b, V_G0, staged)

        def preload_conv(after=(), after1=()):
            return {0: load_win(w0in[0], after), 1: load_win(w0in[1], after1)}

        def preload_pool():
            return {(0, 0): load_winh(w1in[0]), (0, 1): load_winh(w1in[1]), "wg": load_wg(w1g[0])}

        def wout_phase(h, layer):
            nxt = (layer == 1 and h + 1 < NH)
            pre = None
            if layer == 0:
                pre = preload_pool()
            elif nxt:
                pre = preload_conv()
                prologue_state(h + 1)
                for b in range(NB):
                    prologue_stage(h + 1, b)
            pend = []
            for b in range(NB):
                a = stat_begin()
                for k in range(KD):
                    if k == 2:
                        for f in pend:
                            f()
                        pend = []
                    if k == 5 and nxt:
                        pre_apply(b)
                    bank = ctr["wo"] % 6
                    ctr["wo"] += 1
                    mm_group(bank, [lambda e, bank=bank, ec=ec, k=k, b=b: e.matmul(
                        banks[bank][:, 0:BS], wout[:, ec, k * 128:(k + 1) * 128], yb[:, ec, blk(b)],
                        start=(ec == 0), stop=(ec == KE - 1)) for ec in range(KE)],
                        [[R_wout[ec // 4], R_y[ec][b]] for ec in range(KE)])
                    S.op("dve", lambda e, bank=bank, k=k, b=b: e.tensor_tensor(
                        out=xs[:, k, blk(b)], in0=xs[:, k, blk(b)], in1=banks[bank][:, 0:BS], op=ALU.add),
                        reads=[R_x[k][b], R_bank[bank]], writes=[R_x[k][b]])
                    stat_tile(a, k, b)

                def fin(a=a, b=b):
                    stat_finish(a)
                    if layer == 0:
                        apply_norm(a, b, V_G1)
                    else:
                        apply_final(a, h, b)
                    if nxt:
                        prologue_unstage(b)

                if b < NB - 1:
                    pend.append(fin)
                else:
                    fin()
            return pre

        def conv_begin(h, c):
                cf = c % 2
                CF = cvf[cf]
                src = zeros[:, 0:CONV_H] if h == 0 else hconv[:, c, :]
                S.op("act", lambda e, CF=CF, src=src: e.activation(out=CF[:, 0:CONV_H], in_=src, func=AF.Copy),
                     reads=[R_hconv[c], R_const], writes=[R_cvf[cf]])
                S.op("act", lambda e, CF=CF, c=c: e.activation(
                    out=sv(CF[:, CV_S0:CVW], CONV_H + DEC_T)[:, :, 0:CONV_H], in_=stc[:, c, :, :], func=AF.Copy),
                    reads=[R_stc], writes=[R_cvf[cf]])
        cstate = {}

        def conv_block(h, c, b, slot, part="both"):
                    cf = c % 2
                    CF = cvf[cf]
                    if part in ("both", "mm"):
                        st = ctr["cb"] % 2
                        ctr["cb"] += 1
                        bk = [4 * st + q for q in range(4)]
                        if slot == "wq0":
                            wt, rw = wq0, R_wout[0]
                        else:
                            wt, rw = win[slot], R_win[slot]
                        for q in range(4):
                            mm_group(bk[q], [lambda e, q=q, k=k, b=b, wt=wt, bank=bk[q]: e.matmul(
                                banks[bank][:, 0:BS], wt[:, k, q * 128:(q + 1) * 128], xn[:, k, blk(b)],
                                start=(k == 0), stop=(k == KD - 1)) for k in range(KD)],
                                [[rw, R_xn[k][b]] for k in range(KD)])
                        cstate[(c, b)] = (st, bk)
                        if part == "mm":
                            return
                    st, bk = cstate.pop((c, b))
                    Pgb, Pgc, Pv, Pz = (banks[x] for x in bk)
                    S.op("act", lambda e, st=st, Pv=Pv: e.activation(out=vsb[st][:], in_=Pv[:, 0:BS], func=AF.Copy),
                         reads=[R_bank[bk[2]]], writes=[R_vsb[st]])
                    S.op("act", lambda e, st=st, Pz=Pz: e.activation(out=szc[st][:], in_=Pz[:, 0:BS], func=AF.Silu),
                         reads=[R_bank[bk[3]]], writes=[R_szc[st]])
                    lo = b * BS
                    if b < 2:
                        n = BS
                        S.op("dve", lambda e, CF=CF, Pgc=Pgc, st=st, lo=lo: e.tensor_tensor(
                            out=CF[:, CONV_H + lo:CONV_H + lo + BS], in0=Pgc[:, 0:BS], in1=vsb[st][:], op=ALU.mult),
                            reads=[R_bank[bk[1]], R_vsb[st]], writes=[R_cvf[cf]])
                    else:
                        n = TW
                        S.op("dve", lambda e, CF=CF, Pgc=Pgc, st=st, lo=lo: e.tensor_tensor(
                            out=CF[:, CONV_H + lo:CONV_H + lo + PB2], in0=Pgc[:, 0:PB2], in1=vsb[st][:, 0:PB2],
                            op=ALU.mult),
                            reads=[R_bank[bk[1]], R_vsb[st]], writes=[R_cvf[cf]])
                        S.op("dve", lambda e, CF=CF, Pgc=Pgc, st=st: e.tensor_tensor(
                            out=sv(CF[:, CV_S0:CVW], CONV_H + DEC_T)[:, :, CONV_H:],
                            in0=sv(Pgc[:, PB2:BS], DEC_T), in1=sv(vsb[st][:, PB2:BS], DEC_T), op=ALU.mult),
                            reads=[R_bank[bk[1]], R_vsb[st]], writes=[R_cvf[cf]])
                    S.op("dve", lambda e, Pgb=Pgb, st=st: e.tensor_tensor(
                        out=gsb[st][:], in0=Pgb[:, 0:BS], in1=szc[st][:], op=ALU.mult),
                        reads=[R_bank[bk[0]], R_szc[st]], writes=[R_gsb[st]])
                    S.op("act", lambda e, CF=CF, st=st, lo=lo, n=n, c=c: e.activation(
                        out=t1[st][:, 0:n], in_=CF[:, lo + 2:lo + 2 + n], func=AF.Identity,
                        bias=v1(V_CB + c), scale=v1(V_CW + 3 * c + 2)),
                        reads=[R_cvf[cf], R_const], writes=[R_t1[st]])
                    S.op("dve", lambda e, CF=CF, st=st, lo=lo, n=n, c=c: e.scalar_tensor_tensor(
                        out=t2[st][:, 0:n], in0=CF[:, lo + 1:lo + 1 + n], scalar=v1(V_CW + 3 * c + 1),
                        in1=t1[st][:, 0:n], op0=ALU.mult, op1=ALU.add),
                        reads=[R_cvf[cf], R_t1[st], R_const], writes=[R_t2[st]])
                    S.op("dve", lambda e, CF=CF, st=st, lo=lo, n=n, c=c: e.scalar_tensor_tensor(
                        out=t1[st][:, 0:n], in0=CF[:, lo:lo + n], scalar=v1(V_CW + 3 * c + 0),
                        in1=t2[st][:, 0:n], op0=ALU.mult, op1=ALU.add),
                        reads=[R_cvf[cf], R_t2[st], R_const], writes=[R_t1[st]])
                    if b < 2:
                        S.op("dve", lambda e, st=st, c=c, b=b: e.tensor_tensor(
                            out=yb[:, c, blk(b)], in0=gsb[st][:], in1=t1[st][:, 0:BS], op=ALU.mult),
                            reads=[R_gsb[st], R_t1[st]], writes=[R_y[c][b]])
                    else:
                        S.op("dve", lambda e, st=st, c=c: e.tensor_tensor(
                            out=yb[:, c, 2 * BS:2 * BS + PB2], in0=gsb[st][:, 0:PB2], in1=t1[st][:, 0:PB2],
                            op=ALU.mult),
                            reads=[R_gsb[st], R_t1[st]], writes=[R_y[c][b]])
                        S.op("dve", lambda e, st=st, c=c: e.tensor_tensor(
                            out=sv(yb[:, c, PH:TH], DEC_T), in0=sv(gsb[st][:, PB2:BS], DEC_T),
                            in1=sv(t1[st][:, PB2:TW], CONV_H + DEC_T)[:, :, CONV_H:], op=ALU.mult),
                            reads=[R_gsb[st], R_t1[st]], writes=[R_y[c][b]])
        def conv_end(h, c):
                cf = c % 2
                CF = cvf[cf]
                if h == 0:
                    S.op("act", lambda e, CF=CF, c=c: e.activation(out=hconv[:, c, :], in_=CF[:, PH:PH + CONV_H], func=AF.Copy),
                         reads=[R_cvf[cf]], writes=[R_hconv[c]])
                else:
                    S.op("act", lambda e, CF=CF, c=c: e.activation(out=ocp[:, c, :], in_=CF[:, PH:PH + CONV_H], func=AF.Copy),
                         reads=[R_cvf[cf]], writes=[R_ocp])
                S.op("act", lambda e, CF=CF, c=c: e.activation(
                    out=ocs[:, c, :, :], in_=sv(CF[:, CV_S0:CVW], CONV_H + DEC_T)[:, :, DEC_T:], func=AF.Copy),
                    reads=[R_cvf[cf]], writes=[R_ocs])
        def conv_phase(h, pre, hooks=None):
            slots = dict(pre)
            if hooks is not None:
                for c in (0, 1):
                    conv_begin(h, c)
                for b in range(NB):
                    conv_block(h, 0, b, slots[0], part="mm")
                    if b in hooks:
                        hooks[b]()
                    conv_block(h, 0, b, slots[0], part="ew")
                    conv_block(h, 1, b, slots[1])
                for c in (0, 1):
                    conv_end(h, c)
            else:
                for c in (0, 1):
                    conv_begin(h, c)
                    for b in range(NB):
                        conv_block(h, c, b, slots[c])
                    conv_end(h, c)
            for c in range(2, KE):
                if h == 0 and c == 2:
                    slot = "wq0"
                else:
                    slot = load_win(w0in[c])
                if h == 0:
                    if c in (3, 4, 6, 8):
                        load_wout_q(w0out, {3: 0, 4: 1, 6: 2, 8: 3}[c])
                elif c in (2, 4, 6, 8):
                    load_wout_q(w0out, (c - 2) // 2)
                conv_begin(h, c)
                for b in range(NB):
                    if h == 0:
                        pre_stream_step((c - 2) * NB + b)
                    conv_block(h, c, b, slot)
                conv_end(h, c)
            tok = S.op("sp", lambda e, h=h: e.dma_start(out=ncs_d[h], in_=ocs[:]), reads=[R_ocs], dma="ocs")
            out_toks.append(tok)
            if h == 1:
                tok = S.op("sp", lambda e: e.dma_start(out=ncp_d, in_=ocp[:]), reads=[R_ocp], dma="ocp")
                out_toks.append(tok)

        def load_winh(src_ap):
            i = ctr["wh"] % 4
            ctr["wh"] += 1
            dst = win[i // 2][:, :, (i % 2) * 256:(i % 2) * 256 + 256]
            S.op("pool", lambda e, dst=dst, src_ap=src_ap: e.dma_start(out=dst, in_=src_ap),
                 writes=[R_winh[i]], dma="winh%d" % i)
            return i

        def whs(i, j):
            base = (i % 2) * 256 + j * 128
            return win[i // 2], base

        def pool_A_begin(h, c):
            ui = c % NU
            U = uf[ui]
            src = zeros[:, 0:POOL_H] if h == 0 else hpool[:, c, :]
            S.op("act", lambda e, U=U, src=src: e.activation(out=U[:, 0:POOL_H], in_=src, func=AF.Copy),
                 reads=[R_hpool[c], R_const], writes=[R_ufh[ui]])
            S.op("act", lambda e, U=U, c=c: e.activation(
                out=sv(U[:, UF_S0:UFW], POOL_H + DEC_T)[:, :, 0:POOL_H],
                in_=stp_sb[:, c, :].rearrange("p (r s) -> p s r", s=SH), func=AF.Copy),
                reads=[R_stp], writes=[R_ufs[ui]])
        def pool_A_block(h, c, b, hslot):
                ui = c % NU
                U = uf[ui]
                wt, wbase = whs(hslot, c % 2)
                bank = ctr["ub"] % 3
                ctr["ub"] += 1
                mm_group(bank, [lambda e, bank=bank, k=k, b=b, wt=wt, wbase=wbase: e.matmul(
                    banks[bank][:, 0:BS], wt[:, k, wbase:wbase + 128], xn[:, k, blk(b)],
                    start=(k == 0), stop=(k == KD - 1)) for k in range(KD)],
                    [[R_winh[hslot], R_xn[k][b]] for k in range(KD)])
                P = banks[bank]
                lo = POOL_H + b * BS
                if b < 2:
                    S.op("act", lambda e, U=U, P=P, lo=lo: e.activation(out=U[:, lo:lo + BS], in_=P[:, 0:BS], func=AF.Copy),
                         reads=[R_bank[bank]], writes=[R_ufb[ui]])
                else:
                    S.op("act", lambda e, U=U, P=P, lo=lo: e.activation(out=U[:, lo:lo + PB2], in_=P[:, 0:PB2], func=AF.Copy),
                         reads=[R_bank[bank]], writes=[R_ufb[ui]])
                    S.op("act", lambda e, U=U, P=P: e.activation(
                        out=sv(U[:, UF_S0:UFW], POOL_H + DEC_T)[:, :, POOL_H:], in_=sv(P[:, PB2:BS], DEC_T),
                        func=AF.Copy),
                        reads=[R_bank[bank]], writes=[R_ufb[ui]])
        def pool_A_end(h, c):
            ui = c % NU
            U = uf[ui]
            if h == 0:
                S.op("act", lambda e, U=U, c=c: e.activation(out=hpool[:, c, :], in_=U[:, PH:PH + POOL_H], func=AF.Copy),
                     reads=[R_ufb[ui]], writes=[R_hpool[c]])
            else:
                S.op("act", lambda e, U=U, c=c: e.activation(out=opp[:, c, :], in_=U[:, PH:PH + POOL_H], func=AF.Copy),
                     reads=[R_ufb[ui]], writes=[R_opp])
            S.op("act", lambda e, U=U, c=c: e.activation(
                out=onew[:, c, :].rearrange("p (r s) -> p s r", s=SH),
                in_=sv(U[:, UF_S0:UFW], POOL_H + DEC_T)[:, :, POOL_H:], func=AF.Copy),
                reads=[R_ufb[ui]], writes=[R_onew])

        def pool_A_chunk(h, c, hslot):
            pool_A_begin(h, c)
            for b in range(NB):
                pool_A_block(h, c, b, hslot)
            pool_A_end(h, c)

        def pool_chunk(h, c):
            g = c // 4
            ui = c % NU
            U = uf[ui]
            RU = [R_ufh[ui], R_ufs[ui], R_ufb[ui]]
            w = WINDOWS[g]
            cur, Rcur = U, RU
            tmps = [(tmpA, [R_tmpA]), (tmpB, [R_tmpB])]
            if w == 2:
                pi = c % 8
                P = pb[pi]
                Rp = R_pb[pi]
                S.op("dve", lambda e, P=P, U=U: e.tensor_tensor(
                    out=P[:, 0:PH], in0=U[:, POOL_H - 1:POOL_H - 1 + PH], in1=U[:, POOL_H:POOL_H + PH],
                    op=ALU.subtract),
                    reads=RU, writes=[Rp[0], Rp[1], Rp[2]])
                S.op("dve", lambda e, P=P, U=U: e.tensor_tensor(
                    out=sv(P[:, PH:TH], DEC_T),
                    in0=sv(U[:, UF_S0:UFW], POOL_H + DEC_T)[:, :, POOL_H - 1:POOL_H - 1 + DEC_T],
                    in1=sv(U[:, UF_S0:UFW], POOL_H + DEC_T)[:, :, POOL_H:], op=ALU.subtract),
                    reads=RU, writes=[Rp[2]])
                if h == 0:
                    S.op("dve", lambda e, P=P: e.memset(P[:, 0:1], 0.0), writes=[Rp[0]])
                return
            sh = 1
            lvl = 0
            if w == 16:
                S.op("dve", lambda e, U=U: e.tensor_tensor_scan(
                    out=tmpA[:, 0:UFW], data0=U[:, 0:UFW], data1=U[:, 0:UFW], initial=0.0,
                    op0=ALU.add, op1=ALU.bypass),
                    reads=RU, writes=[R_tmpA])
                S.op("dve", lambda e: e.tensor_tensor(
                    out=tmpB[:, 16:UFW], in0=tmpA[:, 16:UFW], in1=tmpA[:, 0:UFW - 16], op=ALU.subtract),
                    reads=[R_tmpA], writes=[R_tmpB])
                S.op("dve", lambda e: e.tensor_copy(out=tmpB[:, 15:16], in_=tmpA[:, 15:16]),
                     reads=[R_tmpA], writes=[R_tmpB])
                cur, Rcur = tmpB, [R_tmpB]
                sh = w
            while sh < w:
                dst, Rdst = tmps[lvl % 2]
                lo = 2 * sh - 1
                S.op("dve", lambda e, dst=dst, cur=cur, lo=lo, sh=sh: e.tensor_tensor(
                    out=dst[:, lo:UFW], in0=cur[:, lo:UFW], in1=cur[:, lo - sh:UFW - sh], op=ALU.add),
                    reads=Rcur, writes=Rdst)
                cur, Rcur = dst, Rdst
                sh *= 2
                lvl += 1
            pi = c % 8
            P = pb[pi]
            Rp = R_pb[pi]
            if h == 0:
                S.op("dve", lambda e, cur=cur, g=g: e.tensor_tensor(
                    out=cur[:, POOL_H:POOL_H + 16], in0=cur[:, POOL_H:POOL_H + 16], in1=invc[:, g, :], op=ALU.mult),
                    reads=Rcur + [R_const], writes=Rcur)
            S.op("dve", lambda e, P=P, cur=cur, U=U, w=w: e.scalar_tensor_tensor(
                out=P[:, 0:PH], in0=cur[:, POOL_H:POOL_H + PH], scalar=1.0 / w, in1=U[:, POOL_H:POOL_H + PH],
                op0=ALU.mult, op1=ALU.subtract),
                reads=Rcur + RU, writes=[Rp[0], Rp[1], Rp[2]])
            S.op("dve", lambda e, P=P, cur=cur, U=U, w=w: e.scalar_tensor_tensor(
                out=sv(P[:, PH:TH], DEC_T), in0=sv(cur[:, UF_S0:UFW], POOL_H + DEC_T)[:, :, POOL_H:],
                scalar=1.0 / w, in1=sv(U[:, UF_S0:UFW], POOL_H + DEC_T)[:, :, POOL_H:],
                op0=ALU.mult, op1=ALU.subtract),
                reads=Rcur + RU, writes=[Rp[2]])

        def pool_B_chunk(h, c, hslot, gslot, blocks=range(NB)):
            g = c // 4
            ci = c % 4
            wt, wbase = whs(hslot, c % 2)
            for b in blocks:
                zb = 3 + ctr["zb"] % 2
                ctr["zb"] += 1
                qb = 5 + ctr["qb"] % 3
                ctr["qb"] += 1
                mm_group(zb, [lambda e, zb=zb, k=k, b=b, wt=wt, wbase=wbase: e.matmul(
                    banks[zb][:, 0:BS], wt[:, k, wbase:wbase + 128], xn[:, k, blk(b)],
                    start=(k == 0), stop=(k == KD - 1)) for k in range(KD)],
                    [[R_winh[hslot], R_xn[k][b]] for k in range(KD)])
                mm_group(qb, [lambda e, qb=qb, kc=kc, b=b, gslot=gslot, ci=ci, pi=(4 * g + kc) % 8: e.matmul(
                    banks[qb][:, 0:BS], wg[gslot][:, kc, ci * 128:(ci + 1) * 128], pb[pi][:, blk(b)],
                    start=(kc == 0), stop=(kc == 3)) for kc in range(4)],
                    [[R_wg[gslot], R_pb[(4 * g + kc) % 8][b]] for kc in range(4)])
                s = ctr["zb"] % 2
                S.op("act", lambda e, s=s, zb=zb: e.activation(out=szp[s][:], in_=banks[zb][:, 0:BS], func=AF.Silu),
                     reads=[R_bank[zb]], writes=[R_szp[s]])
                sc = ps0[:, c:c + 1] if c < 4 else v1(V_PS + c)
                S.op("dve", lambda e, s=s, qb=qb, c=c, b=b, sc=sc: e.scalar_tensor_tensor(
                    out=yb[:, c, blk(b)], in0=banks[qb][:, 0:BS], scalar=sc, in1=szp[s][:],
                    op0=ALU.mult, op1=ALU.mult),
                    reads=[R_bank[qb], R_szp[s], R_const, R_ps0], writes=[R_y[c][b]])

        def pool_phase(h, pre):
            us = dict(pre)
            for c in range(3):
                pool_A_begin(h, c)
                for b in (0, 1):
                    pool_A_block(h, c, b, us[(0, c // 2)])
            for c in range(3):
                pool_A_block(h, c, 2, us[(0, c // 2)])
                pool_A_end(h, c)
                pool_chunk(h, c)
            pool_A_chunk(h, 3, us[(0, 1)])
            pool_chunk(h, 3)
            gnext = us.pop("wg")
            for g in range(4):
                gslot = gnext
                load_wout_q(w1out, g)
                for hg in range(2):
                    if g + 1 < 4:
                        us[(g + 1, hg)] = load_winh(w1in[((g + 1) * 2 + 0) * 2 + hg])
                    zs = load_winh(w1in[(g * 2 + 1) * 2 + hg])
                    if hg == 1 and g + 1 < 4:
                        gnext = load_wg(w1g[g + 1])
                    for cj in range(2):
                        c = 4 * g + 2 * hg + cj
                        if g + 1 < 4:
                            pool_A_chunk(h, c + 4, us[(g + 1, hg)])
                            if c % 4 == 3:
                                pool_chunk(h, c + 4)
                                pool_B_chunk(h, c, zs, gslot)
                            else:
                                pool_B_chunk(h, c, zs, gslot, blocks=(0, 1))
                                pool_chunk(h, c + 4)
                                pool_B_chunk(h, c, zs, gslot, blocks=(2,))
                        else:
                            pool_B_chunk(h, c, zs, gslot)
            tok = S.op("sp", lambda e, h=h: e.dma_start(
                out=nps_d[h].rearrange("c p f -> p c f")[:, :, HK:HK + HN], in_=onew[:]),
                reads=[R_onew], dma="onew")
            out_toks.append(tok)
            if h == 1:
                tok = S.op("sp", lambda e: e.dma_start(out=npp_d, in_=opp[:]), reads=[R_opp], dma="opp")
                out_toks.append(tok)

        for b in range(NB):
            prologue_load(0, b)
        prologue_state(0)
        pre = preload_conv(after=[R_x[k][0] for k in range(KD)], after1=[R_x[k][1] for k in range(KD)])
        wq0 = wout[:, 0:4, :].rearrange("p a (b f) -> p (a b) f", f=512)
        S.op("pool", lambda e: e.dma_start(out=wq0, in_=w0in[2]), reads=[R_x[k][2] for k in range(KD)],
             writes=[R_wout[0]], dma="wout0")
        prologue_tiles(0)
        prologue_finish(0)
        prologue_tiles(1)
        for h in range(NH):
            conv_phase(h, pre, hooks=({0: lambda: (prologue_finish(1), prologue_tiles(2)), 1: lambda: prologue_finish(2)} if h == 0 else None))
            pre = wout_phase(h, 0)
            pool_phase(h, pre)
            pre = wout_phase(h, 1)

        S.final_wait("sp", out_toks)

        with nc.Block() as block:
            S.emit(block)
    return nc


_NC_CACHE = {}


def _get_program():
    if "nc" not in _NC_CACHE:
        _NC_CACHE["nc"] = build_program()
    return _NC_CACHE["nc"]


def _chunk_vec(v):
    return np.ascontiguousarray(v.reshape(-1, 128).T)


def kernel(x_prompt, x_sample, state_conv, state_pool, norm_g, final_norm_g,
           conv_w_in, conv_w, conv_b, conv_w_out,
           pool_w_in, pool_w_grp, pool_scale, pool_w_out):
    f = np.float32
    x_prompt = np.asarray(x_prompt, f)
    x_sample = np.asarray(x_sample, f)
    state_conv = np.asarray(state_conv, f)
    state_pool = np.asarray(state_pool, f)
    SPC = DEC_B // NCORE

    w0 = np.asarray(conv_w_in, f)[0]
    w0in = np.ascontiguousarray(
        w0.reshape(KD, 128, 4, KE, 128).transpose(3, 1, 0, 2, 4).reshape(KE, 128, KD, 512))
    wo0 = np.asarray(conv_w_out, f)[0]
    w0out = np.ascontiguousarray(wo0.reshape(4, 4, 128, D).transpose(0, 2, 1, 3))
    w1 = np.asarray(pool_w_in, f)[0]
    w1in = np.ascontiguousarray(
        w1.reshape(KD, 128, 2, 4, 2, 256).transpose(3, 2, 4, 1, 0, 5).reshape(16, 128, KD, 256))
    wgm = np.asarray(pool_w_grp, f)[0]
    w1g = np.ascontiguousarray(wgm.reshape(4, 4, 128, 512).transpose(0, 2, 1, 3))
    wo1 = np.asarray(pool_w_out, f)[0]
    w1out = np.ascontiguousarray(wo1.reshape(4, 4, 128, D).transpose(0, 2, 1, 3))

    vecs = np.zeros((128, NV), f)
    cw = np.asarray(conv_w, f)[0]
    vecs[:, V_CW:V_CW + 48] = cw.reshape(3, KE, 128).transpose(2, 1, 0).reshape(128, 48)
    vecs[:, V_CB:V_CB + KE] = _chunk_vec(np.asarray(conv_b, f)[0])
    vecs[:, V_PS:V_PS + KE] = _chunk_vec(np.asarray(pool_scale, f)[0])
    ng = np.asarray(norm_g, f)
    vecs[:, V_G0:V_G0 + KD] = _chunk_vec(ng[0])
    vecs[:, V_G1:V_G1 + KD] = _chunk_vec(ng[1])
    vecs[:, V_GF:V_GF + KD] = _chunk_vec(np.asarray(final_norm_g, f))

    invc = np.zeros((128, 4, 16), f)
    for g, w in enumerate(WINDOWS):
        invc[:, g, :] = (np.float32(w) / np.minimum(np.float32(w), np.arange(16, dtype=f) + 1.0)).astype(f)[None, :]

    in_maps = []
    for core in range(NCORE):
        xp = x_prompt[core]
        xsm = x_sample[core * SPC:(core + 1) * SPC]
        halves = []
        for h in range(NH):
            halves.append(np.concatenate(
                [xp[h * PH:(h + 1) * PH], xsm[h * SH:(h + 1) * SH].reshape(SH * DEC_T, D)], axis=0))
        X = np.stack(halves)
        xT = np.ascontiguousarray(X.reshape(NH, NB, BS, KD, 128).transpose(0, 1, 4, 3, 2))
        sc = state_conv[0, core * SPC:(core + 1) * SPC]
        stc = np.ascontiguousarray(sc.reshape(NH, SH, CONV_H, KE, 128).transpose(0, 4, 3, 1, 2))
        sp_ = state_pool[0, core * SPC:(core + 1) * SPC]
        stp = np.ascontiguousarray(
            sp_.reshape(NH, SH, POOL_H, KE, 128).transpose(0, 3, 4, 2, 1).reshape(NH, KE, 128, POOL_H * SH))
        in_maps.append({"xT": xT, "w0in": w0in, "w0out": w0out, "w1in": w1in, "w1g": w1g,
                        "w1out": w1out, "vecs": vecs, "stc": stc, "stp": stp, "invc": invc})

    nc = _get_program()
    res = run_bass_kernel_spmd(nc, in_maps, core_ids=list(range(NCORE)))

    y_prompt = np.empty((NCORE, SEQ, D), f)
    y_sample = np.empty((DEC_B, DEC_T, D), f)
    ncp = np.empty((1, NCORE, CONV_H, E), f)
    ncs = np.empty((1, DEC_B, CONV_H, E), f)
    npp = np.empty((1, NCORE, POOL_H, E), f)
    nps = np.empty((1, DEC_B, POOL_H, E), f)
    for core in range(NCORE):
        r = res.results[core]
        Y = np.asarray(r["yT"]).transpose(0, 1, 4, 3, 2).reshape(NH, TH, D)
        for h in range(NH):
            y_prompt[core, h * PH:(h + 1) * PH] = Y[h, :PH]
            y_sample[core * SPC + h * SH:core * SPC + (h + 1) * SH] = Y[h, PH:].reshape(SH, DEC_T, D)
        ncp[0, core] = np.asarray(r["ncp"]).transpose(2, 1, 0).reshape(CONV_H, E)
        a = np.asarray(r["ncs"])
        ncs[0, core * SPC:(core + 1) * SPC] = a.transpose(0, 3, 4, 2, 1).reshape(SPC, CONV_H, E)
        npp[0, core] = np.asarray(r["npp"]).transpose(2, 1, 0).reshape(POOL_H, E)
        a = np.asarray(r["nps"]).reshape(NH, KE, 128, POOL_H, SH)
        nps[0, core * SPC:(core + 1) * SPC] = a.transpose(0, 4, 3, 1, 2).reshape(SPC, POOL_H, E)
    return (y_prompt, y_sample, ncp, ncs, npp, nps)
```

```python
from contextlib import ExitStack

import numpy as np
import concourse.bass as bass
import concourse.mybir as mybir
from concourse.bass_utils import run_bass_kernel_spmd

F32 = mybir.dt.float32
BF16 = mybir.dt.bfloat16
AF = mybir.ActivationFunctionType
ALU = mybir.AluOpType

NCORE = 8
D = 1024
E = 2048
SEQ = 2048
DEC_B = 128
DEC_T = 4
CONV_H = 2
POOL_H = 15
WINDOWS = (2, 4, 8, 16)
EPS = 1e-6

KD = D // 128
KE = E // 128
NH = 2
PH = SEQ // NH
SH = (DEC_B // NCORE) // NH
TH = PH + SH * DEC_T
NB = 3
BS = TH // NB
PB2 = PH - 2 * BS
CVW = CONV_H + PH + SH * (CONV_H + DEC_T)
UFW = POOL_H + PH + SH * (POOL_H + DEC_T)
CV_S0 = CONV_H + PH
UF_S0 = POOL_H + PH
TW = PB2 + SH * (CONV_H + DEC_T)

V_CW = 0
V_CB = 48
V_PS = 64
V_G0 = 80
V_G1 = 88
V_GF = 96
NV = 104

HN = DEC_T * SH
HK = (POOL_H - DEC_T) * SH
SEM_ROLL = 3000


class Res:
    __slots__ = ("name", "w", "r", "al")

    def __init__(self, name):
        self.name = name
        self.w = None
        self.r = {}
        self.al = []


def alias(a, bs):
    for b in bs:
        a.al.append(b)
        b.al.append(a)


class Sched:
    ENGS = ("pe", "act", "dve", "pool", "sp")

    def __init__(self, nc, stack):
        self.nc = nc
        self.stack = stack
        self.streams = {e: [] for e in self.ENGS}
        self.sems = {}
        self.cnt = {}
        self.cur = {e: (e, 0) for e in ("pe", "act", "dve", "pool")}
        self.seen = {e: {} for e in self.ENGS}
        self.nwaits = 0

    def _sem(self, key):
        if key not in self.sems:
            nm = "s_" + "_".join(str(k) for k in key)
            self.sems[key] = self.stack.enter_context(self.nc.semaphore(nm))
            self.cnt[key] = 0
        return self.sems[key]

    def _deps(self, engine, reads, writes):
        toks = []
        for r0 in reads:
            for r in [r0] + r0.al:
                if r.w is not None:
                    toks.append((r.w, False))
        for r0 in writes:
            for r in [r0] + r0.al:
                if r.w is not None:
                    toks.append((r.w, False))
                for k, v in r.r.items():
                    toks.append(((k, v), True))
        need = {}
        for (k, v), is_war in toks:
            if k[0] == engine and engine == "pe":
                continue
            if self.seen[engine].get(k, 0) >= v:
                continue
            if need.get(k, 0) < v:
                need[k] = v
        waits = []
        for k, v in need.items():
            self.seen[engine][k] = v
            waits.append((k, v))
        return waits

    def op(self, engine, fn, reads=(), writes=(), inc=True, dma=None):
        waits = self._deps(engine, reads, writes)
        self.nwaits += len(waits)
        tok = None
        incspec = None
        if dma is not None:
            key = ("dma", dma)
            self._sem(key)
            self.cnt[key] += 16
            tok = (key, self.cnt[key])
            incspec = (key, 16)
        elif inc:
            key = self.cur[engine]
            self._sem(key)
            if self.cnt[key] >= SEM_ROLL:
                key = (engine, key[1] + 1)
                self.cur[engine] = key
                self._sem(key)
            self.cnt[key] += 1
            tok = (key, self.cnt[key])
            incspec = (key, 1)
        self.streams[engine].append((waits, fn, incspec))
        if tok is not None:
            for r in writes:
                r.w = tok
                r.r = {}
            for r in reads:
                if r.r.get(tok[0], 0) < tok[1]:
                    r.r[tok[0]] = tok[1]
        return tok

    def final_wait(self, engine, toks):
        need = {}
        for (k, v) in toks:
            if self.seen[engine].get(k, 0) >= v:
                continue
            if need.get(k, 0) < v:
                need[k] = v
        waits = []
        for k, v in need.items():
            self.seen[engine][k] = v
            waits.append((k, v))
        self.streams[engine].append((waits, None, None))

    def emit(self, block):
        for e, attr in (("pe", "tensor"), ("act", "scalar"), ("dve", "vector"),
                        ("pool", "gpsimd"), ("sp", "sync")):
            stream = self.streams[e]

            def body(eng, stream=stream):
                for waits, fn, incspec in stream:
                    for (k, v) in waits:
                        eng.wait_ge(self.sems[k], v)
                    if fn is None:
                        continue
                    ins = fn(eng)
                    if incspec is not None:
                        ins.then_inc(self.sems[incspec[0]], incspec[1])

            getattr(block, attr)(body)


def build_program():
    nc = bass.Bass("TRN2", target_bir_lowering=False)

    def din(name, shape):
        return nc.dram_tensor(name, list(shape), F32, kind="ExternalInput").ap()

    def dout(name, shape):
        return nc.dram_tensor(name, list(shape), F32, kind="ExternalOutput").ap()

    xT = din("xT", (NH, NB, 128, KD, BS))
    w0in = din("w0in", (KE, 128, KD, 512))
    w0out = din("w0out", (4, 128, 4, D))
    w1in = din("w1in", (16, 128, KD, 256))
    w1g = din("w1g", (4, 128, 4, 512))
    w1out = din("w1out", (4, 128, 4, D))
    vecs_d = din("vecs", (128, NV))
    stc_d = din("stc", (NH, 128, KE, SH, CONV_H))
    stp_d = din("stp", (NH, KE, 128, POOL_H * SH))
    invc_d = din("invc", (128, 4, 16))

    yT = dout("yT", (NH, NB, 128, KD, BS))
    ncp_d = dout("ncp", (128, KE, CONV_H))
    ncs_d = dout("ncs", (NH, 128, KE, SH, CONV_H))
    npp_d = dout("npp", (128, KE, POOL_H))
    nps_d = dout("nps", (NH, KE, 128, POOL_H * SH))

    with ExitStack() as stack:
        def sb(name, shape, dt=F32):
            return stack.enter_context(nc.sbuf_tensor(name, list(shape), dt))

        def ps(name):
            return stack.enter_context(nc.psum_tensor(name, [128, 512], F32))

        S = Sched(nc, stack)

        xs = sb("xs", (128, KD, TH))
        xn = sb("xn", (128, KD, TH), BF16)
        yb = sb("yb", (128, KE, TH), BF16)
        win = [sb("win%d" % i, (128, KD, 512), BF16) for i in range(2)]
        wout = sb("wout", (128, KE, D), BF16)
        wg = [sb("wg%d" % i, (128, 4, 512), BF16) for i in range(2)]
        vecs = sb("vecs_sb", (128, NV))
        ones = sb("ones", (128, 128))
        zeros = sb("zeros", (128, 16))
        invc = sb("invc_sb", (128, 4, 16))
        stc = sb("stc_sb", (128, KE, SH, CONV_H))
        ocs = sb("ocs", (128, KE, SH, CONV_H))
        ocp = sb("ocp", (128, KE, CONV_H))
        opp = sb("opp", (128, KE, POOL_H))
        hconv = sb("hconv", (128, KE, CONV_H))
        hpool = sb("hpool", (128, KE, POOL_H))
        NU = 3
        scr = sb("scr", (128, 5 * UFW + 16 + 2 * BS))
        uf = [scr[:, i * UFW:(i + 1) * UFW] for i in range(NU)]
        tmpA = scr[:, 3 * UFW:4 * UFW]
        tmpB = scr[:, 4 * UFW:5 * UFW]
        o5 = 5 * UFW
        pfx = scr[:, o5:o5 + 16]
        szp = [scr[:, o5 + 16 + i * BS:o5 + 16 + (i + 1) * BS] for i in range(2)]
        cvf = [uf[0][:, 0:CVW], uf[1][:, 0:CVW]]
        vsb = [uf[2][:, 0:BS], uf[2][:, BS:2 * BS]]
        szc = [uf[2][:, 2 * BS:3 * BS], tmpB[:, TW:TW + BS]]
        gsb = [tmpB[:, TW + BS:TW + 2 * BS], szp[0]]
        t1 = [tmpA[:, 0:TW], tmpA[:, TW:2 * TW]]
        t2 = [tmpA[:, 2 * TW:3 * TW], tmpB[:, 0:TW]]
        pbraw = sb("pbraw", (128, 8 * (TH // 2)))
        pb = [pbraw[:, i * (TH // 2):(i + 1) * (TH // 2)].bitcast(BF16) for i in range(8)]
        XB = KD * BS
        x2s = [scr[:, 0:XB].rearrange("p (k t) -> p k t", k=KD),
               scr[:, XB:2 * XB].rearrange("p (k t) -> p k t", k=KD),
               pbraw[:, 0:XB].rearrange("p (k t) -> p k t", k=KD)]
        NST = 2
        acc = [sb("acc%d" % i, (128, BS)) for i in range(NST)]
        sq = [sb("sq%d" % i, (128, BS)) for i in range(2)]
        rt = [sb("rt%d" % i, (128, BS)) for i in range(NST)]
        rstd = rt
        rrow = sb("rrow", (128, BS))
        ps0 = sb("ps0", (128, 4))
        sq3 = sb("sq3", (128, BS))
        stp_sb = sb("stp_sb", (128, KE, POOL_H * SH))
        onew = sb("onew", (128, KE, DEC_T * SH))

        banks = [ps("bank%d" % i) for i in range(8)]

        R_x = [[Res("x%d_%d" % (k, b)) for b in range(NB)] for k in range(KD)]
        R_xn = [[Res("xn%d_%d" % (k, b)) for b in range(NB)] for k in range(KD)]
        R_y = [[Res("y%d_%d" % (c, b)) for b in range(NB)] for c in range(KE)]
        R_win = [Res("win%d" % i) for i in range(2)]
        R_wout = [Res("wout%d" % i) for i in range(4)]
        R_wg = [Res("wg%d" % i) for i in range(2)]
        R_const = Res("const")
        R_stc = Res("stc")
        R_ocs = Res("ocs")
        R_ocp = Res("ocp")
        R_opp = Res("opp")
        R_hconv = [Res("hconv%d" % c) for c in range(KE)]
        R_hpool = [Res("hpool%d" % c) for c in range(KE)]
        R_cvf = [Res("cvf%d" % i) for i in range(2)]
        R_vsb = [Res("vsb%d" % i) for i in range(2)]
        R_szc = [Res("szc%d" % i) for i in range(2)]
        R_gsb = [Res("gsb%d" % i) for i in range(2)]
        R_t1 = [Res("t1_%d" % i) for i in range(2)]
        R_t2 = [Res("t2_%d" % i) for i in range(2)]
        R_ufh = [Res("ufh%d" % i) for i in range(NU)]
        R_ufs = [Res("ufs%d" % i) for i in range(NU)]
        R_ufb = [Res("ufb%d" % i) for i in range(NU)]
        R_winh = [Res("winh%d" % i) for i in range(4)]
        R_tmpA = Res("tmpA")
        R_tmpB = Res("tmpB")
        R_pfx = Res("pfx")
        R_pb = [[Res("pb%d_%d" % (i, b)) for b in range(NB)] for i in range(8)]
        R_szp = [Res("szp%d" % i) for i in range(2)]
        R_acc = [Res("acc%d" % i) for i in range(3)]
        R_sq = [Res("sq%d" % i) for i in range(2)]
        R_rt = [Res("rt%d" % i) for i in range(3)]
        R_x2 = [Res("x2s%d" % i) for i in range(NB)]
        R_stp = Res("stp")
        R_onew = Res("onew")
        R_rrow = [Res("rrow%d" % i) for i in range(NB)]
        R_sq3 = Res("sq3")
        R_bank = [Res("bank%d" % i) for i in range(8)]
        for rr in (R_ufh[0], R_ufs[0], R_ufb[0]):
            alias(rr, [R_cvf[0]])
        for rr in (R_ufh[1], R_ufs[1], R_ufb[1]):
            alias(rr, [R_cvf[1]])
        for rr in (R_ufh[2], R_ufs[2], R_ufb[2]):
            alias(rr, [R_vsb[0], R_vsb[1], R_szc[0]])
        alias(R_tmpA, [R_t1[0], R_t1[1], R_t2[0]])
        alias(R_tmpB, [R_t2[1], R_szc[1], R_gsb[0]])
        alias(R_szp[0], [R_gsb[1]])
        scr_all = (R_ufh + R_ufs + R_ufb + [R_tmpA, R_tmpB, R_pfx] + R_szp + R_cvf + R_vsb + R_szc
                   + R_gsb + R_t1 + R_t2)
        alias(R_x2[0], scr_all)
        alias(R_x2[1], scr_all)
        alias(R_x2[2], [r for rs in R_pb for r in rs])
        alias(R_win[0], [R_winh[0], R_winh[1]])
        alias(R_win[1], [R_winh[2], R_winh[3]])

        out_toks = []
        ctr = {"sb": 0, "w": 0, "wh": 0, "wg": 0, "cb": 0, "wo": 0, "ss": 0, "sq": 0, "ot": 0,
               "ub": 0, "zb": 0, "qb": 0}

        def blk(b):
            return slice(b * BS, (b + 1) * BS)

        def v1(col):
            return vecs[:, col:col + 1]

        def sv(ap2d, r):
            return ap2d.rearrange("p (s r) -> p s r", r=r)

        def mm_group(bank, fns, reads_list):
            allr = []
            for rl in reads_list:
                for r in rl:
                    if r not in allr:
                        allr.append(r)
            n = len(fns)
            for i in range(n):
                last = i == n - 1
                S.op("pe", fns[i], reads=(allr if last else reads_list[i]), writes=[R_bank[bank]], inc=last)

        S.op("sp", lambda e: e.dma_start(out=vecs[:], in_=vecs_d), writes=[R_const], dma="const")
        S.op("sp", lambda e: e.dma_start(out=invc[:], in_=invc_d), writes=[R_const], dma="const")
        S.op("pool", lambda e: e.memset(ones[:], 1.0), writes=[R_const])
        S.op("pool", lambda e: e.memset(zeros[:], 0.0), writes=[R_const])
        R_ps0 = Res("ps0")
        S.op("dve", lambda e: e.tensor_scalar_mul(out=ps0[:], in0=vecs[:, V_PS:V_PS + 4], scalar1=0.5),
             reads=[R_const], writes=[R_ps0])

        def load_win(src_ap, after=()):
            slot = ctr["w"] % 2
            ctr["w"] += 1
            S.op("pool", lambda e, slot=slot, src_ap=src_ap: e.dma_start(out=win[slot][:], in_=src_ap),
                 reads=list(after), writes=[R_win[slot]], dma="win%d" % slot)
            return slot

        def load_wg(src_ap):
            slot = ctr["wg"] % 2
            ctr["wg"] += 1
            S.op("pool", lambda e, slot=slot, src_ap=src_ap: e.dma_start(out=wg[slot][:], in_=src_ap),
                 writes=[R_wg[slot]], dma="wg%d" % slot)
            return slot

        def load_wout_q(wsrc, q):
            S.op("pool", lambda e, q=q, wsrc=wsrc: e.dma_start(out=wout[:, 4 * q:4 * q + 4, :], in_=wsrc[q]),
                 writes=[R_wout[q]], dma="wout%d" % q)

        def stat_begin():
            for k_ in list(padd.keys()):
                flush_add(k_)
            a = ctr["ss"] % NST
            ctr["ss"] += 1
            return a

        padd = {}

        def flush_add(a):
            f = padd.pop(a, None)
            if f is not None:
                f()

        def stat_tile(a, k, b, staged=False, add_eng="dve"):
            for k_ in list(padd.keys()):
                flush_add(k_)
            xin = x2s[b][:, k, :] if staged else xs[:, k, blk(b)]
            rin = R_x2[b] if staged else R_x[k][b]
            if k == 0:
                S.op("act", lambda e, a=a, xin=xin: e.activation(out=acc[a][:], in_=xin, func=AF.Square),
                     reads=[rin], writes=[R_acc[a]])
            else:
                s = ctr["sq"] % 2
                ctr["sq"] += 1
                S.op("act", lambda e, s=s, xin=xin: e.activation(out=sq[s][:], in_=xin, func=AF.Square),
                     reads=[rin], writes=[R_sq[s]])
                padd[a] = lambda s=s, a=a: S.op(
                    add_eng, lambda e, s=s, a=a: e.tensor_tensor(out=acc[a][:], in0=acc[a][:], in1=sq[s][:], op=ALU.add),
                    reads=[R_acc[a], R_sq[s]], writes=[R_acc[a]])

        def stat_finish(a):
            flush_add(a)
            bank = 6 + ctr["sb"] % 2
            ctr["sb"] += 1
            S.op("pe", lambda e, a=a, bank=bank: e.matmul(banks[bank][:, 0:BS], ones[:], acc[a][:], start=True, stop=True),
                 reads=[R_acc[a], R_const], writes=[R_bank[bank]])
            S.op("act", lambda e, a=a, bank=bank: e.activation(out=rt[a][:], in_=banks[bank][:, 0:BS], func=AF.Sqrt,
                                                                bias=EPS, scale=1.0 / D),
                 reads=[R_bank[bank]], writes=[R_rt[a]])
            S.op("dve", lambda e, a=a: e.reciprocal(out=rt[a][:], in_=rt[a][:]),
                 reads=[R_rt[a]], writes=[R_rt[a]])

        def apply_norm(a, b, gcol, staged=False):
            for k in range(KD):
                xin = x2s[b][:, k, :] if staged else xs[:, k, blk(b)]
                rin = R_x2[b] if staged else R_x[k][b]
                S.op("dve", lambda e, a=a, k=k, b=b, xin=xin: e.scalar_tensor_tensor(
                    out=xn[:, k, blk(b)], in0=xin, scalar=v1(gcol + k), in1=rt[a][:],
                    op0=ALU.mult, op1=ALU.mult),
                    reads=[rin, R_rt[a], R_const], writes=[R_xn[k][b]])

        def apply_final(a, h, b):
            for k in range(KD):
                S.op("dve", lambda e, a=a, k=k, b=b: e.scalar_tensor_tensor(
                    out=xs[:, k, blk(b)], in0=xs[:, k, blk(b)], scalar=v1(V_GF + k), in1=rt[a][:],
                    op0=ALU.mult, op1=ALU.mult),
                    reads=[R_x[k][b], R_rt[a], R_const], writes=[R_x[k][b]])
                tok = S.op("sp", lambda e, h=h, b=b, k=k: e.dma_start(out=yT[h, b, :, k, :], in_=xs[:, k, blk(b)]),
                           reads=[R_x[k][b]], dma="yo%d" % (k % 4))
                out_toks.append(tok)

        pro = {}

        def prologue_load(h, b):
            S.op("sp", lambda e, h=h, b=b: e.dma_start(out=xs[:, :, blk(b)], in_=xT[h, b]),
                 writes=[R_x[k][b] for k in range(KD)], dma="x%d" % b)

        def prologue_state(h, part="all"):
            if part in ("all", "early"):
                S.op("sp", lambda e, h=h: e.dma_start(out=stc[:], in_=stc_d[h]), writes=[R_stc], dma="stc")
            if part == "early":
                return
            S.op("sp", lambda e, h=h: e.dma_start(out=stp_sb[:], in_=stp_d[h].rearrange("c p f -> p c f")),
                 writes=[R_stp], dma="stp")
            tok = S.op("sp", lambda e, h=h: e.dma_start(out=nps_d[h][:, :, 0:HK], in_=stp_d[h][:, :, HN:HN + HK]),
                       dma="npsh")
            out_toks.append(tok)

        pst = {}

        def pre_stream_step(i):
            lbuf = [sq[0], sq[1], sq3]
            lres = [R_sq[0], R_sq[1], R_sq3]
            if i < NB * KD:
                b, k = divmod(i, KD)
                s = i % 3
                S.op("sp", lambda e, s=s, b=b, k=k: e.dma_start(out=lbuf[s][:], in_=xT[1, b][:, k, :]),
                     writes=[lres[s]], dma="xs%d" % s)
            j = i - 2
            if 0 <= j < NB * KD:
                b, k = divmod(j, KD)
                s = j % 3
                if k == 0:
                    pst["a"] = stat_begin()
                a = pst["a"]
                if k == 0:
                    S.op("act", lambda e, a=a, s=s: e.activation(out=acc[a][:], in_=lbuf[s][:], func=AF.Square),
                         reads=[lres[s]], writes=[R_acc[a]])
                else:
                    S.op("act", lambda e, s=s: e.activation(out=lbuf[s][:], in_=lbuf[s][:], func=AF.Square),
                         reads=[lres[s]], writes=[lres[s]])
                    S.op("dve", lambda e, s=s, a=a: e.tensor_tensor(out=acc[a][:], in0=acc[a][:], in1=lbuf[s][:], op=ALU.add),
                         reads=[R_acc[a], lres[s]], writes=[R_acc[a]])
                if k == KD - 1:
                    pst[("fin", i + 3)] = (a, b)
            if ("fin", i) in pst:
                a, b = pst.pop(("fin", i))
                stat_finish(a)
                pst[("row", i + 3)] = (a, b)
            if ("row", i) in pst:
                a, b = pst.pop(("row", i))
                S.op("sp", lambda e, a=a, b=b: e.dma_start(out=rrow[32 * b:32 * b + 1, :], in_=rt[a][0:1, :]),
                     reads=[R_rt[a]], writes=[R_rrow[b]], dma="rrow%d" % b)

        def pre_apply(b):
            bank = 6 + ctr["sb"] % 2
            ctr["sb"] += 1
            p0 = 32 * b
            S.op("pe", lambda e, bank=bank, p0=p0: e.matmul(banks[bank][:, 0:BS], ones[p0:p0 + 1, :], rrow[p0:p0 + 1, :],
                                                           start=True, stop=True),
                 reads=[R_rrow[b], R_const], writes=[R_bank[bank]])
            for k in range(KD):
                S.op("dve", lambda e, k=k, b=b, bank=bank: e.scalar_tensor_tensor(
                    out=xn[:, k, blk(b)], in0=x2s[b][:, k, :], scalar=v1(V_G0 + k), in1=banks[bank][:, 0:BS],
                    op0=ALU.mult, op1=ALU.mult),
                    reads=[R_x2[b], R_bank[bank], R_const], writes=[R_xn[k][b]])

        def prologue_stage(h, b):
            S.op("sp", lambda e, h=h, b=b: e.dma_start(out=x2s[b], in_=xT[h, b]),
                 writes=[R_x2[b]], dma="x2s%d" % b)

        def prologue_unstage(b):
            S.op("sp", lambda e, b=b: e.dma_start(out=xs[:, :, blk(b)], in_=x2s[b]),
                 reads=[R_x2[b]], writes=[R_x[k][b] for k in range(KD)], dma="x%d" % b)

        def prologue_tiles(b, staged=False, add_eng="dve"):
            pro[b] = stat_begin()
            for k in range(KD):
                stat_tile(pro[b], k, b, staged, add_eng)

        def prologue_finish(b, staged=False):
            stat_finish(pro[b])
            apply_norm(pro[b], b, V_G0, staged)

        def preload_conv(after=(), after1=()):
            return {0: load_win(w0in[0], after), 1: load_win(w0in[1], after1)}

        def preload_pool():
            return {(0, 0): load_winh(w1in[0]), (0, 1): load_winh(w1in[1]), "wg": load_wg(w1g[0])}

        def wout_phase(h, layer):
            nxt = (layer == 1 and h + 1 < NH)
            pre = None
            if layer == 0:
                pre = preload_pool()
                if h == 0:
                    prologue_state(0, "late")
            elif nxt:
                pre = preload_conv()
                prologue_state(h + 1)
                for b in range(NB):
                    prologue_stage(h + 1, b)
            pend = []
            for b in range(NB):
                a = stat_begin()
                for k in range(KD):
                    if k == 2:
                        for f in pend:
                            f()
                        pend = []
                    if k == 5 and nxt:
                        pre_apply(b)
                    bank = ctr["wo"] % 6
                    ctr["wo"] += 1
                    mm_group(bank, [lambda e, bank=bank, ec=ec, k=k, b=b: e.matmul(
                        banks[bank][:, 0:BS], wout[:, ec, k * 128:(k + 1) * 128], yb[:, ec, blk(b)],
                        start=(ec == 0), stop=(ec == KE - 1)) for ec in range(KE)],
                        [[R_wout[ec // 4], R_y[ec][b]] for ec in range(KE)])
                    S.op("dve", lambda e, bank=bank, k=k, b=b: e.tensor_tensor(
                        out=xs[:, k, blk(b)], in0=xs[:, k, blk(b)], in1=banks[bank][:, 0:BS], op=ALU.add),
                        reads=[R_x[k][b], R_bank[bank]], writes=[R_x[k][b]])
                    stat_tile(a, k, b)

                def fin(a=a, b=b):
                    stat_finish(a)
                    if layer == 0:
                        apply_norm(a, b, V_G1)
                    else:
                        apply_final(a, h, b)
                    if nxt:
                        prologue_unstage(b)

                if b < NB - 1:
                    pend.append(fin)
                else:
                    fin()
            return pre

        def conv_begin(h, c):
                cf = c % 2
                CF = cvf[cf]
                src = zeros[:, 0:CONV_H] if h == 0 else hconv[:, c, :]
                S.op("act", lambda e, CF=CF, src=src: e.activation(out=CF[:, 0:CONV_H], in_=src, func=AF.Copy),
                     reads=[R_hconv[c], R_const], writes=[R_cvf[cf]])
                S.op("act", lambda e, CF=CF, c=c: e.activation(
                    out=sv(CF[:, CV_S0:CVW], CONV_H + DEC_T)[:, :, 0:CONV_H], in_=stc[:, c, :, :], func=AF.Copy),
                    reads=[R_stc], writes=[R_cvf[cf]])
        cstate = {}

        def conv_block(h, c, b, slot, part="both"):
                    cf = c % 2
                    CF = cvf[cf]
                    if part in ("both", "mm"):
                        st = ctr["cb"] % 2
                        ctr["cb"] += 1
                        bk = [4 * st + q for q in range(4)]
                        if slot == "wq0":
                            wt, rw = wq0, R_wout[0]
                        else:
                            wt, rw = win[slot], R_win[slot]
                        for q in range(4):
                            mm_group(bk[q], [lambda e, q=q, k=k, b=b, wt=wt, bank=bk[q]: e.matmul(
                                banks[bank][:, 0:BS], wt[:, k, q * 128:(q + 1) * 128], xn[:, k, blk(b)],
                                start=(k == 0), stop=(k == KD - 1)) for k in range(KD)],
                                [[rw, R_xn[k][b]] for k in range(KD)])
                        cstate[(c, b)] = (st, bk)
                        if part == "mm":
                            return
                    st, bk = cstate.pop((c, b))
                    Pgb, Pgc, Pv, Pz = (banks[x] for x in bk)
                    S.op("act", lambda e, st=st, Pv=Pv: e.activation(out=vsb[st][:], in_=Pv[:, 0:BS], func=AF.Copy),
                         reads=[R_bank[bk[2]]], writes=[R_vsb[st]])
                    S.op("act", lambda e, st=st, Pz=Pz: e.activation(out=szc[st][:], in_=Pz[:, 0:BS], func=AF.Silu),
                         reads=[R_bank[bk[3]]], writes=[R_szc[st]])
                    lo = b * BS
                    if b < 2:
                        n = BS
                        S.op("dve", lambda e, CF=CF, Pgc=Pgc, st=st, lo=lo: e.tensor_tensor(
                            out=CF[:, CONV_H + lo:CONV_H + lo + BS], in0=Pgc[:, 0:BS], in1=vsb[st][:], op=ALU.mult),
                            reads=[R_bank[bk[1]], R_vsb[st]], writes=[R_cvf[cf]])
                    else:
                        n = TW
                        S.op("dve", lambda e, CF=CF, Pgc=Pgc, st=st, lo=lo: e.tensor_tensor(
                            out=CF[:, CONV_H + lo:CONV_H + lo + PB2], in0=Pgc[:, 0:PB2], in1=vsb[st][:, 0:PB2],
                            op=ALU.mult),
                            reads=[R_bank[bk[1]], R_vsb[st]], writes=[R_cvf[cf]])
                        S.op("dve", lambda e, CF=CF, Pgc=Pgc, st=st: e.tensor_tensor(
                            out=sv(CF[:, CV_S0:CVW], CONV_H + DEC_T)[:, :, CONV_H:],
                            in0=sv(Pgc[:, PB2:BS], DEC_T), in1=sv(vsb[st][:, PB2:BS], DEC_T), op=ALU.mult),
                            reads=[R_bank[bk[1]], R_vsb[st]], writes=[R_cvf[cf]])
                    S.op("dve", lambda e, Pgb=Pgb, st=st: e.tensor_tensor(
                        out=gsb[st][:], in0=Pgb[:, 0:BS], in1=szc[st][:], op=ALU.mult),
                        reads=[R_bank[bk[0]], R_szc[st]], writes=[R_gsb[st]])
                    S.op("act", lambda e, CF=CF, st=st, lo=lo, n=n, c=c: e.activation(
                        out=t1[st][:, 0:n], in_=CF[:, lo + 2:lo + 2 + n], func=AF.Identity,
                        bias=v1(V_CB + c), scale=v1(V_CW + 3 * c + 2)),
                        reads=[R_cvf[cf], R_const], writes=[R_t1[st]])
                    S.op("dve", lambda e, CF=CF, st=st, lo=lo, n=n, c=c: e.scalar_tensor_tensor(
                        out=t2[st][:, 0:n], in0=CF[:, lo + 1:lo + 1 + n], scalar=v1(V_CW + 3 * c + 1),
                        in1=t1[st][:, 0:n], op0=ALU.mult, op1=ALU.add),
                        reads=[R_cvf[cf], R_t1[st], R_const], writes=[R_t2[st]])
                    S.op("dve", lambda e, CF=CF, st=st, lo=lo, n=n, c=c: e.scalar_tensor_tensor(
                        out=t1[st][:, 0:n], in0=CF[:, lo:lo + n], scalar=v1(V_CW + 3 * c + 0),
                        in1=t2[st][:, 0:n], op0=ALU.mult, op1=ALU.add),
                        reads=[R_cvf[cf], R_t2[st], R_const], writes=[R_t1[st]])
                    if b < 2:
                        S.op("dve", lambda e, st=st, c=c, b=b: e.tensor_tensor(
                            out=yb[:, c, blk(b)], in0=gsb[st][:], in1=t1[st][:, 0:BS], op=ALU.mult),
                            reads=[R_gsb[st], R_t1[st]], writes=[R_y[c][b]])
                    else:
                        S.op("dve", lambda e, st=st, c=c: e.tensor_tensor(
                            out=yb[:, c, 2 * BS:2 * BS + PB2], in0=gsb[st][:, 0:PB2], in1=t1[st][:, 0:PB2],
                            op=ALU.mult),
                            reads=[R_gsb[st], R_t1[st]], writes=[R_y[c][b]])
                        S.op("dve", lambda e, st=st, c=c: e.tensor_tensor(
                            out=sv(yb[:, c, PH:TH], DEC_T), in0=sv(gsb[st][:, PB2:BS], DEC_T),
                            in1=sv(t1[st][:, PB2:TW], CONV_H + DEC_T)[:, :, CONV_H:], op=ALU.mult),
                            reads=[R_gsb[st], R_t1[st]], writes=[R_y[c][b]])
        def conv_end(h, c):
                cf = c % 2
                CF = cvf[cf]
                if h == 0:
                    S.op("act", lambda e, CF=CF, c=c: e.activation(out=hconv[:, c, :], in_=CF[:, PH:PH + CONV_H], func=AF.Copy),
                         reads=[R_cvf[cf]], writes=[R_hconv[c]])
                else:
                    S.op("act", lambda e, CF=CF, c=c: e.activation(out=ocp[:, c, :], in_=CF[:, PH:PH + CONV_H], func=AF.Copy),
                         reads=[R_cvf[cf]], writes=[R_ocp])
                S.op("act", lambda e, CF=CF, c=c: e.activation(
                    out=ocs[:, c, :, :], in_=sv(CF[:, CV_S0:CVW], CONV_H + DEC_T)[:, :, DEC_T:], func=AF.Copy),
                    reads=[R_cvf[cf]], writes=[R_ocs])
        def conv_phase(h, pre, hooks=None):
            slots = dict(pre)
            if hooks is not None:
                for c in (0, 1):
                    conv_begin(h, c)
                for b in range(NB):
                    conv_block(h, 0, b, slots[0], part="mm")
                    if b in hooks:
                        hooks[b]()
                    conv_block(h, 0, b, slots[0], part="ew")
                    conv_block(h, 1, b, slots[1])
                for c in (0, 1):
                    conv_end(h, c)
            else:
                for c in (0, 1):
                    conv_begin(h, c)
                    for b in range(NB):
                        conv_block(h, c, b, slots[c])
                    conv_end(h, c)
            for c in range(2, KE):
                if h == 0 and c == 2:
                    slot = "wq0"
                else:
                    slot = load_win(w0in[c])
                if h == 0:
                    if c in (3, 4, 6, 8):
                        load_wout_q(w0out, {3: 0, 4: 1, 6: 2, 8: 3}[c])
                elif c in (2, 4, 6, 8):
                    load_wout_q(w0out, (c - 2) // 2)
                conv_begin(h, c)
                for b in range(NB):
                    if h == 0:
                        pre_stream_step((c - 2) * NB + b)
                    conv_block(h, c, b, slot)
                conv_end(h, c)
            tok = S.op("sp", lambda e, h=h: e.dma_start(out=ncs_d[h], in_=ocs[:]), reads=[R_ocs], dma="ocs")
            out_toks.append(tok)
            if h == 1:
                tok = S.op("sp", lambda e: e.dma_start(out=ncp_d, in_=ocp[:]), reads=[R_ocp], dma="ocp")
                out_toks.append(tok)

        def load_winh(src_ap):
            i = ctr["wh"] % 4
            ctr["wh"] += 1
            dst = win[i // 2][:, :, (i % 2) * 256:(i % 2) * 256 + 256]
            S.op("pool", lambda e, dst=dst, src_ap=src_ap: e.dma_start(out=dst, in_=src_ap),
                 writes=[R_winh[i]], dma="winh%d" % i)
            return i

        def whs(i, j):
            base = (i % 2) * 256 + j * 128
            return win[i // 2], base

        def pool_A_begin(h, c):
            ui = c % NU
            U = uf[ui]
            src = zeros[:, 0:POOL_H] if h == 0 else hpool[:, c, :]
            S.op("act", lambda e, U=U, src=src: e.activation(out=U[:, 0:POOL_H], in_=src, func=AF.Copy),
                 reads=[R_hpool[c], R_const], writes=[R_ufh[ui]])
            S.op("act", lambda e, U=U, c=c: e.activation(
                out=sv(U[:, UF_S0:UFW], POOL_H + DEC_T)[:, :, 0:POOL_H],
                in_=stp_sb[:, c, :].rearrange("p (r s) -> p s r", s=SH), func=AF.Copy),
                reads=[R_stp], writes=[R_ufs[ui]])
        def pool_A_block(h, c, b, hslot):
                ui = c % NU
                U = uf[ui]
                wt, wbase = whs(hslot, c % 2)
                bank = ctr["ub"] % 3
                ctr["ub"] += 1
                mm_group(bank, [lambda e, bank=bank, k=k, b=b, wt=wt, wbase=wbase: e.matmul(
                    banks[bank][:, 0:BS], wt[:, k, wbase:wbase + 128], xn[:, k, blk(b)],
                    start=(k == 0), stop=(k == KD - 1)) for k in range(KD)],
                    [[R_winh[hslot], R_xn[k][b]] for k in range(KD)])
                P = banks[bank]
                lo = POOL_H + b * BS
                if b < 2:
                    S.op("act", lambda e, U=U, P=P, lo=lo: e.activation(out=U[:, lo:lo + BS], in_=P[:, 0:BS], func=AF.Copy),
                         reads=[R_bank[bank]], writes=[R_ufb[ui]])
                else:
                    S.op("act", lambda e, U=U, P=P, lo=lo: e.activation(out=U[:, lo:lo + PB2], in_=P[:, 0:PB2], func=AF.Copy),
                         reads=[R_bank[bank]], writes=[R_ufb[ui]])
                    S.op("act", lambda e, U=U, P=P: e.activation(
                        out=sv(U[:, UF_S0:UFW], POOL_H + DEC_T)[:, :, POOL_H:], in_=sv(P[:, PB2:BS], DEC_T),
                        func=AF.Copy),
                        reads=[R_bank[bank]], writes=[R_ufb[ui]])
        def pool_A_end(h, c):
            ui = c % NU
            U = uf[ui]
            if h == 0:
                S.op("act", lambda e, U=U, c=c: e.activation(out=hpool[:, c, :], in_=U[:, PH:PH + POOL_H], func=AF.Copy),
                     reads=[R_ufb[ui]], writes=[R_hpool[c]])
            else:
                S.op("act", lambda e, U=U, c=c: e.activation(out=opp[:, c, :], in_=U[:, PH:PH + POOL_H], func=AF.Copy),
                     reads=[R_ufb[ui]], writes=[R_opp])
            S.op("act", lambda e, U=U, c=c: e.activation(
                out=onew[:, c, :].rearrange("p (r s) -> p s r", s=SH),
                in_=sv(U[:, UF_S0:UFW], POOL_H + DEC_T)[:, :, POOL_H:], func=AF.Copy),
                reads=[R_ufb[ui]], writes=[R_onew])

        def pool_A_chunk(h, c, hslot):
            pool_A_begin(h, c)
            for b in range(NB):
                pool_A_block(h, c, b, hslot)
            pool_A_end(h, c)

        def pool_chunk(h, c):
            g = c // 4
            ui = c % NU
            U = uf[ui]
            RU = [R_ufh[ui], R_ufs[ui], R_ufb[ui]]
            w = WINDOWS[g]
            cur, Rcur = U, RU
            tmps = [(tmpA, [R_tmpA]), (tmpB, [R_tmpB])]
            if w == 2:
                pi = c % 8
                P = pb[pi]
                Rp = R_pb[pi]
                S.op("dve", lambda e, P=P, U=U: e.tensor_tensor(
                    out=P[:, 0:PH], in0=U[:, POOL_H - 1:POOL_H - 1 + PH], in1=U[:, POOL_H:POOL_H + PH],
                    op=ALU.subtract),
                    reads=RU, writes=[Rp[0], Rp[1], Rp[2]])
                S.op("dve", lambda e, P=P, U=U: e.tensor_tensor(
                    out=sv(P[:, PH:TH], DEC_T),
                    in0=sv(U[:, UF_S0:UFW], POOL_H + DEC_T)[:, :, POOL_H - 1:POOL_H - 1 + DEC_T],
                    in1=sv(U[:, UF_S0:UFW], POOL_H + DEC_T)[:, :, POOL_H:], op=ALU.subtract),
                    reads=RU, writes=[Rp[2]])
                if h == 0:
                    S.op("dve", lambda e, P=P: e.memset(P[:, 0:1], 0.0), writes=[Rp[0]])
                return
            sh = 1
            lvl = 0
            if w == 16:
                S.op("dve", lambda e, U=U: e.tensor_tensor_scan(
                    out=tmpA[:, 0:UFW], data0=U[:, 0:UFW], data1=U[:, 0:UFW], initial=0.0,
                    op0=ALU.add, op1=ALU.bypass),
                    reads=RU, writes=[R_tmpA])
                S.op("dve", lambda e: e.tensor_tensor(
                    out=tmpB[:, 16:UFW], in0=tmpA[:, 16:UFW], in1=tmpA[:, 0:UFW - 16], op=ALU.subtract),
                    reads=[R_tmpA], writes=[R_tmpB])
                S.op("dve", lambda e: e.tensor_copy(out=tmpB[:, 15:16], in_=tmpA[:, 15:16]),
                     reads=[R_tmpA], writes=[R_tmpB])
                cur, Rcur = tmpB, [R_tmpB]
                sh = w
            while sh < w:
                dst, Rdst = tmps[lvl % 2]
                lo = 2 * sh - 1
                S.op("dve", lambda e, dst=dst, cur=cur, lo=lo, sh=sh: e.tensor_tensor(
                    out=dst[:, lo:UFW], in0=cur[:, lo:UFW], in1=cur[:, lo - sh:UFW - sh], op=ALU.add),
                    reads=Rcur, writes=Rdst)
                cur, Rcur = dst, Rdst
                sh *= 2
                lvl += 1
            pi = c % 8
            P = pb[pi]
            Rp = R_pb[pi]
            if h == 0:
                S.op("dve", lambda e, cur=cur, g=g: e.tensor_tensor(
                    out=cur[:, POOL_H:POOL_H + 16], in0=cur[:, POOL_H:POOL_H + 16], in1=invc[:, g, :], op=ALU.mult),
                    reads=Rcur + [R_const], writes=Rcur)
            S.op("dve", lambda e, P=P, cur=cur, U=U, w=w: e.scalar_tensor_tensor(
                out=P[:, 0:PH], in0=cur[:, POOL_H:POOL_H + PH], scalar=1.0 / w, in1=U[:, POOL_H:POOL_H + PH],
                op0=ALU.mult, op1=ALU.subtract),
                reads=Rcur + RU, writes=[Rp[0], Rp[1], Rp[2]])
            S.op("dve", lambda e, P=P, cur=cur, U=U, w=w: e.scalar_tensor_tensor(
                out=sv(P[:, PH:TH], DEC_T), in0=sv(cur[:, UF_S0:UFW], POOL_H + DEC_T)[:, :, POOL_H:],
                scalar=1.0 / w, in1=sv(U[:, UF_S0:UFW], POOL_H + DEC_T)[:, :, POOL_H:],
                op0=ALU.mult, op1=ALU.subtract),
                reads=Rcur + RU, writes=[Rp[2]])

        def pool_B_chunk(h, c, hslot, gslot, blocks=range(NB)):
            g = c // 4
            ci = c % 4
            wt, wbase = whs(hslot, c % 2)
            for b in blocks:
                zb = 3 + ctr["zb"] % 2
                ctr["zb"] += 1
                qb = 5 + ctr["qb"] % 3
                ctr["qb"] += 1
                mm_group(zb, [lambda e, zb=zb, k=k, b=b, wt=wt, wbase=wbase: e.matmul(
                    banks[zb][:, 0:BS], wt[:, k, wbase:wbase + 128], xn[:, k, blk(b)],
                    start=(k == 0), stop=(k == KD - 1)) for k in range(KD)],
                    [[R_winh[hslot], R_xn[k][b]] for k in range(KD)])
                mm_group(qb, [lambda e, qb=qb, kc=kc, b=b, gslot=gslot, ci=ci, pi=(4 * g + kc) % 8: e.matmul(
                    banks[qb][:, 0:BS], wg[gslot][:, kc, ci * 128:(ci + 1) * 128], pb[pi][:, blk(b)],
                    start=(kc == 0), stop=(kc == 3)) for kc in range(4)],
                    [[R_wg[gslot], R_pb[(4 * g + kc) % 8][b]] for kc in range(4)])
                s = ctr["zb"] % 2
                S.op("act", lambda e, s=s, zb=zb: e.activation(out=szp[s][:], in_=banks[zb][:, 0:BS], func=AF.Silu),
                     reads=[R_bank[zb]], writes=[R_szp[s]])
                sc = ps0[:, c:c + 1] if c < 4 else v1(V_PS + c)
                S.op("dve", lambda e, s=s, qb=qb, c=c, b=b, sc=sc: e.scalar_tensor_tensor(
                    out=yb[:, c, blk(b)], in0=banks[qb][:, 0:BS], scalar=sc, in1=szp[s][:],
                    op0=ALU.mult, op1=ALU.mult),
                    reads=[R_bank[qb], R_szp[s], R_const, R_ps0], writes=[R_y[c][b]])

        def pool_phase(h, pre):
            us = dict(pre)
            for c in range(3):
                pool_A_begin(h, c)
                for b in (0, 1):
                    pool_A_block(h, c, b, us[(0, c // 2)])
            for c in range(3):
                pool_A_block(h, c, 2, us[(0, c // 2)])
                pool_A_end(h, c)
                pool_chunk(h, c)
            pool_A_chunk(h, 3, us[(0, 1)])
            pool_chunk(h, 3)
            gnext = us.pop("wg")
            for g in range(4):
                gslot = gnext
                load_wout_q(w1out, g)
                for hg in range(2):
                    if g + 1 < 4:
                        us[(g + 1, hg)] = load_winh(w1in[((g + 1) * 2 + 0) * 2 + hg])
                    zs = load_winh(w1in[(g * 2 + 1) * 2 + hg])
                    if hg == 1 and g + 1 < 4:
                        gnext = load_wg(w1g[g + 1])
                    for cj in range(2):
                        c = 4 * g + 2 * hg + cj
                        if g + 1 < 4:
                            pool_A_chunk(h, c + 4, us[(g + 1, hg)])
                            if c % 4 == 3:
                                pool_chunk(h, c + 4)
                                pool_B_chunk(h, c, zs, gslot)
                            else:
                                pool_B_chunk(h, c, zs, gslot, blocks=(0, 1))
                                pool_chunk(h, c + 4)
                                pool_B_chunk(h, c, zs, gslot, blocks=(2,))
                        else:
                            pool_B_chunk(h, c, zs, gslot)
            tok = S.op("sp", lambda e, h=h: e.dma_start(
                out=nps_d[h].rearrange("c p f -> p c f")[:, :, HK:HK + HN], in_=onew[:]),
                reads=[R_onew], dma="onew")
            out_toks.append(tok)
            if h == 1:
                tok = S.op("sp", lambda e: e.dma_start(out=npp_d, in_=opp[:]), reads=[R_opp], dma="opp")
                out_toks.append(tok)

        for b in range(NB):
            prologue_load(0, b)
        prologue_state(0, "early")
        pre = preload_conv(after=[R_x[k][0] for k in range(KD)], after1=[R_x[k][1] for k in range(KD)])
        wq0 = wout[:, 0:4, :].rearrange("p a (b f) -> p (a b) f", f=512)
        S.op("pool", lambda e: e.dma_start(out=wq0, in_=w0in[2]), reads=[R_x[k][2] for k in range(KD)],
             writes=[R_wout[0]], dma="wout0")
        prologue_tiles(0)
        prologue_finish(0)
        prologue_tiles(1)
        for h in range(NH):
            conv_phase(h, pre, hooks=({0: lambda: (prologue_finish(1), prologue_tiles(2)), 1: lambda: prologue_finish(2)} if h == 0 else None))
            pre = wout_phase(h, 0)
            pool_phase(h, pre)
            pre = wout_phase(h, 1)

        S.final_wait("sp", out_toks)

        with nc.Block() as block:
            S.emit(block)
    return nc


_NC_CACHE = {}


def _get_program():
    if "nc" not in _NC_CACHE:
        _NC_CACHE["nc"] = build_program()
    return _NC_CACHE["nc"]


def _chunk_vec(v):
    return np.ascontiguousarray(v.reshape(-1, 128).T)


def kernel(x_prompt, x_sample, state_conv, state_pool, norm_g, final_norm_g,
           conv_w_in, conv_w, conv_b, conv_w_out,
           pool_w_in, pool_w_grp, pool_scale, pool_w_out):
    f = np.float32
    x_prompt = np.asarray(x_prompt, f)
    x_sample = np.asarray(x_sample, f)
    state_conv = np.asarray(state_conv, f)
    state_pool = np.asarray(state_pool, f)
    SPC = DEC_B // NCORE

    w0 = np.asarray(conv_w_in, f)[0]
    w0in = np.ascontiguousarray(
        w0.reshape(KD, 128, 4, KE, 128).transpose(3, 1, 0, 2, 4).reshape(KE, 128, KD, 512))
    wo0 = np.asarray(conv_w_out, f)[0]
    w0out = np.ascontiguousarray(wo0.reshape(4, 4, 128, D).transpose(0, 2, 1, 3))
    w1 = np.asarray(pool_w_in, f)[0]
    w1in = np.ascontiguousarray(
        w1.reshape(KD, 128, 2, 4, 2, 256).transpose(3, 2, 4, 1, 0, 5).reshape(16, 128, KD, 256))
    wgm = np.asarray(pool_w_grp, f)[0]
    w1g = np.ascontiguousarray(wgm.reshape(4, 4, 128, 512).transpose(0, 2, 1, 3))
    wo1 = np.asarray(pool_w_out, f)[0]
    w1out = np.ascontiguousarray(wo1.reshape(4, 4, 128, D).transpose(0, 2, 1, 3))

    vecs = np.zeros((128, NV), f)
    cw = np.asarray(conv_w, f)[0]
    vecs[:, V_CW:V_CW + 48] = cw.reshape(3, KE, 128).transpose(2, 1, 0).reshape(128, 48)
    vecs[:, V_CB:V_CB + KE] = _chunk_vec(np.asarray(conv_b, f)[0])
    vecs[:, V_PS:V_PS + KE] = _chunk_vec(np.asarray(pool_scale, f)[0])
    ng = np.asarray(norm_g, f)
    vecs[:, V_G0:V_G0 + KD] = _chunk_vec(ng[0])
    vecs[:, V_G1:V_G1 + KD] = _chunk_vec(ng[1])
    vecs[:, V_GF:V_GF + KD] = _chunk_vec(np.asarray(final_norm_g, f))

    invc = np.zeros((128, 4, 16), f)
    for g, w in enumerate(WINDOWS):
        invc[:, g, :] = (np.float32(w) / np.minimum(np.float32(w), np.arange(16, dtype=f) + 1.0)).astype(f)[None, :]

    in_maps = []
    for core in range(NCORE):
        xp = x_prompt[core]
        xsm = x_sample[core * SPC:(core + 1) * SPC]
        halves = []
        for h in range(NH):
            halves.append(np.concatenate(
                [xp[h * PH:(h + 1) * PH], xsm[h * SH:(h + 1) * SH].reshape(SH * DEC_T, D)], axis=0))
        X = np.stack(halves)
        xT = np.ascontiguousarray(X.reshape(NH, NB, BS, KD, 128).transpose(0, 1, 4, 3, 2))
        sc = state_conv[0, core * SPC:(core + 1) * SPC]
        stc = np.ascontiguousarray(sc.reshape(NH, SH, CONV_H, KE, 128).transpose(0, 4, 3, 1, 2))
        sp_ = state_pool[0, core * SPC:(core + 1) * SPC]
        stp = np.ascontiguousarray(
            sp_.reshape(NH, SH, POOL_H, KE, 128).transpose(0, 3, 4, 2, 1).reshape(NH, KE, 128, POOL_H * SH))
        in_maps.append({"xT": xT, "w0in": w0in, "w0out": w0out, "w1in": w1in, "w1g": w1g,
                        "w1out": w1out, "vecs": vecs, "stc": stc, "stp": stp, "invc": invc})

    nc = _get_program()
    res = run_bass_kernel_spmd(nc, in_maps, core_ids=list(range(NCORE)))

    y_prompt = np.empty((NCORE, SEQ, D), f)
    y_sample = np.empty((DEC_B, DEC_T, D), f)
    ncp = np.empty((1, NCORE, CONV_H, E), f)
    ncs = np.empty((1, DEC_B, CONV_H, E), f)
    npp = np.empty((1, NCORE, POOL_H, E), f)
    nps = np.empty((1, DEC_B, POOL_H, E), f)
    for core in range(NCORE):
        r = res.results[core]
        Y = np.asarray(r["yT"]).transpose(0, 1, 4, 3, 2).reshape(NH, TH, D)
        for h in range(NH):
            y_prompt[core, h * PH:(h + 1) * PH] = Y[h, :PH]
            y_sample[core * SPC + h * SH:core * SPC + (h + 1) * SH] = Y[h, PH:].reshape(SH, DEC_T, D)
        ncp[0, core] = np.asarray(r["ncp"]).transpose(2, 1, 0).reshape(CONV_H, E)
        a = np.asarray(r["ncs"])
        ncs[0, core * SPC:(core + 1) * SPC] = a.transpose(0, 3, 4, 2, 1).reshape(SPC, CONV_H, E)
        npp[0, core] = np.asarray(r["npp"]).transpose(2, 1, 0).reshape(POOL_H, E)
        a = np.asarray(r["nps"]).reshape(NH, KE, 128, POOL_H, SH)
        nps[0, core * SPC:(core + 1) * SPC] = a.transpose(0, 4, 3, 1, 2).reshape(SPC, POOL_H, E)
    return (y_prompt, y_sample, ncp, ncs, npp, nps)
```
